# Optimizing a Trainium2 kernel written in Bass

```python
import math
import jax
import jax.numpy as jnp
from jax import lax
import numpy as np

D_MODEL = 1024
BATCH = 8
SEQ = 8192
DEPTH = 2

CTX_LEN = 256
GRID_W = 64
RMS_EPS = 1e-6

HEAD_DIM = 64
N_Q_HEADS = 8
N_KV_HEADS = 2
Q_PER_KV = N_Q_HEADS // N_KV_HEADS
ATTN_W = N_Q_HEADS * HEAD_DIM
KV_W = N_KV_HEADS * HEAD_DIM
Q_BLOCK = 128
ROPE_THETA = 10000.0

HYENA_W = 256
HYENA_ORDER = 2
HYENA_BANDS = 16
HYENA_POS_DIM = 1 + 2 * HYENA_BANDS
HYENA_HIDDEN = 64
HYENA_INNER = 2
HYENA_FILTER_OUT = HYENA_ORDER * 2 * HYENA_W
HYENA_DECAY_TARGET = 1e-2
HYENA_FAST_PCT = 0.3
HYENA_SLOW_PCT = 1.5
HYENA_MIN_DECAY = -math.log(HYENA_DECAY_TARGET) / HYENA_SLOW_PCT
HYENA_MAX_DECAY = -math.log(HYENA_DECAY_TARGET) / HYENA_FAST_PCT

POOL_W = 256
POOL_WINDOWS = (2, 4, 8, 16)
POOL_GROUP_W = POOL_W // len(POOL_WINDOWS)

GMLP_W = 256
GMLP_CHUNK = 128
GMLP_GROUPS = 4
GMLP_GROUP_W = GMLP_W // GMLP_GROUPS

N_BRANCH = 4
FFN_HIDDEN = ((8 * D_MODEL // 3 + 255) // 256) * 256

Q_OFF = 0
K_OFF = Q_OFF + ATTN_W
V_OFF = K_OFF + KV_W
HY_OFF = V_OFF + KV_W
POOL_OFF = HY_OFF + 3 * HYENA_W
GM_OFF = POOL_OFF + POOL_W
GATE_OFF = GM_OFF + 2 * GMLP_W
IN_W = GATE_OFF + N_BRANCH * D_MODEL

kernel_name = 'hybrid_prefix_diffusion_trunk'


def rms_norm(x, w):
    xf = x.astype(jnp.float32)
    y = xf * lax.rsqrt(jnp.mean(xf * xf, axis=-1, keepdims=True) + RMS_EPS)
    return (y * w.astype(jnp.float32)).astype(x.dtype)


def layer_norm(x, w):
    xf = x.astype(jnp.float32)
    mu = jnp.mean(xf, axis=-1, keepdims=True)
    var = jnp.mean(jnp.square(xf - mu), axis=-1, keepdims=True)
    return ((xf - mu) * lax.rsqrt(var + RMS_EPS) * w.astype(jnp.float32)).astype(x.dtype)


def modulate(x, norm_w, shift, scale):
    return rms_norm(x, norm_w) * (1 + scale) + shift


def axial_rope_tables(n_tokens, dtype):
    rows = n_tokens // GRID_W
    r, col = jnp.meshgrid(jnp.arange(rows, dtype=jnp.float32), jnp.arange(GRID_W, dtype=jnp.float32), indexing='ij')
    axis_dim = HEAD_DIM // 2
    inv_freq = ROPE_THETA ** (-jnp.arange(0, axis_dim, 2, dtype=jnp.float32) / axis_dim)
    ang = jnp.concatenate([r.reshape(-1, 1) * inv_freq, col.reshape(-1, 1) * inv_freq], axis=-1)
    return jnp.cos(ang).astype(dtype), jnp.sin(ang).astype(dtype)


def apply_rope(t, cos, sin):
    c = cos[None, :, None, :]
    s = sin[None, :, None, :]
    t1 = t[..., 0::2]
    t2 = t[..., 1::2]
    return jnp.stack([t1 * c - t2 * s, t1 * s + t2 * c], axis=-1).reshape(t.shape)


def attn_q(p_q, q_norm_w):
    b, n = p_q.shape[:2]
    return rms_norm(p_q.reshape(b, n, N_Q_HEADS, HEAD_DIM), q_norm_w)


def attn_kv(p_kv, k_norm_w):
    b, n = p_kv.shape[:2]
    k = rms_norm(p_kv[..., :KV_W].reshape(b, n, N_KV_HEADS, HEAD_DIM), k_norm_w)
    v = p_kv[..., KV_W:].reshape(b, n, N_KV_HEADS, HEAD_DIM)
    return k, v


def gqa_attend(q, k, v):
    b, lq = q.shape[:2]
    qg = q.reshape(b, lq, N_KV_HEADS, Q_PER_KV, HEAD_DIM)
    s = jnp.einsum('bqkgd,bskd->bkgqs', qg, k, preferred_element_type=jnp.float32) * (HEAD_DIM ** -0.5)
    p = jax.nn.softmax(s, axis=-1).astype(v.dtype)
    o = jnp.einsum('bkgqs,bskd->bqkgd', p, v)
    return o.reshape(b, lq, ATTN_W)


def blocked_attention(q, k, v):
    b, lq = q.shape[:2]
    nb = lq // Q_BLOCK
    qb = q.reshape(b, nb, Q_BLOCK, N_Q_HEADS, HEAD_DIM).transpose(1, 0, 2, 3, 4)
    ob = lax.map(lambda qi: gqa_attend(qi, k, v), qb)
    return ob.transpose(1, 0, 2, 3).reshape(b, lq, ATTN_W)


def hyena_filters(n, lp):
    pos = jnp.arange(n, dtype=jnp.float32)
    t = pos / (n - 1)
    bands = jnp.linspace(1e-4, HYENA_BANDS - 1, HYENA_BANDS, dtype=jnp.float32)
    ang = (2.0 * math.pi / n) * pos[:, None] * bands
    z = jnp.concatenate([t[:, None], jnp.cos(ang), jnp.sin(ang)], axis=-1)
    freq = lp['hy_freq'].astype(jnp.float32)
    hdn = jnp.sin(freq * (z @ lp['hy_w1'].astype(jnp.float32) + lp['hy_b1'].astype(jnp.float32)))
    for i in range(HYENA_INNER):
        hdn = jnp.sin(freq * (hdn @ lp['hy_w2'][i].astype(jnp.float32) + lp['hy_b2'][i].astype(jnp.float32)))
    window = jnp.exp(-t[:, None] * jnp.abs(lp['hy_decay'].astype(jnp.float32)))
    filt = ((hdn @ lp['hy_w3'].astype(jnp.float32)) * window).reshape(n, HYENA_ORDER, 2, HYENA_W)
    fwd = filt[:, :, 0]
    bwd = filt[:, :, 1]
    k = jnp.concatenate([fwd, jnp.zeros((1, HYENA_ORDER, HYENA_W), jnp.float32), bwd[:0:-1]], axis=0)
    return k / jnp.sum(jnp.abs(k), axis=0, keepdims=True)


def long_conv(u, kf, skip):
    n = u.shape[1]
    uf = jnp.fft.rfft(u.astype(jnp.float32), n=2 * n, axis=1)
    y = jnp.fft.irfft(uf * kf[None], n=2 * n, axis=1)[:, :n]
    return (y + u.astype(jnp.float32) * skip.astype(jnp.float32)).astype(u.dtype)


def hyena_mixer(p, lp):
    n = p.shape[1]
    w = lp['hy_conv_w']
    zp = jnp.pad(p, ((0, 0), (1, 1), (0, 0)))
    z = zp[:, :-2] * w[0] + zp[:, 1:-1] * w[1] + zp[:, 2:] * w[2] + lp['hy_conv_b']
    v, x1, x2 = jnp.split(z, 3, axis=-1)
    kf = jnp.fft.rfft(hyena_filters(n, lp), n=2 * n, axis=0)
    y = v
    for o, gate in enumerate((x1, x2)):
        y = gate * long_conv(y, kf[:, o], lp['hy_skip'][o])
    return y


def pool_mixer(z, pool_w, pool_scale):
    b, n, _ = z.shape
    zf = z.astype(jnp.float32)
    csum = jnp.concatenate([jnp.zeros((b, 1, POOL_W), jnp.float32), jnp.cumsum(zf, axis=1)], axis=1)
    pos = jnp.arange(n)
    means = []
    for g, win in enumerate(POOL_WINDOWS):
        lo = jnp.clip(pos - win // 2, 0, n)
        hi = jnp.clip(pos + win - win // 2, 0, n)
        cg = csum[..., g * POOL_GROUP_W:(g + 1) * POOL_GROUP_W]
        means.append((cg[:, hi] - cg[:, lo]) / (hi - lo).astype(jnp.float32)[None, :, None])
    d = (jnp.concatenate(means, axis=-1) - zf).astype(z.dtype).reshape(b, n, len(POOL_WINDOWS), POOL_GROUP_W)
    y = jnp.einsum('bngc,gcd->bngd', d, pool_w).reshape(b, n, POOL_W)
    return y * pool_scale


def gmlp_mixer(p, norm_w, ws, bs):
    b, n, _ = p.shape
    u, v = jnp.split(p, 2, axis=-1)
    v = layer_norm(v, norm_w).reshape(b, n // GMLP_CHUNK, GMLP_CHUNK, GMLP_GROUPS, GMLP_GROUP_W)
    mixed = jnp.einsum('gts,bcsgd->bctgd', ws, v) + bs.T[None, None, :, :, None]
    return u * mixed.reshape(b, n, GMLP_W)


def parallel_mixers(p, attn_out, lp):
    hy = hyena_mixer(p[..., HY_OFF:POOL_OFF], lp)
    pl = pool_mixer(p[..., POOL_OFF:GM_OFF], lp['pool_w'], lp['pool_scale'])
    gm = gmlp_mixer(p[..., GM_OFF:GATE_OFF], lp['gm_norm_w'], lp['gm_ws'], lp['gm_bs'])
    g = jax.nn.sigmoid(p[..., GATE_OFF:].reshape(p.shape[:-1] + (N_BRANCH, D_MODEL)))
    merged = (g[..., 0, :] * (attn_out @ lp['w_br_attn'])
              + g[..., 1, :] * (hy @ lp['w_br_hyena'])
              + g[..., 2, :] * (pl @ lp['w_br_pool'])
              + g[..., 3, :] * (gm @ lp['w_br_gmlp']))
    return merged @ lp['w_out']


def swiglu(h, wg, wu, wd):
    return (jax.nn.silu(h @ wg) * (h @ wu)) @ wd


def setup_inputs(seed: int = 0) -> dict:
    key = jax.random.key(seed)
    ks = iter(jax.random.split(key, 48))

    def nrm(shape, scale):
        return jax.random.normal(next(ks), shape, jnp.float32) * scale

    L = DEPTH
    D = D_MODEL
    decay_base = jnp.linspace(HYENA_MIN_DECAY, HYENA_MAX_DECAY, HYENA_FILTER_OUT, dtype=jnp.float32)[None]
    return {
        'x': nrm((BATCH, SEQ, D), 1.0),
        'c': nrm((BATCH, D), 1.0),
        'ctx': nrm((BATCH, CTX_LEN, D), 1.0),
        'c_ctx': nrm((D,), 1.0),
        'w_mod': nrm((L, D, 6 * D), 0.5 * D ** -0.5),
        'b_mod': nrm((L, 6 * D), 0.02),
        'norm1_w': 1.0 + nrm((L, D), 0.05),
        'norm2_w': 1.0 + nrm((L, D), 0.05),
        'w_in': nrm((L, D, IN_W), D ** -0.5),
        'q_norm_w': 1.0 + nrm((L, HEAD_DIM), 0.05),
        'k_norm_w': 1.0 + nrm((L, HEAD_DIM), 0.05),
        'hy_conv_w': nrm((L, 3, 3 * HYENA_W), 3 ** -0.5),
        'hy_conv_b': nrm((L, 3 * HYENA_W), 0.02),
        'hy_w1': nrm((L, HYENA_POS_DIM, HYENA_HIDDEN), HYENA_POS_DIM ** -0.5),
        'hy_b1': nrm((L, HYENA_HIDDEN), 0.1),
        'hy_w2': nrm((L, HYENA_INNER, HYENA_HIDDEN, HYENA_HIDDEN), HYENA_HIDDEN ** -0.5),
        'hy_b2': nrm((L, HYENA_INNER, HYENA_HIDDEN), 0.1),
        'hy_w3': nrm((L, HYENA_HIDDEN, HYENA_FILTER_OUT), HYENA_HIDDEN ** -0.5),
        'hy_freq': 1.0 + nrm((L, HYENA_HIDDEN), 0.05),
        'hy_decay': decay_base * (1.0 + nrm((L, HYENA_FILTER_OUT), 0.05)),
        'hy_skip': nrm((L, HYENA_ORDER, HYENA_W), 0.5),
        'pool_w': nrm((L, len(POOL_WINDOWS), POOL_GROUP_W, POOL_GROUP_W), POOL_GROUP_W ** -0.5),
        'pool_scale': 1.0 + nrm((L, POOL_W), 0.1),
        'gm_norm_w': 1.0 + nrm((L, GMLP_W), 0.05),
        'gm_ws': nrm((L, GMLP_GROUPS, GMLP_CHUNK, GMLP_CHUNK), GMLP_CHUNK ** -0.5),
        'gm_bs': nrm((L, GMLP_GROUPS, GMLP_CHUNK), 0.1),
        'w_br_attn': nrm((L, ATTN_W, D), ATTN_W ** -0.5),
        'w_br_hyena': nrm((L, HYENA_W, D), HYENA_W ** -0.5),
        'w_br_pool': nrm((L, POOL_W, D), POOL_W ** -0.5),
        'w_br_gmlp': nrm((L, GMLP_W, D), GMLP_W ** -0.5),
        'w_out': nrm((L, D, D), D ** -0.5),
        'ffn_w_gate': nrm((L, D, FFN_HIDDEN), D ** -0.5),
        'ffn_w_up': nrm((L, D, FFN_HIDDEN), D ** -0.5),
        'ffn_w_down': nrm((L, FFN_HIDDEN, D), FFN_HIDDEN ** -0.5),
        'final_norm_w': 1.0 + nrm((D,), 0.05),
    }


def reference(x, c, ctx, c_ctx, w_mod, b_mod, norm1_w, norm2_w, w_in, q_norm_w, k_norm_w,
              hy_conv_w, hy_conv_b, hy_w1, hy_b1, hy_w2, hy_b2, hy_w3, hy_freq, hy_decay, hy_skip,
              pool_w, pool_scale, gm_norm_w, gm_ws, gm_bs,
              w_br_attn, w_br_hyena, w_br_pool, w_br_gmlp, w_out,
              ffn_w_gate, ffn_w_up, ffn_w_down, final_norm_w):
    cos, sin = axial_rope_tables(x.shape[1], x.dtype)
    xc = ctx
    act_lat = jax.nn.silu(c)
    act_ctx = jax.nn.silu(c_ctx)
    for l in range(DEPTH):
        lp = {
            'hy_conv_w': hy_conv_w[l], 'hy_conv_b': hy_conv_b[l],
            'hy_w1': hy_w1[l], 'hy_b1': hy_b1[l], 'hy_w2': hy_w2[l], 'hy_b2': hy_b2[l],
            'hy_w3': hy_w3[l], 'hy_freq': hy_freq[l], 'hy_decay': hy_decay[l], 'hy_skip': hy_skip[l],
            'pool_w': pool_w[l], 'pool_scale': pool_scale[l],
            'gm_norm_w': gm_norm_w[l], 'gm_ws': gm_ws[l], 'gm_bs': gm_bs[l],
            'w_br_attn': w_br_attn[l], 'w_br_hyena': w_br_hyena[l],
            'w_br_pool': w_br_pool[l], 'w_br_gmlp': w_br_gmlp[l], 'w_out': w_out[l],
        }
        mod = (act_lat @ w_mod[l] + b_mod[l])[:, None, :]
        sh1, s1, g1, sh2, s2, g2 = jnp.split(mod, 6, axis=-1)
        csh1, cs1, cg1, csh2, cs2, cg2 = jnp.split(act_ctx @ w_mod[l] + b_mod[l], 6)

        hc = modulate(xc, norm1_w[l], csh1, cs1)
        if l < DEPTH - 1:
            pc = hc @ w_in[l]
            k_c, v_c = attn_kv(pc[..., K_OFF:HY_OFF], k_norm_w[l])
            q_c = attn_q(pc[..., Q_OFF:K_OFF], q_norm_w[l])
            xc = xc + cg1 * parallel_mixers(pc, gqa_attend(q_c, k_c, v_c), lp)
            xc = xc + cg2 * swiglu(modulate(xc, norm2_w[l], csh2, cs2), ffn_w_gate[l], ffn_w_up[l], ffn_w_down[l])
        else:
            k_c, v_c = attn_kv(hc @ w_in[l][:, K_OFF:HY_OFF], k_norm_w[l])

        p = modulate(x, norm1_w[l], sh1, s1) @ w_in[l]
        q = apply_rope(attn_q(p[..., Q_OFF:K_OFF], q_norm_w[l]), cos, sin)
        k, v = attn_kv(p[..., K_OFF:HY_OFF], k_norm_w[l])
        k = apply_rope(k, cos, sin)
        attn = blocked_attention(q, jnp.concatenate([k_c, k], axis=1), jnp.concatenate([v_c, v], axis=1))
        x = x + g1 * parallel_mixers(p, attn, lp)
        x = x + g2 * swiglu(modulate(x, norm2_w[l], sh2, s2), ffn_w_gate[l], ffn_w_up[l], ffn_w_down[l])
    return rms_norm(x, final_norm_w)
```

```python
import numpy as np
import ml_dtypes
from contextlib import ExitStack
import concourse.bass as bass
import concourse.mybir as mybir
from concourse.bass_utils import run_bass_kernel_spmd

F32 = mybir.dt.float32
BF16 = mybir.dt.bfloat16
AF = mybir.ActivationFunctionType
ALU = mybir.AluOpType

D = 1024
SEQ = 8192
CTX = 256
DEPTH = 2
NCORE = 8
HID = 2816
EPS = 1e-6
NKEY = CTX + SEQ
PADC = 8


class Res:
    __slots__ = ("name", "readers", "writers", "multi", "sem")

    def __init__(self, name, multi=False):
        self.name = name
        self.readers = {}
        self.writers = {}
        self.multi = multi
        self.sem = None


class V:
    __slots__ = ("ap", "res", "sb")

    def __init__(self, ap, res, sb=True):
        self.ap = ap
        self.res = res
        self.sb = sb

    def __getitem__(self, k):
        return V(self.ap[k], self.res, self.sb)

    def rearrange(self, s, **kw):
        return V(self.ap.rearrange(s, **kw), self.res, self.sb)

    def bcast(self, shape):
        return V(self.ap.broadcast_to(list(shape)), self.res, self.sb)

    def unsqueeze(self, a):
        return V(self.ap.unsqueeze(a), self.res, self.sb)

    def pbcast(self, n):
        return V(self.ap.partition_broadcast(n), self.res, self.sb)

    def wr(self, res):
        return V(self.ap, res, self.sb)


class KB:
    def __init__(self, nc):
        self.nc = nc
        self.eng = {"pe": nc.tensor, "act": nc.scalar, "dve": nc.vector, "pool": nc.gpsimd, "sp": nc.sync}
        self.psem = {}
        self.pcnt = {}
        self.root = ExitStack()
        for e in ("pe", "act", "dve", "pool"):
            self.psem[e] = self.root.enter_context(nc.semaphore(f"prog_{e}"))
            self.pcnt[e] = 0
        self.waited = {e: {} for e in self.eng}
        self.dma_all = []
        self.dma_free = []
        self.phase_res = []
        self.stacks = [self.root]
        self.uid = 0
        self.dram_res = {}

    def _name(self, n):
        self.uid += 1
        return f"{n}_{self.uid}"

    def tile(self, name, shape, dt, multi=False):
        t = self.stacks[-1].enter_context(self.nc.sbuf_tensor(self._name(name), list(shape), dt))
        r = Res(name, multi)
        self.phase_res.append(r)
        return V(t[tuple(slice(None) for _ in shape)], r, True)

    def ring(self, name, n, shape, dt, multi=False):
        return [self.tile(f"{name}{i}", shape, dt, multi) for i in range(n)]

    def psum(self, name, shape, dt=F32):
        t = self.stacks[-1].enter_context(self.nc.psum_tensor(self._name(name), list(shape), dt))
        r = Res(name)
        self.phase_res.append(r)
        return V(t[tuple(slice(None) for _ in shape)], r, True)

    def dram(self, name, shape, dt, kind="Internal"):
        t = self.nc.dram_tensor(name, list(shape), dt, kind=kind)
        return V(t.ap(), Res(name, True), False)

    def dres(self, v, key):
        k = (v.res.name, key)
        if k not in self.dram_res:
            self.dram_res[k] = Res(f"{v.res.name}:{key}", True)
        return v.wr(self.dram_res[k])

    def push(self):
        self.stacks.append(ExitStack())
        self.phase_res = []

    def pop(self):
        self.barrier()
        for r in self.phase_res:
            if r.sem is not None:
                if "hw" in r.sem:
                    self.dma_free.append(r.sem["hw"])
                r.sem = None
        self.phase_res = []
        self.stacks.pop().close()

    def _deps(self, outs, ins):
        d = {}

        def add(tokd):
            for k, (s, c) in tokd.items():
                if k not in d or d[k][1] < c:
                    d[k] = (s, c)

        for v in ins:
            add(v.res.writers)
        for v in outs:
            add(v.res.readers)
            if not v.res.multi:
                add(v.res.writers)
        return d

    def _wait(self, e, deps, skip_own=True):
        own = self.psem[e].name if (skip_own and e == "pe") else None
        w = self.waited[e]
        for k, (s, c) in deps.items():
            if k == own:
                continue
            if w.get(k, 0) >= c:
                continue
            self.eng[e].wait_ge(s, c)
            w[k] = c

    def _commit(self, outs, ins, key, tok):
        for v in ins:
            r = v.res.readers
            if key not in r or r[key][1] < tok[1]:
                r[key] = tok
        for v in outs:
            if v.res.multi:
                v.res.writers[key] = tok
                v.res.readers = {}
            else:
                v.res.writers = {key: tok}
                v.res.readers = {}

    def op(self, e, fn, outs, ins):
        outs = [o for o in outs if isinstance(o, V)]
        ins = [i for i in ins if isinstance(i, V)]
        self._wait(e, self._deps(outs, ins))
        inst = fn()
        self.pcnt[e] += 1
        inst.then_inc(self.psem[e], 1)
        self._commit(outs, ins, self.psem[e].name, (self.psem[e], self.pcnt[e]))
        return inst

    def dma(self, q, out, in_, extra_in=(), extra_out=()):
        sbv = out if out.sb else in_
        outs = [out] + list(extra_out)
        ins = [in_] + list(extra_in)
        self._wait(q, self._deps(outs, ins), skip_own=False)
        res = sbv.res
        qt = "sw" if q == "pool" else "hw"
        if res.sem is None:
            res.sem = {}
        if qt not in res.sem:
            if qt == "hw" and self.dma_free:
                res.sem[qt] = self.dma_free.pop()
            else:
                s = self.root.enter_context(self.nc.semaphore(self._name("dma" + qt)))
                res.sem[qt] = [s, 0]
                self.dma_all.append(res.sem[qt])
        sm = res.sem[qt]
        inst = self.eng[q].dma_start(out=out.ap, in_=in_.ap)
        sm[1] += 16
        inst.then_inc(sm[0], 16)
        self._commit(outs, ins, sm[0].name, (sm[0], sm[1]))
        return inst

    def barrier(self):
        toks = {}
        for e in self.psem:
            toks[self.psem[e].name] = (self.psem[e], self.pcnt[e])
        for s in self.dma_all:
            if s[1] > 0:
                toks[s[0].name] = (s[0], s[1])
        for e in self.eng:
            self._wait(e, toks, skip_own=True)

    @staticmethod
    def _a(x):
        return x.ap if isinstance(x, V) else x

    def mm(self, out, lhsT, rhs, start=True, stop=True):
        return self.op("pe", lambda: self.nc.tensor.matmul(out.ap, lhsT=lhsT.ap, rhs=rhs.ap, start=start, stop=stop),
                       [out], [lhsT, rhs])

    def transpose(self, out, in_, ident):
        return self.op("pe", lambda: self.nc.tensor.transpose(out.ap, in_.ap, ident.ap), [out], [in_, ident])

    def act(self, out, in_, func, bias=0.0, scale=1.0, accum=None):
        a = self._a
        kw = {}
        if accum is not None:
            kw["accum_out"] = accum.ap
        return self.op("act", lambda: self.nc.scalar.activation(out=out.ap, in_=in_.ap, func=func, bias=a(bias),
                                                                scale=a(scale), **kw),
                       [out, accum], [in_, bias, scale])

    def tt(self, e, out, in0, in1, op):
        return self.op(e, lambda: self.eng[e].tensor_tensor(out=out.ap, in0=in0.ap, in1=in1.ap, op=op),
                       [out], [in0, in1])

    def ts(self, e, out, in0, s1, op0, s2=None, op1=None):
        a = self._a
        if op1 is None:
            f = lambda: self.eng[e].tensor_scalar(out=out.ap, in0=in0.ap, scalar1=a(s1), scalar2=None, op0=op0)
        else:
            f = lambda: self.eng[e].tensor_scalar(out=out.ap, in0=in0.ap, scalar1=a(s1), scalar2=a(s2), op0=op0,
                                                  op1=op1)
        return self.op(e, f, [out], [in0, s1, s2])

    def stt(self, out, in0, scalar, in1, op0, op1):
        a = self._a
        return self.op("dve", lambda: self.nc.vector.scalar_tensor_tensor(out=out.ap, in0=in0.ap, scalar=a(scalar),
                                                                          in1=in1.ap, op0=op0, op1=op1),
                       [out], [in0, scalar, in1])

    def copy(self, e, out, in_):
        if e == "act":
            return self.op("act", lambda: self.nc.scalar.copy(out=out.ap, in_=in_.ap), [out], [in_])
        return self.op(e, lambda: self.eng[e].tensor_copy(out=out.ap, in_=in_.ap), [out], [in_])

    def memset(self, e, out, val):
        return self.op(e, lambda: self.eng[e].memset(out.ap, val), [out], [])

    def recip(self, out, in_):
        return self.op("dve", lambda: self.nc.vector.reciprocal(out=out.ap, in_=in_.ap), [out], [in_])


def _bf(a):
    return np.asarray(a, dtype=np.float32).astype(ml_dtypes.bfloat16)


def _rope_tables():
    rows = SEQ // 64
    r, col = np.meshgrid(np.arange(rows, dtype=np.float32), np.arange(64, dtype=np.float32), indexing="ij")
    inv = (np.float32(10000.0) ** (-np.arange(0, 32, 2, dtype=np.float32) / np.float32(32))).astype(np.float32)
    ang = np.concatenate([r.reshape(-1, 1) * inv, col.reshape(-1, 1) * inv], axis=-1).astype(np.float32)
    cos = np.cos(ang).astype(np.float32)
    sin = np.sin(ang).astype(np.float32)
    p = np.arange(128)
    i = (p % 64) // 2
    cosF = cos[:, i].T.copy()
    sgn = np.where(p % 2 == 0, -1.0, 1.0).astype(np.float32)
    sinS = (sin[:, i].T * sgn[:, None]).astype(np.float32)
    return np.ascontiguousarray(cosF), np.ascontiguousarray(sinS)


def _hy_pos(n):
    pos = np.arange(n, dtype=np.float32)
    t = (pos / np.float32(n - 1)).astype(np.float32)
    bands = np.linspace(1e-4, 15, 16, dtype=np.float32)
    ang = (np.float32(2.0 * np.pi / n) * pos[:, None] * bands).astype(np.float32)
    z = np.concatenate([t[:, None], np.cos(ang), np.sin(ang)], axis=-1).astype(np.float32)
    return z, t


def _pool_invcnt(n):
    out = np.zeros((128, 2, n), np.float32)
    pos = np.arange(n)
    for g, win in enumerate((2, 4, 8, 16)):
        lo = np.clip(pos - win // 2, 0, n)
        hi = np.clip(pos + win - win // 2, 0, n)
        ic = (1.0 / (hi - lo).astype(np.float32)).astype(np.float32)
        ch, half = g // 2, g % 2
        out[half * 64:(half + 1) * 64, ch, :] = ic[None, :]
    return out


def _hy_tables(n):
    z, t = _hy_pos(n)
    zf = z.T.copy()
    zr = np.empty_like(zf)
    zr[:, 0] = zf[:, 0]
    zr[:, 1:] = zf[:, :0:-1]
    tr = np.zeros((2, n), np.float32)
    tr[0] = t
    tr[1, 1:] = t[:0:-1]
    return np.stack([zf, zr]).astype(np.float32), tr


def _dft_tables(n1):
    N = n1 * 128
    nh = n1 // 2
    a = np.arange(128, dtype=np.float64)
    b = np.arange(n1, dtype=np.float64)
    C = np.cos(2 * np.pi * np.outer(a, a) / 128.0)
    S_ = np.sin(2 * np.pi * np.outer(a, a) / 128.0)
    C1 = np.cos(2 * np.pi * np.outer(b, b) / n1)
    S1 = np.sin(2 * np.pi * np.outer(b, b) / n1)
    twc = np.cos(2 * np.pi * np.outer(a, b) / N)
    tws = np.sin(2 * np.pi * np.outer(a, b) / N)
    FC = 4 * n1 + 512
    tf = np.zeros((128, FC), np.float32)
    tf[:, 0:n1] = twc
    tf[:, n1:2 * n1] = twc
    tf[:, 2 * n1:3 * n1] = tws
    tf[:, 3 * n1:4 * n1] = -tws
    o = 4 * n1
    tf[0:n1, o:o + 128] = twc.T
    tf[0:n1, o + 128:o + 256] = twc.T
    tf[0:n1, o + 256:o + 384] = tws.T
    tf[0:n1, o + 384:o + 512] = -tws.T
    BC = 2 * n1 + 128 * 3 + 512 + 2 * nh
    tb = np.zeros((128, BC), np.float32)
    tb[0:n1, 0:n1] = C1
    tb[0:n1, n1:2 * n1] = -S1
    o = 2 * n1
    tb[:, o:o + 128] = C
    tb[:, o + 128:o + 256] = S_
    tb[:, o + 256:o + 384] = -S_
    o += 384
    tb[:, o:o + 128] = C
    tb[:, o + 128:o + 256] = S_
    tb[:, o + 256:o + 384] = -S_
    tb[:, o + 384:o + 512] = C
    o += 512
    tb[0:n1, o:o + nh] = C1[:, 0:nh] / N
    tb[0:n1, o + nh:o + 2 * nh] = -S1[:, 0:nh] / N
    return tf, _bf(tb)


class Stream:
    pass


def build_program(dbg=None, nlayers=DEPTH, stop_phase=None, skip=()):
    nc = bass.Bass("TRN2", target_bir_lowering=False)
    kb = KB(nc)
    dbg = dbg or []

    def din(name, shape, dt=F32):
        return kb.dram(name, shape, dt, kind="ExternalInput")

    def dsc(name, shape, dt):
        return kb.dram(name, shape, dt, kind=("ExternalOutput" if name in dbg else "Internal"))

    xT_in = din("xT", [D, SEQ])
    ctxT_in = din("ctxT", [D, CTX])
    cvec = din("cvec", [128, 8, 2])
    w_mod = din("w_mod", [DEPTH, D, 6 * D])
    w_in = din("w_in", [DEPTH, D, 6400])
    w_br_attn = din("w_br_attn", [DEPTH, 512, D])
    w_br_hy = din("w_br_hyena", [DEPTH, 256, D])
    w_br_pool = din("w_br_pool", [DEPTH, 256, D])
    w_br_gm = din("w_br_gmlp", [DEPTH, 256, D])
    w_out = din("w_out", [DEPTH, D, D])
    ffn_g = din("ffn_w_gate", [DEPTH, D, HID])
    ffn_u = din("ffn_w_up", [DEPTH, D, HID])
    ffn_d = din("ffn_w_down", [DEPTH, HID, D])
    smallp_d = din("smallp", [128, NSP])
    gm_wsT = din("gm_wsT", [DEPTH, 4, 128, 128])
    gm_nw = din("gm_norm_w", [DEPTH, 256])
    pool_w = din("pool_w", [DEPTH, 4, 64, 64])
    hy_w1 = din("hy_w1", [DEPTH, 33, 64])
    hy_w2 = din("hy_w2", [DEPTH, 2, 64, 64])
    hy_w3 = din("hy_w3", [DEPTH, 64, 1024])
    hy_skip = din("hy_skip", [DEPTH, 512])
    cbf = din("cbf", [128, 3, 128], BF16)
    ropeT = din("ropeT", [2, 128, SEQ])
    invcnt_l = din("invcnt_l", [128, 2, SEQ])
    invcnt_c = din("invcnt_c", [128, 2, CTX])
    outT = kb.dram("outT", [D, SEQ], F32, kind="ExternalOutput")

    xa = dsc("xa", [D, SEQ], F32)
    xb = dsc("xb", [D, SEQ], F32)
    ca = dsc("ca", [D, CTX], F32)
    cb = dsc("cb", [D, CTX], F32)
    kT2d = dsc("kT2d", [128, 2, NKEY], BF16)
    vaugd = dsc("vaugd", [NKEY // 128, 128, 2, 66], BF16)

    def mk_stream(name, S, T, xin, koff):
        st = Stream()
        st.name, st.S, st.T, st.NT, st.koff = name, S, T, S // T, koff
        st.x = xin
        st.qT = dsc(f"qT_{name}", [128, 4, S], BF16)
        st.pT = dsc(f"pT_{name}", [D, S + 2 * PADC], BF16)
        st.zT = dsc(f"zT_{name}", [768, S], BF16)
        st.plT = dsc(f"plT_{name}", [256, S], BF16)
        st.gmT = dsc(f"gmT_{name}", [256, S], BF16)
        st.hyT = dsc(f"hyT_{name}", [256, S], BF16)
        st.attnT = dsc(f"attnT_{name}", [64, 8, S], BF16)
        st.rope = (name == "lat")
        st.si = 0 if name == "lat" else 1
        st.invcnt = invcnt_l if name == "lat" else invcnt_c
        n1 = 2 * S // 128
        st.zpos = din(f"zpos_{name}", [2, 33, S])
        st.trow = din(f"trow_{name}", [2, S])
        st.dftf = din(f"dftf_{name}", [128, 4 * n1 + 512])
        st.dftb = din(f"dftb_{name}", [128, 2 * n1 + 384 + 512 + n1], BF16)
        st.kfT = dsc(f"kfT_{name}", [512, 2 * S], BF16)
        return st

    lat = mk_stream("lat", SEQ, 512, xT_in, CTX)
    cxs = mk_stream("ctx", CTX, 256, ctxT_in, 0)

    identb = kb.tile("identb", [128, 128], BF16)
    permb = kb.tile("permb", [128, 128], BF16)
    ones1024 = kb.tile("ones1024", [128, 128], BF16)
    blk64 = kb.tile("blk64", [128, 128], BF16)
    onesf = kb.tile("onesf", [128, 64], F32)
    epst = kb.tile("epst", [128, 1], F32)
    smallp = kb.tile("smallp", [128, NSP], F32)
    modp = kb.tile("modp", [128, 2, 6, 8], F32)
    actbf = kb.tile("actbf", [128, 8, 2], BF16)
    zero_bf = kb.tile("zero_bf", [128, 8, PADC], BF16)

    kb.dma("sp", identb, cbf[:, 0, :])
    kb.dma("sp", permb, cbf[:, 1, :])
    kb.dma("sp", smallp, smallp_d)
    kb.memset("dve", ones1024, 1.0 / 1024.0)
    kb.memset("dve", blk64, 0.0)
    kb.memset("dve", blk64[0:64, 0:64], 1.0 / 64.0)
    kb.memset("dve", blk64[64:128, 64:128], 1.0 / 64.0)
    kb.memset("dve", onesf, 1.0)
    kb.memset("dve", epst, EPS)
    kb.memset("dve", zero_bf, 0.0)
    for st in (lat, cxs):
        pv = st.pT.rearrange("(c p) t -> p c t", p=128)
        kb.dma("sp", kb.dres(pv[:, :, 0:PADC], "padlo"), zero_bf)
        kb.dma("sp", kb.dres(pv[:, :, PADC + st.S:2 * PADC + st.S], "padhi"), zero_bf)
    kb.push()
    cv = kb.tile("cv", [128, 8, 2], F32)
    kb.dma("sp", cv, cvec)
    kb.act(actbf, cv, AF.Silu)
    kb.pop()

    def sp_col(name, l=None):
        o, n = SP_OFF[name]
        if l is not None:
            per = n // DEPTH
            return smallp[:, o + l * per:o + (l + 1) * per]
        return smallp[:, o:o + n]

    def phase_M(l):
        kb.push()
        wm = kb.ring("wm", 2, [128, 8, 1536], BF16)
        pm = kb.psum("pm", [128, 48, 2])
        modt = kb.tile("modt", [128, 48, 2], F32)
        wsrc = w_mod[l].rearrange("(kc p) n -> p kc n", p=128)
        for blk in range(4):
            w = wm[blk % 2]
            for kc in range(8):
                kb.dma("pool", w[:, kc, :], wsrc[:, kc, blk * 1536:(blk + 1) * 1536])
            for j in range(12):
                oc = blk * 12 + j
                for kc in range(8):
                    kb.mm(pm[:, oc, :], w[:, kc, j * 128:(j + 1) * 128], actbf[:, kc, :], start=(kc == 0),
                          stop=(kc == 7))
        bm = sp_col("b_mod", l)
        kb.tt("dve", modt, pm, bm.unsqueeze(2).bcast([128, 48, 2]), ALU.add)
        n1 = sp_col("norm1_w", l)
        n2 = sp_col("norm2_w", l)
        for s in range(2):
            kb.stt(modp[:, s, 0, :], modt[:, 8:16, s], 1.0, n1, ALU.add, ALU.mult)
            kb.copy("dve", modp[:, s, 1, :], modt[:, 0:8, s])
            kb.copy("dve", modp[:, s, 2, :], modt[:, 16:24, s])
            kb.stt(modp[:, s, 3, :], modt[:, 32:40, s], 1.0, n2, ALU.add, ALU.mult)
            kb.copy("dve", modp[:, s, 4, :], modt[:, 24:32, s])
            kb.copy("dve", modp[:, s, 5, :], modt[:, 40:48, s])
        kb.pop()

    def norm_mod(st, xt, sq, h, ps_ss, srt, rstd, which, TT):
        ai = 0 if which == 1 else 3
        kb.act(sq[:, :, :TT], xt[:, :, :TT], AF.Square)
        for c in range(8):
            kb.mm(ps_ss[:, :TT], ones1024, sq[:, c, :TT], start=(c == 0), stop=(c == 7))
        kb.act(srt[:, :TT], ps_ss[:, :TT], AF.Sqrt, bias=epst)
        kb.recip(rstd[:, :TT], srt[:, :TT])
        return ai

    def norm_apply(st, xt, xs, h, rstd, ai, TT):
        kb.tt("dve", xs[:, :, :TT], xt[:, :, :TT], rstd[:, :TT].unsqueeze(1).bcast([128, 8, TT]), ALU.mult)
        for c in range(8):
            kb.act(h[:, c, :TT], xs[:, c, :TT], AF.Identity, bias=modp[:, st.si, ai + 1, c:c + 1],
                   scale=modp[:, st.si, ai, c:c + 1])

    def phase_A(l):
        kb.push()
        wA = kb.tile("wA", [128, 8, 2432], BF16, multi=True)
        wsrc = w_in[l].rearrange("(kc p) n -> p kc n", p=128)
        for (d0, s0, n) in ((0, 0, 512), (512, 512, 64), (576, 512, 64), (640, 576, 64), (704, 576, 64),
                            (768, 768, 768), (1536, 1536, 256), (1792, 640, 128), (1920, 1792, 512)):
            for kc in range(8):
                kb.dma("pool", wA[:, kc, d0:d0 + n], wsrc[:, kc, s0:s0 + n])
        poolW = kb.tile("poolW", [128, 2, 128], BF16)
        kb.memset("dve", poolW, 0.0)
        for g in range(4):
            ch, hf = g // 2, g % 2
            kb.dma("pool", poolW[hf * 64:(hf + 1) * 64, ch, hf * 64:(hf + 1) * 64], pool_w[l, g])
        wsT = kb.tile("wsT", [128, 4, 128], BF16, multi=True)
        for g in range(4):
            kb.dma("pool", wsT[:, g, :], gm_wsT[l, g])
        gnw = kb.tile("gnw", [128, 256], F32)
        kb.dma("sp", gnw, gm_nw[l, :].pbcast(128))

        xt_r = kb.ring("xt", 1, [128, 8, 512], F32)
        sq = kb.tile("sq", [128, 8, 512], BF16)
        srt = kb.tile("srt", [128, 512], F32)
        rstd = kb.tile("rstd", [128, 512], F32)
        h_r = kb.ring("h", 2, [128, 8, 512], BF16)
        sqq_r = kb.ring("sqq", 2, [128, 512], BF16)
        srq = kb.tile("srq", [128, 512], F32)
        rq = kb.tile("rq", [128, 512], F32)
        qn_r = kb.ring("qn", 2, [128, 512], BF16)
        t1 = kb.tile("t1", [128, 512], F32)
        t2 = kb.tile("t2", [128, 512], F32)
        rope_r = kb.ring("rope", 2, [128, 2, 512], F32)
        qst_r = kb.ring("qst", 1, [128, 4, 512], BF16)
        kst_r = kb.ring("kst", 2, [128, 2, 512], BF16)
        vst_r = kb.ring("vst", 2, [128, 4, 2, 66], BF16)
        pst_r = kb.ring("pst", 1, [128, 8, 512], BF16)
        usb_r = kb.ring("usb", 2, [128, 256], F32)
        vnn = kb.tile("vnn", [128, 256], F32)
        vn_r = kb.ring("vn", 2, [128, 256], BF16)
        gmt_r = kb.ring("gmt", 2, [128, 256], BF16)
        stat = kb.tile("stat", [128, 8], F32)
        junk = kb.tile("junk", [128, 256], F32)
        gmst_r = kb.ring("gmst", 2, [128, 2, 512], BF16)
        pp_r = kb.ring("pp", 2, [128, 8, 512 + 2 * PADC], BF16)
        c1 = kb.tile("c1", [128, 512], F32)
        c2 = kb.tile("c2", [128, 512], F32)
        zst_r = kb.ring("zst", 1, [128, 6, 512], BF16)
        pa = [kb.tile(f"pa{i}", [128, 2, 512 + 2 * PADC], F32) for i in range(2)]
        icn_r = kb.ring("icn", 1, [128, 2, 512], F32)
        dpl = kb.tile("dpl", [128, 2, 512], F32)
        dpb = kb.tile("dpb", [128, 2, 512], BF16)
        plst_r = kb.ring("plst", 2, [128, 2, 512], BF16)

        ps_main = [kb.psum(f"ps_main{i}", [128, 512]) for i in range(2)]
        ps_qk = [kb.psum(f"ps_qk{i}", [128, 512]) for i in range(2)]
        ps_ss = ps_qk[0]
        ps_tok = [kb.psum(f"ps_tok{i}", [128, 512]) for i in range(2)]
        ps_gs = kb.psum("ps_gs", [128, 256])
        ps_gt_t = kb.stacks[-1].enter_context(nc.psum_tensor(kb._name("ps_gt"), [128, 2, 128], BF16))
        ps_gt = V(ps_gt_t[:, :, :], Res("ps_gt"), True)

        for r in vst_r:
            kb.memset("dve", r[:, :, :, 64:66], 1.0)

        wq = sp_col("q_norm_w", l)
        wk = sp_col("k_norm_w", l)
        hw = sp_col("hy_conv_w", l)
        hb = sp_col("hy_conv_b", l)
        pscale = sp_col("pool_scale", l)
        bsT = sp_col("gm_bs", l)

        cnt = {"main": 0, "qk": 0, "tok": 0, "h": 0}

        hcur = {}

        def prep_A(st, i):
            TT = st.T
            t0 = i * TT
            xt = xt_r[0]
            h = h_r[cnt["h"] % 2]
            cnt["h"] += 1
            xsrc = st.x.rearrange("(c p) t -> p c t", p=128)
            kb.dma("sp", xt[:, :, :TT], kb.dres(xsrc[:, :, t0:t0 + TT], i))
            rp = None
            if st.rope:
                rp = rope_r[i % 2]
                kb.dma("sp", rp[:, :, :TT], ropeT[:, :, t0:t0 + TT].rearrange("a p t -> p a t"))
            ai = norm_mod(st, xt, sq, h, ps_ss, srt, rstd, 1, TT)
            norm_apply(st, xt, xt, h, rstd, ai, TT)
            hcur[(st.name, i)] = (h, rp)

        def tile_A(st, i, full, mid_hook=None):
            TT = st.T
            t0 = i * TT
            h, rp = hcur.pop((st.name, i))
            qst = qst_r[0]
            kst = kst_r[i % 2]
            vst = vst_r[i % 2]
            pst = pst_r[0]
            gmst = gmst_r[i % 2]
            chunks = []
            if full:
                chunks += [("q", j, j * 128) for j in range(4)]
            chunks += [("k", 0, 512), ("k", 1, 640)]
            if full:
                chunks += [("p", j, 768 + j * 128) for j in range(8)]
            for ci, (kind, idx, c0) in enumerate(chunks):
                if mid_hook is not None and ci == min(8, len(chunks) - 1):
                    mid_hook()
                ps = ps_main[cnt["main"] % 2]
                cnt["main"] += 1
                for kc in range(8):
                    kb.mm(ps[:, :TT], wA[:, kc, c0:c0 + 128], h[:, kc, :TT], start=(kc == 0), stop=(kc == 7))
                if kind == "p":
                    kb.copy("act", pst[:, idx, :TT], ps[:, :TT])
                    continue
                sqq = sqq_r[cnt["qk"] % 2]
                qn = qn_r[cnt["qk"] % 2]
                psq = ps_qk[0]
                psw = ps_qk[1]
                cnt["qk"] += 1
                kb.act(sqq[:, :TT], ps[:, :TT], AF.Square)
                kb.mm(psq[:, :TT], blk64, sqq[:, :TT])
                kb.act(srq[:, :TT], psq[:, :TT], AF.Sqrt, bias=epst)
                kb.recip(rq[:, :TT], srq[:, :TT])
                dst = qst[:, idx, :TT] if kind == "q" else kst[:, idx, :TT]
                wn = wq if kind == "q" else wk
                if st.rope:
                    kb.stt(qn[:, :TT], ps[:, :TT], wn[:, 0:1], rq[:, :TT], ALU.mult, ALU.mult)
                    kb.mm(psw[:, :TT], permb, qn[:, :TT])
                    kb.tt("pool", t1[:, :TT], qn[:, :TT], rp[:, 0, :TT], ALU.mult)
                    kb.tt("dve", t2[:, :TT], psw[:, :TT], rp[:, 1, :TT], ALU.mult)
                    kb.tt("pool", dst, t1[:, :TT], t2[:, :TT], ALU.add)
                else:
                    kb.stt(dst, ps[:, :TT], wn[:, 0:1], rq[:, :TT], ALU.mult, ALU.mult)
            nsub = TT // 128
            for sub in range(nsub):
                ts_ = slice(sub * 128, (sub + 1) * 128)
                ps = ps_tok[cnt["tok"] % 2]
                cnt["tok"] += 1
                for kc in range(8):
                    kb.mm(ps[:, 0:128], h[:, kc, ts_], wA[:, kc, 1792:1920], start=(kc == 0), stop=(kc == 7))
                kb.copy("act", vst[:, sub, :, 0:64], ps[:, 0:128].rearrange("p (k d) -> p k d", k=2))
                if not full:
                    continue
                ps = ps_tok[cnt["tok"] % 2]
                cnt["tok"] += 1
                for kc in range(8):
                    kb.mm(ps[:, 0:512], h[:, kc, ts_], wA[:, kc, 1920:2432], start=(kc == 0), stop=(kc == 7))
                usb = usb_r[sub % 2]
                vn = vn_r[sub % 2]
                gmt = gmt_r[sub % 2]
                kb.copy("act", usb, ps[:, 0:256])
                kb.memset("dve", stat[:, 0:2], 0.0)
                kb.act(junk, ps[:, 256:512], AF.Identity, accum=stat[:, 0:1])
                kb.act(junk, ps[:, 256:512], AF.Square, accum=stat[:, 1:2])
                kb.ts("dve", stat[:, 2:3], stat[:, 0:1], 1.0 / 256.0, ALU.mult)
                kb.tt("dve", stat[:, 3:4], stat[:, 2:3], stat[:, 2:3], ALU.mult)
                kb.stt(stat[:, 4:5], stat[:, 1:2], 1.0 / 256.0, stat[:, 3:4], ALU.mult, ALU.subtract)
                kb.act(stat[:, 5:6], stat[:, 4:5], AF.Sqrt, bias=epst)
                kb.recip(stat[:, 6:7], stat[:, 5:6])
                kb.stt(stat[:, 7:8], stat[:, 2:3], -1.0, stat[:, 6:7], ALU.mult, ALU.mult)
                kb.act(vnn, ps[:, 256:512], AF.Identity, bias=stat[:, 7:8], scale=stat[:, 6:7])
                kb.tt("pool", vn, vnn, gnw, ALU.mult)
                for g in range(4):
                    gs = slice(g * 64, (g + 1) * 64)
                    kb.mm(ps_gs[:, gs], wsT[:, g, :], vn[:, gs])
                for g in range(4):
                    gs = slice(g * 64, (g + 1) * 64)
                    kb.stt(gmt[:, gs], ps_gs[:, gs], bsT[:, g:g + 1], usb[:, gs], ALU.add, ALU.mult)
                for hf in range(2):
                    kb.transpose(ps_gt[:, hf, :], gmt[:, hf * 128:(hf + 1) * 128], identb)
                kb.copy("dve", gmst[:, :, ts_], ps_gt)
            if full:
                kb.dma("sp", kb.dres(st.qT[:, :, t0:t0 + TT], i), qst[:, :, :TT])
                pv = st.pT.rearrange("(c p) t -> p c t", p=128)
                kb.dma("sp", kb.dres(pv[:, :, PADC + t0:PADC + t0 + TT], i), pst[:, :, :TT])
                gv = st.gmT.rearrange("(c p) t -> p c t", p=128)
                kb.dma("sp", kb.dres(gv[:, :, t0:t0 + TT], i), gmst[:, :, :TT])
            kb.dma("sp", kb.dres(kT2d[:, :, st.koff + t0:st.koff + t0 + TT], (st.name, i)), kst[:, :, :TT])
            c0 = (st.koff + t0) // 128
            kb.dma("sp", kb.dres(vaugd[c0:c0 + nsub].rearrange("c p k d -> p c k d"), (st.name, i)),
                   vst[:, :nsub])

        def tile_A2(st, i):
            TT = st.T
            t0 = i * TT
            W = TT + 2 * PADC
            pp = pp_r[i % 2]
            icn = icn_r[0]
            zst = zst_r[0]
            plst = plst_r[i % 2]
            pv = st.pT.rearrange("(c p) t -> p c t", p=128)
            extra = [kb.dres(pv, k) for k in (i - 1, i + 1) if 0 <= k < st.NT]
            extra += [kb.dres(pv, "padlo"), kb.dres(pv, "padhi")]
            kb.dma("sp", pp[:, :, :W], kb.dres(pv[:, :, t0:t0 + W], i), extra_in=extra)
            kb.dma("sp", icn[:, :, :TT], st.invcnt[:, :, t0:t0 + TT])
            for c in range(6):
                kb.ts("pool", c1[:, :TT], pp[:, c, PADC - 1:PADC - 1 + TT], hw[:, c:c + 1], ALU.mult)
                kb.ts("pool", c2[:, :TT], pp[:, c, PADC:PADC + TT], hw[:, 6 + c:7 + c], ALU.mult)
                kb.tt("pool", c1[:, :TT], c1[:, :TT], c2[:, :TT], ALU.add)
                kb.ts("pool", c2[:, :TT], pp[:, c, PADC + 1:PADC + 1 + TT], hw[:, 12 + c:13 + c], ALU.mult)
                kb.tt("pool", c1[:, :TT], c1[:, :TT], c2[:, :TT], ALU.add)
                kb.ts("pool", zst[:, c, :TT], c1[:, :TT], hb[:, c:c + 1], ALU.add)
            zv = st.zT.rearrange("(c p) t -> p c t", p=128)
            kb.dma("sp", kb.dres(zv[:, :, t0:t0 + TT], i), zst[:, :, :TT])
            z = pp[:, 6:8, :]
            A_, B_ = pa[0], pa[1]
            kb.tt("dve", A_[:, :, 1:W], z[:, :, 1:W], z[:, :, 0:W - 1], ALU.add)
            kb.tt("dve", B_[:, :, 3:W], A_[:, :, 3:W], A_[:, :, 1:W - 2], ALU.add)

            def sel(hf, ch, arr, sh):
                ps_ = slice(hf * 64, (hf + 1) * 64)
                kb.tt("dve", dpl[ps_, ch, :TT], arr[ps_, ch, PADC + sh:PADC + sh + TT], icn[ps_, ch, :TT], ALU.mult)
                kb.tt("dve", dpb[ps_, ch, :TT], dpl[ps_, ch, :TT], pp[ps_, 6 + ch, PADC:PADC + TT], ALU.subtract)

            sel(0, 0, A_, 0)
            sel(1, 0, B_, 1)
            kb.tt("dve", A_[:, 1, 7:W], B_[:, 1, 7:W], B_[:, 1, 3:W - 4], ALU.add)
            kb.tt("dve", B_[:, 1, 15:W], A_[:, 1, 15:W], A_[:, 1, 7:W - 8], ALU.add)
            sel(0, 1, A_, 3)
            sel(1, 1, B_, 7)
            for ch in range(2):
                ps = ps_main[cnt["main"] % 2]
                cnt["main"] += 1
                kb.mm(ps[:, :TT], poolW[:, ch, :], dpb[:, ch, :TT])
                kb.act(plst[:, ch, :TT], ps[:, :TT], AF.Identity, scale=pscale[:, ch:ch + 1])
            plv = st.plT.rearrange("(c p) t -> p c t", p=128)
            kb.dma("sp", kb.dres(plv[:, :, t0:t0 + TT], i), plst[:, :, :TT])

        fullc = (l < DEPTH - 1)
        prep_A(cxs, 0)
        tile_A(cxs, 0, fullc, mid_hook=lambda: prep_A(lat, 0))
        if fullc:
            tile_A2(cxs, 0)
        for i in range(lat.NT):
            hook = (lambda j=i: prep_A(lat, j + 1)) if i + 1 < lat.NT else None
            tile_A(lat, i, True, mid_hook=hook)
            if i >= 1:
                tile_A2(lat, i - 1)
        tile_A2(lat, lat.NT - 1)
        kb.pop()

    def phase_H(l, st):
        S = st.S
        N1 = 2 * S // 128
        NH = N1 // 2
        PI = float(np.pi)
        kb.push()
        SLH = min(2048, S)
        SLW = min(512, S)
        NSL = S // SLW
        w1 = kb.tile("w1", [33, 64], F32)
        kb.dma("sp", w1, hy_w1[l])
        w2 = kb.tile("w2", [64, 2, 64], F32, multi=True)
        for i in range(2):
            kb.dma("sp", w2[:, i, :], hy_w2[l, i])
        w3 = kb.tile("w3", [64, 1024], F32)
        kb.dma("sp", w3, hy_w3[l])
        fr = sp_col("hy_freq", l)[0:64]
        b1 = sp_col("hy_b1", l)[0:64]
        b2 = sp_col("hy_b2", l)[0:64]
        fb = kb.tile("fb", [64, 3], F32)
        kb.tt("dve", fb[:, 0:1], fr, b1, ALU.mult)
        kb.tt("dve", fb[:, 1:3], b2, fr.bcast([64, 2]), ALU.mult)
        negdec = kb.tile("negdec", [128, 8], F32)
        kb.act(negdec, sp_col("hy_decay", l), AF.Abs)
        kb.ts("dve", negdec, negdec, -1.0, ALU.mult)
        hd3 = kb.tile("hd3", [64, 2, S], F32, multi=True)
        zt_r = kb.ring("zt", 2, [33, SLH], F32)
        ha = kb.tile("ha", [64, SLH], F32)
        m1 = kb.tile("m1", [64, SLH], F32)
        hb_r = kb.ring("hb", 2, [64, SLH], F32)
        ps_h = kb.psum("ps_h", [64, SLH])
        n = 0
        for d in range(2):
            for sl in range(S // SLH):
                cs = slice(sl * SLH, (sl + 1) * SLH)
                zt = zt_r[n % 2]
                n += 1
                kb.dma("sp", zt, st.zpos[d, :, cs])
                cur, Kc = zt, 33
                for layer in range(3):
                    lhsT = w1 if layer == 0 else w2[:, layer - 1, :]
                    for q in range(max(1, SLH // 512)):
                        qs = slice(q * 512, min((q + 1) * 512, SLH))
                        kb.mm(ps_h[:, qs], lhsT, cur[0:Kc, qs])
                    kb.act(ha, ps_h, AF.Identity, scale=fr, bias=fb[:, layer:layer + 1])
                    kb.ts("dve", m1, ha, PI, ALU.is_gt, -2.0 * PI, ALU.mult)
                    kb.tt("dve", ha, ha, m1, ALU.add)
                    kb.ts("dve", m1, ha, -PI, ALU.is_lt, 2.0 * PI, ALU.mult)
                    kb.tt("dve", ha, ha, m1, ALU.add)
                    dst = hd3[:, d, cs] if layer == 2 else hb_r[layer % 2]
                    kb.act(dst, ha, AF.Sin)
                    cur, Kc = dst, 64
        tr_r = kb.ring("tr", 2, [128, 2, SLW], F32)
        win_r = kb.ring("win", 2, [128, SLW], F32)
        f_r = kb.ring("f", 2, [128, SLW], F32)
        junk = kb.tile("junk", [128, SLW], F32)
        kst_r = kb.ring("kst", 2, [128, SLW], BF16)
        part = kb.tile("part", [128, 8, NSL], F32)
        nrm8 = kb.tile("nrm8", [128, 8], F32)
        rinv = kb.tile("rinv", [128, 4], F32)
        ps_f = [kb.psum(f"ps_f{i}", [128, 512]) for i in range(2)]
        kb.memset("dve", part, 0.0)
        n = 0
        for pas in (1, 2):
            for sl in range(NSL):
                cs = slice(sl * SLW, (sl + 1) * SLW)
                tr = tr_r[(pas * NSL + sl) % 2]
                kb.dma("sp", tr, st.trow[:, cs].pbcast(128))
                for oc in range(8):
                    o, d, wh = oc // 4, (oc % 4) // 2, oc % 2
                    ps = ps_f[n % 2]
                    win = win_r[n % 2]
                    f = f_r[n % 2]
                    kst = kst_r[n % 2]
                    n += 1
                    kb.mm(ps[:, :SLW], w3[:, oc * 128:(oc + 1) * 128], hd3[:, d, cs])
                    kb.act(win, tr[:, d, :], AF.Exp, scale=negdec[:, oc:oc + 1])
                    kb.tt("dve", f, ps[:, :SLW], win, ALU.mult)
                    if d == 1 and sl == 0:
                        kb.memset("dve", f[:, 0:1], 0.0)
                    if pas == 1:
                        kb.act(junk, f, AF.Abs, accum=part[:, oc, sl:sl + 1])
                    else:
                        kb.ts("dve", kst, f, rinv[:, 2 * o + wh:2 * o + wh + 1], ALU.mult)
                        r0 = (2 * o + wh) * 128
                        kb.dma("sp", st.kfT[r0:r0 + 128, d * S + sl * SLW:d * S + (sl + 1) * SLW], kst)
            if pas == 1:
                kb.op("dve", lambda: nc.vector.reduce_sum(out=nrm8.ap, in_=part.ap, axis=mybir.AxisListType.X),
                      [nrm8], [part])
                n4 = nrm8.rearrange("p (o d w) -> p o d w", o=2, d=2)
                kb.tt("dve", rinv.rearrange("p (o w) -> p o w", o=2), n4[:, :, 0, :], n4[:, :, 1, :], ALU.add)
                kb.recip(rinv, rinv)
        kb.pop()

        kb.push()
        tabf = kb.tile("tabf", [128, 4 * N1 + 512], F32)
        tabb = kb.tile("tabb", [128, 2 * N1 + 896 + N1], BF16)
        kb.dma("sp", tabf, st.dftf)
        kb.dma("sp", tabb, st.dftb)
        skipb = kb.tile("skipb", [128, 512], F32)
        kb.dma("sp", skipb[0:NH, :], hy_skip[l, :].pbcast(NH))
        TWc2 = tabf[:, 0:2 * N1].rearrange("p (h k) -> p h k", h=2).unsqueeze(1).bcast([128, 4, 2, N1])
        TWs = tabf[:, 2 * N1:3 * N1].unsqueeze(1).bcast([128, 4, N1])
        nTWs = tabf[:, 3 * N1:4 * N1].unsqueeze(1).bcast([128, 4, N1])
        o_ = 4 * N1
        iTWc2 = tabf[0:N1, o_:o_ + 256].rearrange("p (h k) -> p h k", h=2).unsqueeze(1).bcast([N1, 4, 2, 128])
        iTWs = tabf[0:N1, o_ + 256:o_ + 384].unsqueeze(1).bcast([N1, 4, 128])
        inTWs = tabf[0:N1, o_ + 384:o_ + 512].unsqueeze(1).bcast([N1, 4, 128])
        F1 = tabb[:, 0:2 * N1]
        o_ = 2 * N1
        Cm, Sm, nSm = tabb[:, o_:o_ + 128], tabb[:, o_ + 128:o_ + 256], tabb[:, o_ + 256:o_ + 384]
        CS, nSC = tabb[:, o_ + 384:o_ + 640], tabb[:, o_ + 640:o_ + 896]
        o_ += 896
        C1n, nS1n = tabb[0:N1, o_:o_ + NH], tabb[0:N1, o_ + NH:o_ + 2 * NH]
        Cg = 16
        NG = 256 // Cg
        vin_r = kb.ring("vin", 2, [128, Cg, 128], BF16)
        x1_r = kb.ring("x1in", 2, [128, Cg, 128], BF16)
        x2_r = kb.ring("x2in", 2, [128, Cg, 128], BF16)
        kf_r = [kb.ring(f"kf{o}", 2, [128, Cg, 128], BF16) for o in range(2)]
        hyst_r = kb.ring("hyst", 2, [128, Cg, 128], BF16)
        X1 = kb.tile("X1", [128, 4, 2, 128], F32)
        X2 = kb.tile("X2", [128, 4, 2, 128], F32)
        Ap_r = kb.ring("Ap", 2, [128, 4, 2, 128], BF16)
        Kf = kb.tile("Kf", [128, 4, 2, 128], F32)
        T_ = [kb.tile(f"T{i}", [128, 4, 128], F32) for i in range(4)]
        Y_r = kb.ring("Y", 2, [128, 4, 2, 128], BF16)
        XB1 = kb.tile("XB1", [128, 4, 2, 128], F32)
        XB2 = kb.tile("XB2", [128, 4, 2, 128], F32)
        Bp_r = kb.ring("Bp", 2, [128, 4, 2, 128], BF16)
        tp = kb.tile("tp", [128, 4, 128], F32)
        u2_r = kb.ring("u2", 2, [128, 4, 128], BF16)
        psA = kb.psum("psA", [128, 4, 256])
        psUr = kb.psum("psUr", [128, 512])
        psUi = kb.psum("psUi", [128, 512])
        psB = kb.psum("psB", [128, 4, 256])
        psY = kb.psum("psY", [128, 512])
        ur = psUr[:, 0:4 * N1].rearrange("p (c k) -> p c k", c=4)
        ui = psUi[:, 0:4 * N1].rearrange("p (c k) -> p c k", c=4)
        yv = psY[0:NH, 0:512].rearrange("p (c k) -> p c k", c=4)
        cnt = {"n": 0}

        def fwd_fft(src, K):
            Ap = Ap_r[cnt["n"] % 2]
            cnt["n"] += 1
            for c in range(4):
                kb.mm(psA[:, c, 0:2 * N1], src[0:K, c, :], F1[0:K, :])
            pA4 = psA[:, :, 0:2 * N1].rearrange("p c (h k) -> p c h k", h=2)
            kb.tt("dve", X1[:, :, :, 0:N1], pA4, TWc2, ALU.mult)
            kb.tt("dve", X2[:, :, 0, 0:N1], pA4[:, :, 1, :], TWs, ALU.mult)
            kb.tt("dve", X2[:, :, 1, 0:N1], pA4[:, :, 0, :], nTWs, ALU.mult)
            kb.tt("pool", Ap[:, :, :, 0:N1], X1[:, :, :, 0:N1], X2[:, :, :, 0:N1], ALU.add)
            rr, ri = Ap[:, :, 0, 0:N1], Ap[:, :, 1, 0:N1]
            kb.mm(ur, Cm, rr, start=True, stop=False)
            kb.mm(ur, Sm, ri, start=False, stop=True)
            kb.mm(ui, Cm, ri, start=True, stop=False)
            kb.mm(ui, nSm, rr, start=False, stop=True)

        def conv_block(o, kfblk, ublk, gateblk, dst, sk):
            fwd_fft(kfblk, N1)
            kb.copy("act", Kf[:, :, 0, 0:N1], ur)
            kb.copy("act", Kf[:, :, 1, 0:N1], ui)
            fwd_fft(ublk, NH)
            Y = Y_r[o % 2]
            Kr, Ki = Kf[:, :, 0, 0:N1], Kf[:, :, 1, 0:N1]
            t = [x[:, :, 0:N1] for x in T_]
            kb.tt("dve", t[0], ur, Kr, ALU.mult)
            kb.tt("dve", t[1], ui, Ki, ALU.mult)
            kb.tt("pool", Y[:, :, 0, 0:N1], t[0], t[1], ALU.subtract)
            kb.tt("dve", t[2], ur, Ki, ALU.mult)
            kb.tt("dve", t[3], ui, Kr, ALU.mult)
            kb.tt("pool", Y[:, :, 1, 0:N1], t[2], t[3], ALU.add)
            Bp = Bp_r[o % 2]
            for c in range(4):
                kb.mm(psB[0:N1, c, :], Y[:, c, 0, 0:N1], CS, start=True, stop=False)
                kb.mm(psB[0:N1, c, :], Y[:, c, 1, 0:N1], nSC, start=False, stop=True)
            pB4 = psB[0:N1].rearrange("p c (h k) -> p c h k", h=2)
            kb.tt("dve", XB1[0:N1], pB4, iTWc2, ALU.mult)
            kb.tt("dve", XB2[0:N1, :, 0, :], pB4[:, :, 1, :], inTWs, ALU.mult)
            kb.tt("dve", XB2[0:N1, :, 1, :], pB4[:, :, 0, :], iTWs, ALU.mult)
            kb.tt("pool", Bp[0:N1], XB1[0:N1], XB2[0:N1], ALU.add)
            kb.mm(yv, C1n, Bp[0:N1, :, 0, :], start=True, stop=False)
            kb.mm(yv, nS1n, Bp[0:N1, :, 1, :], start=False, stop=True)
            kb.tt("pool", tp[0:NH], ublk[0:NH], sk.unsqueeze(2).bcast([NH, 4, 128]), ALU.mult)
            kb.tt("dve", tp[0:NH], tp[0:NH], yv, ALU.add)
            kb.tt("pool", dst, tp[0:NH], gateblk[0:NH], ALU.mult)

        def fftv(src_rows, c0, nrow):
            return src_rows[c0:c0 + Cg, :].rearrange("c (a b) -> a c b", b=128)

        for g in range(NG):
            c0 = g * Cg
            vin, x1in, x2in, hyst = vin_r[g % 2], x1_r[g % 2], x2_r[g % 2], hyst_r[g % 2]
            kfs = [kf_r[0][g % 2], kf_r[1][g % 2]]
            kb.dma("sp", vin[0:NH], fftv(st.zT[0:256], c0, NH))
            kb.dma("sp", x1in[0:NH], fftv(st.zT[256:512], c0, NH))
            kb.dma("sp", x2in[0:NH], fftv(st.zT[512:768], c0, NH))
            for o in range(2):
                kb.dma("sp", kfs[o][0:N1], fftv(st.kfT[o * 256:(o + 1) * 256], c0, N1))
            for b in range(Cg // 4):
                bs = slice(4 * b, 4 * b + 4)
                u2 = u2_r[b % 2]
                cc = c0 + 4 * b
                conv_block(0, kfs[0][:, bs, :], vin[:, bs, :], x1in[:, bs, :], u2[0:NH],
                           skipb[0:NH, cc:cc + 4])
                conv_block(1, kfs[1][:, bs, :], u2, x2in[:, bs, :], hyst[0:NH, bs, :],
                           skipb[0:NH, 256 + cc:256 + cc + 4])
            kb.dma("sp", fftv(st.hyT, c0, NH), hyst[0:NH])
        kb.pop()

    def phase_B(l, streams):
        kb.push()
        kT = kb.tile("kT", [128, 2, NKEY], BF16, multi=True)
        va = kb.tile("va", [128, NKEY // 128, 2, 66], BF16, multi=True)
        nld = 6
        per = (NKEY // 128) // nld
        for i in range(nld):
            kb.dma("sp", kT[:, :, i * per * 128:(i + 1) * per * 128], kT2d[:, :, i * per * 128:(i + 1) * per * 128])
            kb.dma("sp", va[:, i * per:(i + 1) * per], vaugd[i * per:(i + 1) * per].rearrange("c p k d -> p c k d"))
        qt_r = kb.ring("qt", 2, [128, 4, 512], BF16)
        pt_r = kb.ring("pt", 3, [128, 2, 512], BF16)
        ost_r = kb.ring("ost", 2, [64, 8, 512], BF16)
        osb_r = kb.ring("osb", 2, [64, 512], F32)
        rden_r = kb.ring("rden", 2, [128, 512], F32)
        ps_s = [kb.psum(f"ps_s{i}", [128, 2, 512]) for i in range(2)]
        ps_o = [kb.psum(f"ps_o{i}", [128, 512]) for i in range(2)]
        ps_bc = kb.psum("ps_bc", [64, 512])

        items = []
        for st in streams:
            nk = CTX if st.name == "ctx" else NKEY
            pairs = nk // 256
            for i in range(st.NT):
                for hd in range(8):
                    for cp in range(pairs):
                        items.append((st, i, hd, cp, pairs))
        qcur = {}
        tord = {}
        for (st_, i_, _h, _c, _p) in items:
            tord.setdefault((st_.name, i_), len(tord))

        def get_q(st, i):
            key = (st.name, i)
            if key not in qcur:
                qt = qt_r[tord[key] % 2]
                kb.dma("sp", qt[:, :, :st.T], st.qT[:, :, i * st.T:(i + 1) * st.T])
                qcur[key] = qt
            return qcur[key]

        def emit_qk(n):
            st, i, hd, cp, pairs = items[n]
            qt = get_q(st, i)
            j, hf, kv = hd // 2, hd % 2, hd // 4
            P = slice(hf * 64, hf * 64 + 64)
            pss = ps_s[n % 2]
            for u in range(2):
                ch = 2 * cp + u
                kb.mm(pss[:, u, :st.T], kT[P, kv, ch * 128:(ch + 1) * 128], qt[P, j, :st.T])

        pending = []

        def finalize2(st, i, hd, po, rden, osb, ost):
            NQ = st.T
            kb.mm(ps_bc[:, :NQ], onesf[64:65, 0:64], rden[64:65, :NQ])
            kb.tt("dve", ost[0:64, hd, :NQ], osb[:, :NQ], ps_bc[:, :NQ], ALU.mult)
            if hd == 7:
                kb.dma("sp", st.attnT[:, :, i * NQ:(i + 1) * NQ], ost[:, :, :NQ])

        emit_qk(0)
        hcount = 0
        for n in range(len(items)):
            st, i, hd, cp, pairs = items[n]
            NQ = st.T
            kv = hd // 4
            if n + 1 < len(items):
                emit_qk(n + 1)
            pt = pt_r[n % 3]
            kb.act(pt[:, :, :NQ], ps_s[n % 2][:, :, :NQ], AF.Exp, scale=0.125)
            if cp == 0:
                po = ps_o[hcount % 2]
            for u in range(2):
                ch = 2 * cp + u
                kb.mm(po[0:65, :NQ], va[:, ch, kv, 0:65], pt[:, u, :NQ], start=(cp == 0 and u == 0),
                      stop=(cp == pairs - 1 and u == 1))
            while pending:
                finalize2(*pending.pop(0))
            if cp == pairs - 1:
                rden = rden_r[hcount % 2]
                osb = osb_r[hcount % 2]
                ost = ost_r[tord[(st.name, i)] % 2]
                kb.recip(rden[64:65, :NQ], po[64:65, :NQ])
                kb.copy("act", osb[:, :NQ], po[0:64, :NQ])
                pending.append((st, i, hd, po, rden, osb, ost))
                hcount += 1
        while pending:
            finalize2(*pending.pop(0))
        kb.pop()

    def prep_C(st, i, TT, which, xt, sq, srt, rstd, h, ps_ss):
        t0 = i * TT
        xsrc = st.x.rearrange("(c p) t -> p c t", p=128)
        kb.dma("sp", xt[:, :, :TT], xsrc[:, :, t0:t0 + TT])
        ai = norm_mod(st, xt, sq, h, ps_ss, srt, rstd, which, TT)
        norm_apply(st, xt, xt, h, rstd, ai, TT)

    def phase_C1(l, streams, TT):
        kb.push()
        wG = kb.tile("wG", [128, 8, 4096], BF16, multi=True)
        wBA = kb.tile("wBA", [64, 8, 1024], BF16, multi=True)
        wBH = kb.tile("wBH", [128, 2, 1024], BF16, multi=True)
        wBP = kb.tile("wBP", [128, 2, 1024], BF16, multi=True)
        wBG = kb.tile("wBG", [128, 2, 1024], BF16, multi=True)
        wO = kb.tile("wO", [128, 8, 1024], BF16, multi=True)
        wsrc = w_in[l].rearrange("(kc p) n -> p kc n", p=128)
        for kc in range(8):
            kb.dma("pool", wG[:, kc, :], wsrc[:, kc, 2304:6400])
        for hd in range(8):
            kb.dma("pool", wBA[:, hd, :], w_br_attn[l, hd * 64:(hd + 1) * 64, :])
        for (wt, src) in ((wBH, w_br_hy), (wBP, w_br_pool), (wBG, w_br_gm)):
            for c in range(2):
                kb.dma("pool", wt[:, c, :], src[l, c * 128:(c + 1) * 128, :])
        for kc in range(8):
            kb.dma("pool", wO[:, kc, :], w_out[l, kc * 128:(kc + 1) * 128, :])
        xt = kb.tile("xt", [128, 8, TT], F32)
        sq = kb.tile("sq", [128, 8, TT], BF16)
        srt = kb.tile("srt", [128, TT], F32)
        rstd = kb.tile("rstd", [128, TT], F32)
        h_r = kb.ring("h", 2, [128, 8, TT], BF16)
        at = kb.tile("at", [64, 8, TT], BF16)
        hyt_r = kb.ring("hyt", 2, [128, 2, TT], BF16)
        plt_r = kb.ring("plt", 2, [128, 2, TT], BF16)
        gmt_r = kb.ring("gmt", 2, [128, 2, TT], BF16)
        g_r = kb.ring("g", 2, [128, TT], BF16)
        tmp_r = kb.ring("tmp", 2, [128, TT], F32)
        acc = kb.tile("acc", [128, TT], F32)
        mg = kb.tile("mg", [128, 8, TT], BF16)
        xr_r = kb.ring("xr", 2, [128, TT], F32)
        xo_r = kb.ring("xo", 2, [128, TT], F32)
        ps_g = [kb.psum(f"ps_g{i}", [128, 512]) for i in range(2)]
        ps_b = [kb.psum(f"ps_b{i}", [128, 512]) for i in range(2)]
        ps_o = [kb.psum(f"ps_o{i}", [128, 512]) for i in range(2)]
        ps_ss = kb.psum("ps_ss", [128, 512])
        tiles = [(st, i) for st in streams for i in range(st.S // min(TT, st.S))]
        cnt = {"h": 0, "n": 0, "o": 0}
        hmap = {}

        def prep(k):
            st, i = tiles[k]
            T_ = min(TT, st.S)
            h = h_r[cnt["h"] % 2]
            cnt["h"] += 1
            prep_C(st, i, T_, 1, xt, sq, srt, rstd, h, ps_ss)
            hmap[k] = h

        prep(0)
        for k, (st, i) in enumerate(tiles):
            T_ = min(TT, st.S)
            t0 = i * T_
            h = hmap.pop(k)
            hyt, plt, gmt = hyt_r[k % 2], plt_r[k % 2], gmt_r[k % 2]
            for (dst, src) in ((hyt, st.hyT), (plt, st.plT), (gmt, st.gmT)):
                kb.dma("sp", dst[:, :, :T_], src.rearrange("(c p) t -> p c t", p=128)[:, :, t0:t0 + T_])
            kb.dma("sp", at[:, :, :T_], st.attnT[:, :, t0:t0 + T_])
            branches = ((1, wBH, hyt, 2, 128), (2, wBP, plt, 2, 128), (3, wBG, gmt, 2, 128), (0, wBA, at, 8, 64))
            for m in range(8):
                if m == 4 and k + 1 < len(tiles):
                    prep(k + 1)
                ms = slice(m * 128, (m + 1) * 128)
                for bi, (gi, wB, src, nk, kp) in enumerate(branches):
                    pg = ps_g[cnt["n"] % 2]
                    pb = ps_b[cnt["n"] % 2]
                    g = g_r[cnt["n"] % 2]
                    tmp = tmp_r[cnt["n"] % 2]
                    cnt["n"] += 1
                    for kc in range(8):
                        kb.mm(pg[:, :T_], wG[:, kc, gi * 1024 + m * 128:gi * 1024 + (m + 1) * 128], h[:, kc, :T_],
                              start=(kc == 0), stop=(kc == 7))
                    for c in range(nk):
                        kb.mm(pb[:, :T_], wB[0:kp, c, ms], src[0:kp, c, :T_], start=(c == 0), stop=(c == nk - 1))
                    kb.act(g[:, :T_], pg[:, :T_], AF.Sigmoid)
                    if bi == 0:
                        kb.tt("dve", acc[:, :T_], g[:, :T_], pb[:, :T_], ALU.mult)
                    else:
                        kb.tt("dve", tmp[:, :T_], g[:, :T_], pb[:, :T_], ALU.mult)
                        dst = acc[:, :T_] if bi < 3 else mg[:, m, :T_]
                        kb.tt("pool", dst, acc[:, :T_], tmp[:, :T_], ALU.add)
            xsrc = st.x.rearrange("(c p) t -> p c t", p=128)
            xdst = st.xmid.rearrange("(c p) t -> p c t", p=128)
            for m in range(8):
                po = ps_o[cnt["o"] % 2]
                xr = xr_r[cnt["o"] % 2]
                xo = xo_r[cnt["o"] % 2]
                cnt["o"] += 1
                kb.dma("sp", xr[:, :T_], xsrc[:, m, t0:t0 + T_])
                for kc in range(8):
                    kb.mm(po[:, :T_], wO[:, kc, m * 128:(m + 1) * 128], mg[:, kc, :T_], start=(kc == 0), stop=(kc == 7))
                kb.stt(xo[:, :T_], po[:, :T_], modp[:, st.si, 2, m:m + 1], xr[:, :T_], ALU.mult, ALU.add)
                kb.dma("sp", xdst[:, m, t0:t0 + T_], xo[:, :T_])
        kb.pop()

    def phase_C2(l, streams, TT, final):
        kb.push()
        wg = kb.tile("wg", [128, 8, HID], BF16, multi=True)
        wu = kb.tile("wu", [128, 8, HID], BF16, multi=True)
        wd = kb.tile("wd", [128, 22, D], BF16, multi=True)
        for kc in range(8):
            kb.dma("pool", wg[:, kc, :], ffn_g[l, kc * 128:(kc + 1) * 128, :])
            kb.dma("pool", wu[:, kc, :], ffn_u[l, kc * 128:(kc + 1) * 128, :])
        for j in range(22):
            kb.dma("pool", wd[:, j, :], ffn_d[l, j * 128:(j + 1) * 128, :])
        xt = kb.tile("xt", [128, 8, TT], F32)
        sq = kb.tile("sq", [128, 8, TT], BF16)
        srt = kb.tile("srt", [128, TT], F32)
        rstd = kb.tile("rstd", [128, TT], F32)
        h_r = kb.ring("h", 2, [128, 8, TT], BF16)
        a_t = kb.tile("a", [128, 22, TT], BF16)
        sl_r = kb.ring("sl", 2, [128, TT], F32)
        xr_r = kb.ring("xr", 2, [128, TT], F32)
        xo_r = kb.ring("xo", 2, [128, TT], F32)
        if final:
            xf = kb.tile("xf", [128, 8, TT], F32)
            fsq = kb.tile("fsq", [128, 8, TT], BF16)
            fnw = sp_col("final_norm_w")
        ps_g = [kb.psum(f"ps_g{i}", [128, 512]) for i in range(2)]
        ps_u = [kb.psum(f"ps_u{i}", [128, 512]) for i in range(2)]
        ps_o = [kb.psum(f"ps_o{i}", [128, 512]) for i in range(2)]
        ps_ss = kb.psum("ps_ss", [128, 512])
        tiles = [(st, i) for st in streams for i in range(st.S // min(TT, st.S))]
        cnt = {"h": 0, "n": 0, "o": 0}
        hmap = {}

        def prep(k):
            st, i = tiles[k]
            T_ = min(TT, st.S)
            h = h_r[cnt["h"] % 2]
            cnt["h"] += 1
            prep_C(st, i, T_, 2, xt, sq, srt, rstd, h, ps_ss)
            hmap[k] = h

        prep(0)
        for k, (st, i) in enumerate(tiles):
            T_ = min(TT, st.S)
            t0 = i * T_
            h = hmap.pop(k)
            for j in range(22):
                if j == 12 and k + 1 < len(tiles):
                    prep(k + 1)
                pg = ps_g[cnt["n"] % 2]
                pu = ps_u[cnt["n"] % 2]
                sl = sl_r[cnt["n"] % 2]
                cnt["n"] += 1
                for kc in range(8):
                    kb.mm(pg[:, :T_], wg[:, kc, j * 128:(j + 1) * 128], h[:, kc, :T_], start=(kc == 0), stop=(kc == 7))
                for kc in range(8):
                    kb.mm(pu[:, :T_], wu[:, kc, j * 128:(j + 1) * 128], h[:, kc, :T_], start=(kc == 0), stop=(kc == 7))
                kb.act(sl[:, :T_], pg[:, :T_], AF.Silu)
                kb.tt("dve", a_t[:, j, :T_], sl[:, :T_], pu[:, :T_], ALU.mult)
            xsrc = st.x.rearrange("(c p) t -> p c t", p=128)
            xdst = st.xnext.rearrange("(c p) t -> p c t", p=128)
            dofinal = final and st.name == "lat"
            for m in range(8):
                po = ps_o[cnt["o"] % 2]
                xr = xr_r[cnt["o"] % 2]
                xo = xo_r[cnt["o"] % 2]
                cnt["o"] += 1
                kb.dma("sp", xr[:, :T_], xsrc[:, m, t0:t0 + T_])
                for j in range(22):
                    kb.mm(po[:, :T_], wd[:, j, m * 128:(m + 1) * 128], a_t[:, j, :T_], start=(j == 0), stop=(j == 21))
                if dofinal:
                    kb.stt(xf[:, m, :T_], po[:, :T_], modp[:, st.si, 5, m:m + 1], xr[:, :T_], ALU.mult, ALU.add)
                else:
                    kb.stt(xo[:, :T_], po[:, :T_], modp[:, st.si, 5, m:m + 1], xr[:, :T_], ALU.mult, ALU.add)
                    kb.dma("sp", xdst[:, m, t0:t0 + T_], xo[:, :T_])
            if dofinal:
                kb.act(fsq[:, :, :T_], xf[:, :, :T_], AF.Square)
                for c in range(8):
                    kb.mm(ps_ss[:, :T_], ones1024, fsq[:, c, :T_], start=(c == 0), stop=(c == 7))
                kb.act(srt[:, :T_], ps_ss[:, :T_], AF.Sqrt, bias=epst)
                kb.recip(rstd[:, :T_], srt[:, :T_])
                kb.tt("dve", xf[:, :, :T_], xf[:, :, :T_], rstd[:, :T_].unsqueeze(1).bcast([128, 8, T_]), ALU.mult)
                odst = outT.rearrange("(c p) t -> p c t", p=128)
                for c in range(8):
                    xo = xo_r[cnt["o"] % 2]
                    cnt["o"] += 1
                    kb.act(xo[:, :T_], xf[:, c, :T_], AF.Identity, scale=fnw[:, c:c + 1])
                    kb.dma("sp", odst[:, c, t0:t0 + T_], xo[:, :T_])
        kb.pop()

    TC1 = 512
    TC2 = 256
    for l in range(nlayers):
        last = (l == DEPTH - 1)
        streams = [lat] if last else [cxs, lat]
        lat.xmid, cxs.xmid = xa, ca
        lat.xnext, cxs.xnext = xb, cb
        phase_M(l)
        if stop_phase == ("M", l):
            break
        phase_A(l)
        if stop_phase == ("A", l):
            break
        if "H" not in skip:
            for st in streams:
                phase_H(l, st)
        if stop_phase == ("H", l):
            break
        phase_B(l, streams)
        if stop_phase == ("B", l):
            break
        phase_C1(l, streams, TC1)
        for st in streams:
            st.x = st.xmid
        if stop_phase == ("C1", l):
            break
        phase_C2(l, streams, TC2, final=last)
        for st in streams:
            st.x = st.xnext
        if stop_phase == ("C2", l):
            break

    kb.barrier()
    return nc


SP_SPEC = [("norm1_w", 16), ("norm2_w", 16), ("b_mod", 96), ("final_norm_w", 8), ("q_norm_w", 2), ("k_norm_w", 2),
           ("hy_conv_w", 36), ("hy_conv_b", 12), ("pool_scale", 4), ("gm_bs", 8), ("hy_b1", 2), ("hy_freq", 2),
           ("hy_b2", 4), ("hy_decay", 16)]
SP_OFF = {}
_o = 0
for _n, _c in SP_SPEC:
    SP_OFF[_n] = (_o, _c)
    _o += _c
NSP = _o


def _pack_small(inp):
    sp = np.zeros((128, NSP), np.float32)

    def put(name, arr):
        o, n = SP_OFF[name]
        assert arr.shape == (128, n), (name, arr.shape, n)
        sp[:, o:o + n] = arr

    def fm(a, nch):
        L = a.shape[0]
        return a.reshape(L, nch, 128).transpose(2, 0, 1).reshape(128, L * nch)

    put("norm1_w", fm(inp["norm1_w"], 8))
    put("norm2_w", fm(inp["norm2_w"], 8))
    put("b_mod", fm(inp["b_mod"], 48))
    put("final_norm_w", inp["final_norm_w"].reshape(8, 128).T)
    put("q_norm_w", np.tile(inp["q_norm_w"].T, (2, 1)))
    put("k_norm_w", np.tile(inp["k_norm_w"].T, (2, 1)))
    put("hy_conv_w", inp["hy_conv_w"].reshape(DEPTH, 3, 6, 128).transpose(3, 0, 1, 2).reshape(128, DEPTH * 18))
    put("hy_conv_b", fm(inp["hy_conv_b"], 6))
    put("pool_scale", fm(inp["pool_scale"], 2))
    put("gm_bs", inp["gm_bs"].transpose(2, 0, 1).reshape(128, DEPTH * 4))
    h64 = lambda a: np.concatenate([a, np.zeros_like(a)], axis=0)
    put("hy_b1", h64(inp["hy_b1"].T))
    put("hy_freq", h64(inp["hy_freq"].T))
    put("hy_b2", h64(inp["hy_b2"].transpose(2, 0, 1).reshape(64, DEPTH * 2)))
    put("hy_decay", fm(inp["hy_decay"], 8))
    return sp


_CONST_CACHE = {}


def _consts():
    if _CONST_CACHE:
        return _CONST_CACHE
    cbf = np.zeros((128, 3, 128), np.float32)
    cbf[:, 0, :] = np.eye(128)
    k = np.arange(128)
    cbf[k, 1, k ^ 1] = 1.0
    cosF, sinS = _rope_tables()
    _CONST_CACHE.update(dict(
        cbf=_bf(cbf), ropeT=np.stack([cosF, sinS]).astype(np.float32),
        invcnt_l=_pool_invcnt(SEQ), invcnt_c=_pool_invcnt(CTX)))
    for nm, n in (("lat", SEQ), ("ctx", CTX)):
        zz, tr = _hy_tables(n)
        tf, tb = _dft_tables(2 * n // 128)
        _CONST_CACHE["zpos_" + nm] = zz
        _CONST_CACHE["trow_" + nm] = tr
        _CONST_CACHE["dftf_" + nm] = tf
        _CONST_CACHE["dftb_" + nm] = tb
    return _CONST_CACHE


def make_in_maps(inp):
    inp = {k: np.asarray(v) for k, v in inp.items()}
    shared = dict(_consts())
    for k in ("w_mod", "w_in", "w_br_attn", "w_br_hyena", "w_br_pool", "w_br_gmlp", "w_out", "ffn_w_gate",
              "ffn_w_up", "ffn_w_down", "gm_norm_w", "pool_w", "hy_w1", "hy_w2", "hy_w3", "hy_skip"):
        shared[k] = np.ascontiguousarray(inp[k], dtype=np.float32)
    shared["smallp"] = _pack_small(inp)
    shared["gm_wsT"] = np.ascontiguousarray(inp["gm_ws"].transpose(0, 1, 3, 2))
    shared["hy_skip"] = np.ascontiguousarray(inp["hy_skip"].reshape(DEPTH, 512), dtype=np.float32)
    maps = []
    for b in range(NCORE):
        m = dict(shared)
        m["xT"] = np.ascontiguousarray(inp["x"][b].T)
        m["ctxT"] = np.ascontiguousarray(inp["ctx"][b].T)
        cv = np.stack([inp["c"][b].reshape(8, 128).T, inp["c_ctx"].reshape(8, 128).T], axis=-1)
        m["cvec"] = np.ascontiguousarray(cv, dtype=np.float32)
        maps.append(m)
    return maps


def kernel(**inputs):
    nc = build_program()
    maps = make_in_maps(inputs)
    res = run_bass_kernel_spmd(nc, maps, core_ids=list(range(NCORE)))
    out = np.stack([np.ascontiguousarray(r["outT"].T) for r in res.results], axis=0)
    return out.astype(np.float32)
```

```python
import numpy as np
import ml_dtypes
from contextlib import ExitStack
import concourse.bass as bass
import concourse.mybir as mybir
from concourse.bass_utils import run_bass_kernel_spmd

F32 = mybir.dt.float32
BF16 = mybir.dt.bfloat16
AF = mybir.ActivationFunctionType
ALU = mybir.AluOpType

D = 1024
SEQ = 8192
CTX = 256
DEPTH = 2
NCORE = 8
HID = 2816
EPS = 1e-6
NKEY = CTX + SEQ
PADC = 8


class Res:
    __slots__ = ("name", "readers", "writers", "multi", "sem")

    def __init__(self, name, multi=False):
        self.name = name
        self.readers = {}
        self.writers = {}
        self.multi = multi
        self.sem = None


class V:
    __slots__ = ("ap", "res", "sb")

    def __init__(self, ap, res, sb=True):
        self.ap = ap
        self.res = res
        self.sb = sb

    def __getitem__(self, k):
        return V(self.ap[k], self.res, self.sb)

    def rearrange(self, s, **kw):
        return V(self.ap.rearrange(s, **kw), self.res, self.sb)

    def bcast(self, shape):
        return V(self.ap.broadcast_to(list(shape)), self.res, self.sb)

    def unsqueeze(self, a):
        return V(self.ap.unsqueeze(a), self.res, self.sb)

    def pbcast(self, n):
        return V(self.ap.partition_broadcast(n), self.res, self.sb)

    def wr(self, res):
        return V(self.ap, res, self.sb)


class KB:
    def __init__(self, nc):
        self.nc = nc
        self.eng = {"pe": nc.tensor, "act": nc.scalar, "dve": nc.vector, "pool": nc.gpsimd, "sp": nc.sync}
        self.psem = {}
        self.pcnt = {}
        self.root = ExitStack()
        for e in ("pe", "act", "dve", "pool"):
            self.psem[e] = self.root.enter_context(nc.semaphore(f"prog_{e}"))
            self.pcnt[e] = 0
        self.waited = {e: {} for e in self.eng}
        self.dma_all = []
        self.dma_free = []
        self.phase_res = []
        self.stacks = [self.root]
        self.uid = 0
        self.dram_res = {}

    def _name(self, n):
        self.uid += 1
        return f"{n}_{self.uid}"

    def tile(self, name, shape, dt, multi=False):
        t = self.stacks[-1].enter_context(self.nc.sbuf_tensor(self._name(name), list(shape), dt))
        r = Res(name, multi)
        self.phase_res.append(r)
        return V(t[tuple(slice(None) for _ in shape)], r, True)

    def ring(self, name, n, shape, dt, multi=False):
        return [self.tile(f"{name}{i}", shape, dt, multi) for i in range(n)]

    def psum(self, name, shape, dt=F32):
        t = self.stacks[-1].enter_context(self.nc.psum_tensor(self._name(name), list(shape), dt))
        r = Res(name)
        self.phase_res.append(r)
        return V(t[tuple(slice(None) for _ in shape)], r, True)

    def dram(self, name, shape, dt, kind="Internal"):
        t = self.nc.dram_tensor(name, list(shape), dt, kind=kind)
        return V(t.ap(), Res(name, True), False)

    def dres(self, v, key):
        k = (v.res.name, key)
        if k not in self.dram_res:
            self.dram_res[k] = Res(f"{v.res.name}:{key}", True)
        return v.wr(self.dram_res[k])

    def push(self):
        self.stacks.append(ExitStack())
        self.phase_res = []

    def pop(self):
        self.barrier()
        for r in self.phase_res:
            if r.sem is not None:
                if "hw" in r.sem:
                    self.dma_free.append(r.sem["hw"])
                r.sem = None
        self.phase_res = []
        self.stacks.pop().close()

    def _deps(self, outs, ins):
        d = {}

        def add(tokd):
            for k, (s, c) in tokd.items():
                if k not in d or d[k][1] < c:
                    d[k] = (s, c)

        for v in ins:
            add(v.res.writers)
        for v in outs:
            add(v.res.readers)
            if not v.res.multi:
                add(v.res.writers)
        return d

    def _wait(self, e, deps, skip_own=True):
        own = self.psem[e].name if (skip_own and e == "pe") else None
        w = self.waited[e]
        for k, (s, c) in deps.items():
            if k == own:
                continue
            if w.get(k, 0) >= c:
                continue
            self.eng[e].wait_ge(s, c)
            w[k] = c

    def _commit(self, outs, ins, key, tok):
        for v in ins:
            r = v.res.readers
            if key not in r or r[key][1] < tok[1]:
                r[key] = tok
        for v in outs:
            if v.res.multi:
                v.res.writers[key] = tok
                v.res.readers = {}
            else:
                v.res.writers = {key: tok}
                v.res.readers = {}

    def op(self, e, fn, outs, ins):
        outs = [o for o in outs if isinstance(o, V)]
        ins = [i for i in ins if isinstance(i, V)]
        self._wait(e, self._deps(outs, ins))
        inst = fn()
        self.pcnt[e] += 1
        inst.then_inc(self.psem[e], 1)
        self._commit(outs, ins, self.psem[e].name, (self.psem[e], self.pcnt[e]))
        return inst

    def dma(self, q, out, in_, extra_in=(), extra_out=()):
        sbv = out if out.sb else in_
        outs = [out] + list(extra_out)
        ins = [in_] + list(extra_in)
        self._wait(q, self._deps(outs, ins), skip_own=False)
        res = sbv.res
        qt = "sw" if q == "pool" else "hw"
        if res.sem is None:
            res.sem = {}
        if qt not in res.sem:
            if qt == "hw" and self.dma_free:
                res.sem[qt] = self.dma_free.pop()
            else:
                s = self.root.enter_context(self.nc.semaphore(self._name("dma" + qt)))
                res.sem[qt] = [s, 0]
                self.dma_all.append(res.sem[qt])
        sm = res.sem[qt]
        inst = self.eng[q].dma_start(out=out.ap, in_=in_.ap)
        sm[1] += 16
        inst.then_inc(sm[0], 16)
        self._commit(outs, ins, sm[0].name, (sm[0], sm[1]))
        return inst

    def barrier(self):
        toks = {}
        for e in self.psem:
            toks[self.psem[e].name] = (self.psem[e], self.pcnt[e])
        for s in self.dma_all:
            if s[1] > 0:
                toks[s[0].name] = (s[0], s[1])
        for e in self.eng:
            self._wait(e, toks, skip_own=True)

    @staticmethod
    def _a(x):
        return x.ap if isinstance(x, V) else x

    def mm(self, out, lhsT, rhs, start=True, stop=True):
        return self.op("pe", lambda: self.nc.tensor.matmul(out.ap, lhsT=lhsT.ap, rhs=rhs.ap, start=start, stop=stop),
                       [out], [lhsT, rhs])

    def transpose(self, out, in_, ident):
        return self.op("pe", lambda: self.nc.tensor.transpose(out.ap, in_.ap, ident.ap), [out], [in_, ident])

    def act(self, out, in_, func, bias=0.0, scale=1.0, accum=None):
        a = self._a
        kw = {}
        if accum is not None:
            kw["accum_out"] = accum.ap
        return self.op("act", lambda: self.nc.scalar.activation(out=out.ap, in_=in_.ap, func=func, bias=a(bias),
                                                                scale=a(scale), **kw),
                       [out, accum], [in_, bias, scale])

    def tt(self, e, out, in0, in1, op):
        return self.op(e, lambda: self.eng[e].tensor_tensor(out=out.ap, in0=in0.ap, in1=in1.ap, op=op),
                       [out], [in0, in1])

    def ts(self, e, out, in0, s1, op0, s2=None, op1=None):
        a = self._a
        if op1 is None:
            f = lambda: self.eng[e].tensor_scalar(out=out.ap, in0=in0.ap, scalar1=a(s1), scalar2=None, op0=op0)
        else:
            f = lambda: self.eng[e].tensor_scalar(out=out.ap, in0=in0.ap, scalar1=a(s1), scalar2=a(s2), op0=op0,
                                                  op1=op1)
        return self.op(e, f, [out], [in0, s1, s2])

    def stt(self, out, in0, scalar, in1, op0, op1):
        a = self._a
        return self.op("dve", lambda: self.nc.vector.scalar_tensor_tensor(out=out.ap, in0=in0.ap, scalar=a(scalar),
                                                                          in1=in1.ap, op0=op0, op1=op1),
                       [out], [in0, scalar, in1])

    def copy(self, e, out, in_):
        if e == "act":
            return self.op("act", lambda: self.nc.scalar.copy(out=out.ap, in_=in_.ap), [out], [in_])
        return self.op(e, lambda: self.eng[e].tensor_copy(out=out.ap, in_=in_.ap), [out], [in_])

    def memset(self, e, out, val):
        return self.op(e, lambda: self.eng[e].memset(out.ap, val), [out], [])

    def recip(self, out, in_):
        return self.op("dve", lambda: self.nc.vector.reciprocal(out=out.ap, in_=in_.ap), [out], [in_])


def _bf(a):
    return np.asarray(a, dtype=np.float32).astype(ml_dtypes.bfloat16)


def _rope_tables():
    rows = SEQ // 64
    r, col = np.meshgrid(np.arange(rows, dtype=np.float32), np.arange(64, dtype=np.float32), indexing="ij")
    inv = (np.float32(10000.0) ** (-np.arange(0, 32, 2, dtype=np.float32) / np.float32(32))).astype(np.float32)
    ang = np.concatenate([r.reshape(-1, 1) * inv, col.reshape(-1, 1) * inv], axis=-1).astype(np.float32)
    cos = np.cos(ang).astype(np.float32)
    sin = np.sin(ang).astype(np.float32)
    p = np.arange(128)
    i = (p % 64) // 2
    cosF = cos[:, i].T.copy()
    sgn = np.where(p % 2 == 0, -1.0, 1.0).astype(np.float32)
    sinS = (sin[:, i].T * sgn[:, None]).astype(np.float32)
    return np.ascontiguousarray(cosF), np.ascontiguousarray(sinS)


def _hy_pos(n):
    pos = np.arange(n, dtype=np.float32)
    t = (pos / np.float32(n - 1)).astype(np.float32)
    bands = np.linspace(1e-4, 15, 16, dtype=np.float32)
    ang = (np.float32(2.0 * np.pi / n) * pos[:, None] * bands).astype(np.float32)
    z = np.concatenate([t[:, None], np.cos(ang), np.sin(ang)], axis=-1).astype(np.float32)
    return z, t


def _pool_invcnt(n):
    out = np.zeros((128, 2, n), np.float32)
    pos = np.arange(n)
    for g, win in enumerate((2, 4, 8, 16)):
        lo = np.clip(pos - win // 2, 0, n)
        hi = np.clip(pos + win - win // 2, 0, n)
        ic = (1.0 / (hi - lo).astype(np.float32)).astype(np.float32)
        ch, half = g // 2, g % 2
        out[half * 64:(half + 1) * 64, ch, :] = ic[None, :]
    return out


def _hy_tables(n):
    z, t = _hy_pos(n)
    zf = z.T.copy()
    zr = np.empty_like(zf)
    zr[:, 0] = zf[:, 0]
    zr[:, 1:] = zf[:, :0:-1]
    tr = np.zeros((2, n), np.float32)
    tr[0] = t
    tr[1, 1:] = t[:0:-1]
    return np.stack([zf, zr]).astype(np.float32), tr


def _dft_tables(n1):
    N = n1 * 128
    nh = n1 // 2
    a = np.arange(128, dtype=np.float64)
    b = np.arange(n1, dtype=np.float64)
    C = np.cos(2 * np.pi * np.outer(a, a) / 128.0)
    S_ = np.sin(2 * np.pi * np.outer(a, a) / 128.0)
    C1 = np.cos(2 * np.pi * np.outer(b, b) / n1)
    S1 = np.sin(2 * np.pi * np.outer(b, b) / n1)
    twc = np.cos(2 * np.pi * np.outer(a, b) / N)
    tws = np.sin(2 * np.pi * np.outer(a, b) / N)
    FC = 4 * n1 + 512
    tf = np.zeros((128, FC), np.float32)
    tf[:, 0:n1] = twc
    tf[:, n1:2 * n1] = twc
    tf[:, 2 * n1:3 * n1] = tws
    tf[:, 3 * n1:4 * n1] = -tws
    o = 4 * n1
    tf[0:n1, o:o + 128] = twc.T
    tf[0:n1, o + 128:o + 256] = twc.T
    tf[0:n1, o + 256:o + 384] = tws.T
    tf[0:n1, o + 384:o + 512] = -tws.T
    BC = 2 * n1 + 128 * 3 + 512 + 2 * nh
    tb = np.zeros((128, BC), np.float32)
    tb[0:n1, 0:n1] = C1
    tb[0:n1, n1:2 * n1] = -S1
    o = 2 * n1
    tb[:, o:o + 128] = C
    tb[:, o + 128:o + 256] = S_
    tb[:, o + 256:o + 384] = -S_
    o += 384
    tb[:, o:o + 128] = C
    tb[:, o + 128:o + 256] = S_
    tb[:, o + 256:o + 384] = -S_
    tb[:, o + 384:o + 512] = C
    o += 512
    tb[0:n1, o:o + nh] = C1[:, 0:nh] / N
    tb[0:n1, o + nh:o + 2 * nh] = -S1[:, 0:nh] / N
    return tf, _bf(tb)


class Stream:
    pass


def build_program(dbg=None, nlayers=DEPTH, stop_phase=None, skip=()):
    nc = bass.Bass("TRN2", target_bir_lowering=False)
    kb = KB(nc)
    dbg = dbg or []

    def din(name, shape, dt=F32):
        return kb.dram(name, shape, dt, kind="ExternalInput")

    def dsc(name, shape, dt):
        return kb.dram(name, shape, dt, kind=("ExternalOutput" if name in dbg else "Internal"))

    xT_in = din("xT", [D, SEQ])
    ctxT_in = din("ctxT", [D, CTX])
    cvec = din("cvec", [128, 8, 2])
    w_mod = din("w_mod", [DEPTH, D, 6 * D])
    w_in = din("w_in", [DEPTH, D, 6400])
    w_br_attn = din("w_br_attn", [DEPTH, 512, D])
    w_br_hy = din("w_br_hyena", [DEPTH, 256, D])
    w_br_pool = din("w_br_pool", [DEPTH, 256, D])
    w_br_gm = din("w_br_gmlp", [DEPTH, 256, D])
    w_out = din("w_out", [DEPTH, D, D])
    ffn_g = din("ffn_w_gate", [DEPTH, D, HID])
    ffn_u = din("ffn_w_up", [DEPTH, D, HID])
    ffn_d = din("ffn_w_down", [DEPTH, HID, D])
    smallp_d = din("smallp", [128, NSP])
    gm_wsT = din("gm_wsT", [DEPTH, 4, 128, 128])
    gm_nw = din("gm_norm_w", [DEPTH, 256])
    pool_w = din("pool_w", [DEPTH, 4, 64, 64])
    hy_w1 = din("hy_w1", [DEPTH, 33, 64])
    hy_w2 = din("hy_w2", [DEPTH, 2, 64, 64])
    hy_w3 = din("hy_w3", [DEPTH, 64, 1024])
    hy_skip = din("hy_skip", [DEPTH, 512])
    cbf = din("cbf", [128, 3, 128], BF16)
    ropeT = din("ropeT", [2, 128, SEQ])
    invcnt_l = din("invcnt_l", [128, 2, SEQ])
    invcnt_c = din("invcnt_c", [128, 2, CTX])
    outT = kb.dram("outT", [D, SEQ], F32, kind="ExternalOutput")

    xa = dsc("xa", [D, SEQ], F32)
    xb = dsc("xb", [D, SEQ], F32)
    ca = dsc("ca", [D, CTX], F32)
    cb = dsc("cb", [D, CTX], F32)
    kT2d = dsc("kT2d", [128, 2, NKEY], BF16)
    vaugd = dsc("vaugd", [NKEY // 128, 128, 2, 66], BF16)

    def mk_stream(name, S, T, xin, koff):
        st = Stream()
        st.name, st.S, st.T, st.NT, st.koff = name, S, T, S // T, koff
        st.x = xin
        st.qT = dsc(f"qT_{name}", [128, 4, S], BF16)
        st.pT = dsc(f"pT_{name}", [D, S + 2 * PADC], BF16)
        st.zT = dsc(f"zT_{name}", [768, S], BF16)
        st.plT = dsc(f"plT_{name}", [256, S], BF16)
        st.gmT = dsc(f"gmT_{name}", [256, S], BF16)
        st.hyT = dsc(f"hyT_{name}", [256, S], BF16)
        st.attnT = dsc(f"attnT_{name}", [128, 4, S], BF16)
        st.rope = (name == "lat")
        st.si = 0 if name == "lat" else 1
        st.invcnt = invcnt_l if name == "lat" else invcnt_c
        n1 = 2 * S // 128
        st.zpos = din(f"zpos_{name}", [2, 33, S])
        st.trow = din(f"trow_{name}", [2, S])
        st.dftf = din(f"dftf_{name}", [128, 4 * n1 + 512])
        st.dftb = din(f"dftb_{name}", [128, 2 * n1 + 384 + 512 + n1], BF16)
        st.kfT = dsc(f"kfT_{name}", [512, 2 * S], BF16)
        return st

    lat = mk_stream("lat", SEQ, 512, xT_in, CTX)
    cxs = mk_stream("ctx", CTX, 256, ctxT_in, 0)

    identb = kb.tile("identb", [128, 128], BF16)
    permb = kb.tile("permb", [128, 128], BF16)
    ones1024 = kb.tile("ones1024", [128, 128], BF16)
    blk64 = kb.tile("blk64", [128, 128], BF16)
    onesf = kb.tile("onesf", [128, 64], F32)
    epst = kb.tile("epst", [128, 1], F32)
    smallp = kb.tile("smallp", [128, NSP], F32)
    modp = kb.tile("modp", [128, 2, 6, 8], F32)
    actbf = kb.tile("actbf", [128, 8, 2], BF16)
    zero_bf = kb.tile("zero_bf", [128, 8, PADC], BF16)

    kb.dma("sp", identb, cbf[:, 0, :])
    kb.dma("sp", permb, cbf[:, 1, :])
    kb.dma("sp", smallp, smallp_d)
    kb.memset("dve", ones1024, 1.0 / 1024.0)
    kb.memset("dve", blk64, 0.0)
    kb.memset("dve", blk64[0:64, 0:64], 1.0 / 64.0)
    kb.memset("dve", blk64[64:128, 64:128], 1.0 / 64.0)
    kb.memset("dve", onesf, 1.0)
    kb.memset("dve", epst, EPS)
    kb.memset("dve", zero_bf, 0.0)
    for st in (lat, cxs):
        pv = st.pT.rearrange("(c p) t -> p c t", p=128)
        kb.dma("sp", kb.dres(pv[:, :, 0:PADC], "padlo"), zero_bf)
        kb.dma("sp", kb.dres(pv[:, :, PADC + st.S:2 * PADC + st.S], "padhi"), zero_bf)
    kb.push()
    cv = kb.tile("cv", [128, 8, 2], F32)
    kb.dma("sp", cv, cvec)
    kb.act(actbf, cv, AF.Silu)
    kb.pop()

    def sp_col(name, l=None):
        o, n = SP_OFF[name]
        if l is not None:
            per = n // DEPTH
            return smallp[:, o + l * per:o + (l + 1) * per]
        return smallp[:, o:o + n]

    def phase_M(l):
        kb.push()
        wm = kb.ring("wm", 2, [128, 8, 1536], BF16)
        pm = kb.psum("pm", [128, 48, 2])
        modt = kb.tile("modt", [128, 48, 2], F32)
        wsrc = w_mod[l].rearrange("(kc p) n -> p kc n", p=128)
        for blk in range(4):
            w = wm[blk % 2]
            for kc in range(8):
                kb.dma("pool", w[:, kc, :], wsrc[:, kc, blk * 1536:(blk + 1) * 1536])
            for j in range(12):
                oc = blk * 12 + j
                for kc in range(8):
                    kb.mm(pm[:, oc, :], w[:, kc, j * 128:(j + 1) * 128], actbf[:, kc, :], start=(kc == 0),
                          stop=(kc == 7))
        bm = sp_col("b_mod", l)
        kb.tt("dve", modt, pm, bm.unsqueeze(2).bcast([128, 48, 2]), ALU.add)
        n1 = sp_col("norm1_w", l)
        n2 = sp_col("norm2_w", l)
        for s in range(2):
            kb.stt(modp[:, s, 0, :], modt[:, 8:16, s], 1.0, n1, ALU.add, ALU.mult)
            kb.copy("dve", modp[:, s, 1, :], modt[:, 0:8, s])
            kb.copy("dve", modp[:, s, 2, :], modt[:, 16:24, s])
            kb.stt(modp[:, s, 3, :], modt[:, 32:40, s], 1.0, n2, ALU.add, ALU.mult)
            kb.copy("dve", modp[:, s, 4, :], modt[:, 24:32, s])
            kb.copy("dve", modp[:, s, 5, :], modt[:, 40:48, s])
        kb.pop()

    def norm_mod(st, xt, sq, h, ps_ss, srt, rstd, which, TT):
        ai = 0 if which == 1 else 3
        kb.act(sq[:, :, :TT], xt[:, :, :TT], AF.Square)
        for c in range(8):
            kb.mm(ps_ss[:, :TT], ones1024, sq[:, c, :TT], start=(c == 0), stop=(c == 7))
        kb.act(srt[:, :TT], ps_ss[:, :TT], AF.Ln, bias=epst)
        kb.act(rstd[:, :TT], srt[:, :TT], AF.Exp, scale=-0.5)
        return ai

    def norm_apply(st, xt, xs, h, rstd, ai, TT):
        kb.tt("dve", xs[:, :, :TT], xt[:, :, :TT], rstd[:, :TT].unsqueeze(1).bcast([128, 8, TT]), ALU.mult)
        for c in range(8):
            kb.act(h[:, c, :TT], xs[:, c, :TT], AF.Identity, bias=modp[:, st.si, ai + 1, c:c + 1],
                   scale=modp[:, st.si, ai, c:c + 1])

    def phase_A(l):
        kb.push()
        wA = kb.tile("wA", [128, 8, 2432], BF16, multi=True)
        wsrc = w_in[l].rearrange("(kc p) n -> p kc n", p=128)
        for (d0, s0, n) in ((0, 0, 512), (512, 512, 64), (576, 512, 64), (640, 576, 64), (704, 576, 64),
                            (768, 768, 768), (1536, 1536, 256), (1792, 640, 128), (1920, 1792, 512)):
            for kc in range(8):
                kb.dma("pool", wA[:, kc, d0:d0 + n], wsrc[:, kc, s0:s0 + n])
        poolW = kb.tile("poolW", [128, 2, 128], BF16)
        kb.memset("dve", poolW, 0.0)
        for g in range(4):
            ch, hf = g // 2, g % 2
            kb.dma("pool", poolW[hf * 64:(hf + 1) * 64, ch, hf * 64:(hf + 1) * 64], pool_w[l, g])
        wsT = kb.tile("wsT", [128, 4, 128], BF16, multi=True)
        for g in range(4):
            kb.dma("pool", wsT[:, g, :], gm_wsT[l, g])
        gnw = kb.tile("gnw", [128, 256], F32)
        kb.dma("sp", gnw, gm_nw[l, :].pbcast(128))

        xt_r = kb.ring("xt", 1, [128, 8, 512], F32)
        sq = kb.tile("sq", [128, 8, 512], BF16)
        srt = kb.tile("srt", [128, 512], F32)
        rstd = kb.tile("rstd", [128, 512], F32)
        h_r = kb.ring("h", 2, [128, 8, 512], BF16)
        sqq_r = kb.ring("sqq", 2, [128, 512], BF16)
        srq = kb.tile("srq", [128, 512], F32)
        rq = kb.tile("rq", [128, 512], F32)
        qn_r = kb.ring("qn", 2, [128, 512], BF16)
        t1 = kb.tile("t1", [128, 512], F32)
        t2 = kb.tile("t2", [128, 512], F32)
        rope_r = kb.ring("rope", 2, [128, 2, 512], F32)
        qst_r = kb.ring("qst", 1, [128, 4, 512], BF16)
        kst_r = kb.ring("kst", 2, [128, 2, 512], BF16)
        vst_r = kb.ring("vst", 2, [128, 4, 2, 66], BF16)
        pst_r = kb.ring("pst", 1, [128, 8, 512], BF16)
        usb_r = kb.ring("usb", 2, [128, 256], F32)
        vnn = kb.tile("vnn", [128, 256], F32)
        vn_r = kb.ring("vn", 2, [128, 256], BF16)
        gmt_r = kb.ring("gmt", 2, [128, 256], BF16)
        stat = kb.tile("stat", [128, 8], F32)
        junk = kb.tile("junk", [128, 256], F32)
        gmst_r = kb.ring("gmst", 2, [128, 2, 512], BF16)
        pp_r = kb.ring("pp", 2, [128, 8, 512 + 2 * PADC], BF16)
        zst_r = kb.ring("zst", 1, [128, 6, 512], BF16)
        pa = [kb.tile(f"pa{i}", [128, 2, 512 + 2 * PADC], F32) for i in range(2)]
        icn_r = kb.ring("icn", 1, [128, 2, 512], F32)
        dpl = kb.tile("dpl", [128, 2, 512], F32)
        dpb = kb.tile("dpb", [128, 2, 512], BF16)
        plst_r = kb.ring("plst", 2, [128, 2, 512], BF16)

        ps_main = [kb.psum(f"ps_main{i}", [128, 512]) for i in range(2)]
        ps_qk = [kb.psum(f"ps_qk{i}", [128, 512]) for i in range(2)]
        ps_ss = ps_qk[0]
        ps_tok = [kb.psum(f"ps_tok{i}", [128, 512]) for i in range(2)]
        ps_gs = kb.psum("ps_gs", [128, 256])
        ps_gt_t = kb.stacks[-1].enter_context(nc.psum_tensor(kb._name("ps_gt"), [128, 2, 128], BF16))
        ps_gt = V(ps_gt_t[:, :, :], Res("ps_gt"), True)

        for r in vst_r:
            kb.memset("dve", r[:, :, :, 64:66], 1.0)

        wq = sp_col("q_norm_w", l)
        wk = sp_col("k_norm_w", l)
        hw = sp_col("hy_conv_w", l)
        hb = sp_col("hy_conv_b", l)
        pscale = sp_col("pool_scale", l)
        bsT = sp_col("gm_bs", l)

        cnt = {"main": 0, "qk": 0, "tok": 0, "h": 0}
        dg = kb.tile("dg", [128, 18, 128], BF16)
        for k in range(18):
            kb.ts("dve", dg[:, k, :], identb, hw[:, k:k + 1], ALU.mult)

        hcur = {}

        def prep_A(st, i):
            TT = st.T
            t0 = i * TT
            xt = xt_r[0]
            h = h_r[cnt["h"] % 2]
            cnt["h"] += 1
            xsrc = st.x.rearrange("(c p) t -> p c t", p=128)
            kb.dma("sp", xt[:, :, :TT], kb.dres(xsrc[:, :, t0:t0 + TT], i))
            rp = None
            if st.rope:
                rp = rope_r[i % 2]
                kb.dma("sp", rp[:, :, :TT], ropeT[:, :, t0:t0 + TT].rearrange("a p t -> p a t"))
            ai = norm_mod(st, xt, sq, h, ps_ss, srt, rstd, 1, TT)
            norm_apply(st, xt, xt, h, rstd, ai, TT)
            hcur[(st.name, i)] = (h, rp)

        def tile_A(st, i, full, mid_hook=None):
            TT = st.T
            t0 = i * TT
            h, rp = hcur.pop((st.name, i))
            qst = qst_r[0]
            kst = kst_r[i % 2]
            vst = vst_r[i % 2]
            pst = pst_r[0]
            gmst = gmst_r[i % 2]
            chunks = []
            if full:
                chunks += [("q", j, j * 128) for j in range(4)]
            chunks += [("k", 0, 512), ("k", 1, 640)]
            if full:
                chunks += [("p", j, 768 + j * 128) for j in range(8)]
            for ci, (kind, idx, c0) in enumerate(chunks):
                if mid_hook is not None and ci == min(8, len(chunks) - 1):
                    mid_hook()
                ps = ps_main[cnt["main"] % 2]
                cnt["main"] += 1
                for kc in range(8):
                    kb.mm(ps[:, :TT], wA[:, kc, c0:c0 + 128], h[:, kc, :TT], start=(kc == 0), stop=(kc == 7))
                if kind == "p":
                    kb.copy("act", pst[:, idx, :TT], ps[:, :TT])
                    continue
                sqq = sqq_r[cnt["qk"] % 2]
                qn = qn_r[cnt["qk"] % 2]
                psq = ps_qk[0]
                psw = ps_qk[1]
                cnt["qk"] += 1
                kb.act(sqq[:, :TT], ps[:, :TT], AF.Square)
                kb.mm(psq[:, :TT], blk64, sqq[:, :TT])
                kb.act(srq[:, :TT], psq[:, :TT], AF.Ln, bias=epst)
                kb.act(rq[:, :TT], srq[:, :TT], AF.Exp, scale=-0.5)
                dst = qst[:, idx, :TT] if kind == "q" else kst[:, idx, :TT]
                wn = wq if kind == "q" else wk
                if st.rope:
                    kb.stt(qn[:, :TT], ps[:, :TT], wn[:, 0:1], rq[:, :TT], ALU.mult, ALU.mult)
                    kb.mm(psw[:, :TT], permb, qn[:, :TT])
                    kb.tt("dve", t1[:, :TT], qn[:, :TT], rp[:, 0, :TT], ALU.mult)
                    kb.tt("dve", t2[:, :TT], psw[:, :TT], rp[:, 1, :TT], ALU.mult)
                    kb.tt("dve", dst, t1[:, :TT], t2[:, :TT], ALU.add)
                else:
                    kb.stt(dst, ps[:, :TT], wn[:, 0:1], rq[:, :TT], ALU.mult, ALU.mult)
            nsub = TT // 128
            for sub in range(nsub):
                ts_ = slice(sub * 128, (sub + 1) * 128)
                ps = ps_tok[cnt["tok"] % 2]
                cnt["tok"] += 1
                for kc in range(8):
                    kb.mm(ps[:, 0:128], h[:, kc, ts_], wA[:, kc, 1792:1920], start=(kc == 0), stop=(kc == 7))
                kb.copy("act", vst[:, sub, :, 0:64], ps[:, 0:128].rearrange("p (k d) -> p k d", k=2))
                if not full:
                    continue
                ps = ps_tok[cnt["tok"] % 2]
                cnt["tok"] += 1
                for kc in range(8):
                    kb.mm(ps[:, 0:512], h[:, kc, ts_], wA[:, kc, 1920:2432], start=(kc == 0), stop=(kc == 7))
                usb = usb_r[sub % 2]
                vn = vn_r[sub % 2]
                gmt = gmt_r[sub % 2]
                kb.copy("act", usb, ps[:, 0:256])
                kb.memset("dve", stat[:, 0:2], 0.0)
                kb.act(junk, ps[:, 256:512], AF.Identity, accum=stat[:, 0:1])
                kb.act(junk, ps[:, 256:512], AF.Square, accum=stat[:, 1:2])
                kb.ts("dve", stat[:, 2:3], stat[:, 0:1], 1.0 / 256.0, ALU.mult)
                kb.tt("dve", stat[:, 3:4], stat[:, 2:3], stat[:, 2:3], ALU.mult)
                kb.stt(stat[:, 4:5], stat[:, 1:2], 1.0 / 256.0, stat[:, 3:4], ALU.mult, ALU.subtract)
                kb.act(stat[:, 5:6], stat[:, 4:5], AF.Ln, bias=epst)
                kb.act(stat[:, 6:7], stat[:, 5:6], AF.Exp, scale=-0.5)
                kb.stt(stat[:, 7:8], stat[:, 2:3], -1.0, stat[:, 6:7], ALU.mult, ALU.mult)
                kb.act(vnn, ps[:, 256:512], AF.Identity, bias=stat[:, 7:8], scale=stat[:, 6:7])
                kb.tt("dve", vn, vnn, gnw, ALU.mult)
                for g in range(4):
                    gs = slice(g * 64, (g + 1) * 64)
                    kb.mm(ps_gs[:, gs], wsT[:, g, :], vn[:, gs])
                for g in range(4):
                    gs = slice(g * 64, (g + 1) * 64)
                    kb.stt(gmt[:, gs], ps_gs[:, gs], bsT[:, g:g + 1], usb[:, gs], ALU.add, ALU.mult)
                for hf in range(2):
                    kb.transpose(ps_gt[:, hf, :], gmt[:, hf * 128:(hf + 1) * 128], identb)
                kb.copy("dve", gmst[:, :, ts_], ps_gt)
            if full:
                kb.dma("sp", kb.dres(st.qT[:, :, t0:t0 + TT], i), qst[:, :, :TT])
                pv = st.pT.rearrange("(c p) t -> p c t", p=128)
                kb.dma("sp", kb.dres(pv[:, :, PADC + t0:PADC + t0 + TT], i), pst[:, :, :TT])
                gv = st.gmT.rearrange("(c p) t -> p c t", p=128)
                kb.dma("sp", kb.dres(gv[:, :, t0:t0 + TT], i), gmst[:, :, :TT])
            kb.dma("sp", kb.dres(kT2d[:, :, st.koff + t0:st.koff + t0 + TT], (st.name, i)), kst[:, :, :TT])
            c0 = (st.koff + t0) // 128
            kb.dma("sp", kb.dres(vaugd[c0:c0 + nsub].rearrange("c p k d -> p c k d"), (st.name, i)),
                   vst[:, :nsub])

        def tile_A2(st, i):
            TT = st.T
            t0 = i * TT
            W = TT + 2 * PADC
            pp = pp_r[i % 2]
            icn = icn_r[0]
            zst = zst_r[0]
            plst = plst_r[i % 2]
            pv = st.pT.rearrange("(c p) t -> p c t", p=128)
            extra = [kb.dres(pv, k) for k in (i - 1, i + 1) if 0 <= k < st.NT]
            extra += [kb.dres(pv, "padlo"), kb.dres(pv, "padhi")]
            kb.dma("sp", pp[:, :, :W], kb.dres(pv[:, :, t0:t0 + W], i), extra_in=extra)
            kb.dma("sp", icn[:, :, :TT], st.invcnt[:, :, t0:t0 + TT])
            for c in range(6):
                ps = ps_main[cnt["main"] % 2]
                cnt["main"] += 1
                for tap in range(3):
                    kb.mm(ps[:, :TT], dg[:, tap * 6 + c, :], pp[:, c, PADC - 1 + tap:PADC - 1 + tap + TT],
                          start=(tap == 0), stop=(tap == 2))
                kb.act(zst[:, c, :TT], ps[:, :TT], AF.Identity, bias=hb[:, c:c + 1])
            zv = st.zT.rearrange("(c p) t -> p c t", p=128)
            kb.dma("sp", kb.dres(zv[:, :, t0:t0 + TT], i), zst[:, :, :TT])
            z = pp[:, 6:8, :]
            A_, B_ = pa[0], pa[1]
            kb.tt("dve", A_[:, :, 1:W], z[:, :, 1:W], z[:, :, 0:W - 1], ALU.add)
            kb.tt("dve", B_[:, :, 3:W], A_[:, :, 3:W], A_[:, :, 1:W - 2], ALU.add)

            def sel(hf, ch, arr, sh):
                ps_ = slice(hf * 64, (hf + 1) * 64)
                kb.tt("dve", dpl[ps_, ch, :TT], arr[ps_, ch, PADC + sh:PADC + sh + TT], icn[ps_, ch, :TT], ALU.mult)
                kb.tt("dve", dpb[ps_, ch, :TT], dpl[ps_, ch, :TT], pp[ps_, 6 + ch, PADC:PADC + TT], ALU.subtract)

            sel(0, 0, A_, 0)
            sel(1, 0, B_, 1)
            kb.tt("dve", A_[:, 1, 7:W], B_[:, 1, 7:W], B_[:, 1, 3:W - 4], ALU.add)
            kb.tt("dve", B_[:, 1, 15:W], A_[:, 1, 15:W], A_[:, 1, 7:W - 8], ALU.add)
            sel(0, 1, A_, 3)
            sel(1, 1, B_, 7)
            for ch in range(2):
                ps = ps_main[cnt["main"] % 2]
                cnt["main"] += 1
                kb.mm(ps[:, :TT], poolW[:, ch, :], dpb[:, ch, :TT])
                kb.act(plst[:, ch, :TT], ps[:, :TT], AF.Identity, scale=pscale[:, ch:ch + 1])
            plv = st.plT.rearrange("(c p) t -> p c t", p=128)
            kb.dma("sp", kb.dres(plv[:, :, t0:t0 + TT], i), plst[:, :, :TT])

        fullc = (l < DEPTH - 1)
        prep_A(cxs, 0)
        tile_A(cxs, 0, fullc, mid_hook=lambda: prep_A(lat, 0))
        if fullc:
            tile_A2(cxs, 0)
        for i in range(lat.NT):
            hook = (lambda j=i: prep_A(lat, j + 1)) if i + 1 < lat.NT else None
            tile_A(lat, i, True, mid_hook=hook)
            if i >= 1:
                tile_A2(lat, i - 1)
        tile_A2(lat, lat.NT - 1)
        kb.pop()

    def phase_H(l, st):
        S = st.S
        N1 = 2 * S // 128
        NH = N1 // 2
        PI = float(np.pi)
        kb.push()
        SLH = min(2048, S)
        SLW = min(512, S)
        NSL = S // SLW
        w1 = kb.tile("w1", [33, 64], F32)
        kb.dma("sp", w1, hy_w1[l])
        w2 = kb.tile("w2", [64, 2, 64], F32, multi=True)
        for i in range(2):
            kb.dma("sp", w2[:, i, :], hy_w2[l, i])
        w3 = kb.tile("w3", [64, 1024], F32)
        kb.dma("sp", w3, hy_w3[l])
        fr = sp_col("hy_freq", l)[0:64]
        b1 = sp_col("hy_b1", l)[0:64]
        b2 = sp_col("hy_b2", l)[0:64]
        fb = kb.tile("fb", [64, 3], F32)
        kb.tt("dve", fb[:, 0:1], fr, b1, ALU.mult)
        kb.tt("dve", fb[:, 1:3], b2, fr.bcast([64, 2]), ALU.mult)
        negdec = kb.tile("negdec", [128, 8], F32)
        kb.act(negdec, sp_col("hy_decay", l), AF.Abs)
        kb.ts("dve", negdec, negdec, -1.0, ALU.mult)
        hd3 = kb.tile("hd3", [64, 2, S], F32, multi=True)
        zt_r = kb.ring("zt", 2, [33, SLH], F32)
        ha = kb.tile("ha", [64, SLH], F32)
        m1 = kb.tile("m1", [64, SLH], F32)
        hb_r = kb.ring("hb", 2, [64, SLH], F32)
        ps_h = kb.psum("ps_h", [64, SLH])
        n = 0
        for d in range(2):
            for sl in range(S // SLH):
                cs = slice(sl * SLH, (sl + 1) * SLH)
                zt = zt_r[n % 2]
                n += 1
                kb.dma("sp", zt, st.zpos[d, :, cs])
                cur, Kc = zt, 33
                for layer in range(3):
                    lhsT = w1 if layer == 0 else w2[:, layer - 1, :]
                    for q in range(max(1, SLH // 512)):
                        qs = slice(q * 512, min((q + 1) * 512, SLH))
                        kb.mm(ps_h[:, qs], lhsT, cur[0:Kc, qs])
                    kb.act(ha, ps_h, AF.Identity, scale=fr, bias=fb[:, layer:layer + 1])
                    kb.ts("dve", m1, ha, PI, ALU.is_gt, -2.0 * PI, ALU.mult)
                    kb.tt("dve", ha, ha, m1, ALU.add)
                    kb.ts("dve", m1, ha, -PI, ALU.is_lt, 2.0 * PI, ALU.mult)
                    kb.tt("dve", ha, ha, m1, ALU.add)
                    dst = hd3[:, d, cs] if layer == 2 else hb_r[layer % 2]
                    kb.act(dst, ha, AF.Sin)
                    cur, Kc = dst, 64
        tr_r = kb.ring("tr", 2, [128, 2, SLW], F32)
        win_r = kb.ring("win", 2, [128, SLW], F32)
        f_r = kb.ring("f", 2, [128, SLW], F32)
        junk = kb.tile("junk", [128, SLW], F32)
        kst_r = kb.ring("kst", 2, [128, SLW], BF16)
        part = kb.tile("part", [128, 8, NSL], F32)
        nrm8 = kb.tile("nrm8", [128, 8], F32)
        rinv = kb.tile("rinv", [128, 4], F32)
        ps_f = [kb.psum(f"ps_f{i}", [128, 512]) for i in range(2)]
        kb.memset("dve", part, 0.0)
        n = 0
        for pas in (1, 2):
            for sl in range(NSL):
                cs = slice(sl * SLW, (sl + 1) * SLW)
                tr = tr_r[(pas * NSL + sl) % 2]
                kb.dma("sp", tr, st.trow[:, cs].pbcast(128))
                for oc in range(8):
                    o, d, wh = oc // 4, (oc % 4) // 2, oc % 2
                    ps = ps_f[n % 2]
                    win = win_r[n % 2]
                    f = f_r[n % 2]
                    kst = kst_r[n % 2]
                    n += 1
                    kb.mm(ps[:, :SLW], w3[:, oc * 128:(oc + 1) * 128], hd3[:, d, cs])
                    kb.act(win, tr[:, d, :], AF.Exp, scale=negdec[:, oc:oc + 1])
                    kb.tt("dve", f, ps[:, :SLW], win, ALU.mult)
                    if d == 1 and sl == 0:
                        kb.memset("dve", f[:, 0:1], 0.0)
                    if pas == 1:
                        kb.act(junk, f, AF.Abs, accum=part[:, oc, sl:sl + 1])
                    else:
                        kb.ts("dve", kst, f, rinv[:, 2 * o + wh:2 * o + wh + 1], ALU.mult)
                        r0 = (2 * o + wh) * 128
                        kb.dma("sp", st.kfT[r0:r0 + 128, d * S + sl * SLW:d * S + (sl + 1) * SLW], kst)
            if pas == 1:
                kb.op("dve", lambda: nc.vector.reduce_sum(out=nrm8.ap, in_=part.ap, axis=mybir.AxisListType.X),
                      [nrm8], [part])
                n4 = nrm8.rearrange("p (o d w) -> p o d w", o=2, d=2)
                kb.tt("dve", rinv.rearrange("p (o w) -> p o w", o=2), n4[:, :, 0, :], n4[:, :, 1, :], ALU.add)
                kb.recip(rinv, rinv)
        kb.pop()

        kb.push()
        tabf = kb.tile("tabf", [128, 4 * N1 + 512], F32)
        tabb = kb.tile("tabb", [128, 2 * N1 + 896 + N1], BF16)
        kb.dma("sp", tabf, st.dftf)
        kb.dma("sp", tabb, st.dftb)
        skipb = kb.tile("skipb", [128, 512], F32)
        kb.dma("sp", skipb[0:NH, :], hy_skip[l, :].pbcast(NH))
        TWc2 = tabf[:, 0:2 * N1].rearrange("p (h k) -> p h k", h=2).unsqueeze(1).bcast([128, 4, 2, N1])
        TWs = tabf[:, 2 * N1:3 * N1].unsqueeze(1).bcast([128, 4, N1])
        nTWs = tabf[:, 3 * N1:4 * N1].unsqueeze(1).bcast([128, 4, N1])
        o_ = 4 * N1
        iTWc2 = tabf[0:N1, o_:o_ + 256].rearrange("p (h k) -> p h k", h=2).unsqueeze(1).bcast([N1, 4, 2, 128])
        iTWs = tabf[0:N1, o_ + 256:o_ + 384].unsqueeze(1).bcast([N1, 4, 128])
        inTWs = tabf[0:N1, o_ + 384:o_ + 512].unsqueeze(1).bcast([N1, 4, 128])
        F1 = tabb[:, 0:2 * N1]
        o_ = 2 * N1
        Cm, Sm, nSm = tabb[:, o_:o_ + 128], tabb[:, o_ + 128:o_ + 256], tabb[:, o_ + 256:o_ + 384]
        CS, nSC = tabb[:, o_ + 384:o_ + 640], tabb[:, o_ + 640:o_ + 896]
        o_ += 896
        C1n, nS1n = tabb[0:N1, o_:o_ + NH], tabb[0:N1, o_ + NH:o_ + 2 * NH]
        Cg = 16
        NG = 256 // Cg
        vin_r = kb.ring("vin", 2, [128, Cg, 128], BF16)
        x1_r = kb.ring("x1in", 2, [128, Cg, 128], BF16)
        x2_r = kb.ring("x2in", 2, [128, Cg, 128], BF16)
        kf_r = [kb.ring(f"kf{o}", 2, [128, Cg, 128], BF16) for o in range(2)]
        hyst_r = kb.ring("hyst", 2, [128, Cg, 128], BF16)
        def mk_tmp(i):
            t = {}
            t["X1"] = kb.tile(f"X1_{i}", [128, 4, 2, 128], F32)
            t["X2"] = kb.tile(f"X2_{i}", [128, 4, 2, 128], F32)
            t["Ap"] = kb.tile(f"Ap_{i}", [128, 4, 2, 128], BF16)
            t["Kf"] = kb.tile(f"Kf_{i}", [128, 4, 2, 128], F32)
            t["T"] = [kb.tile(f"T{j}_{i}", [128, 4, 128], F32) for j in range(4)]
            t["Y"] = kb.tile(f"Y_{i}", [128, 4, 2, 128], BF16)
            t["XB1"] = kb.tile(f"XB1_{i}", [128, 4, 2, 128], F32)
            t["XB2"] = kb.tile(f"XB2_{i}", [128, 4, 2, 128], F32)
            t["Bp"] = kb.tile(f"Bp_{i}", [128, 4, 2, 128], BF16)
            t["tp"] = kb.tile(f"tp_{i}", [128, 4, 128], F32)
            t["u2"] = kb.tile(f"u2_{i}", [128, 4, 128], BF16)
            return t

        tmps = [mk_tmp(0), mk_tmp(1)]
        psA = kb.psum("psA", [128, 4, 256])
        psUr = kb.psum("psUr", [128, 512])
        psUi = kb.psum("psUi", [128, 512])
        psB = kb.psum("psB", [128, 4, 256])
        psY = kb.psum("psY", [128, 512])
        ur = psUr[:, 0:4 * N1].rearrange("p (c k) -> p c k", c=4)
        ui = psUi[:, 0:4 * N1].rearrange("p (c k) -> p c k", c=4)
        yv = psY[0:NH, 0:512].rearrange("p (c k) -> p c k", c=4)

        def fwd_fft(src, K, t):
            X1, X2, Ap = t["X1"], t["X2"], t["Ap"]
            for c in range(4):
                kb.mm(psA[:, c, 0:2 * N1], src[0:K, c, :], F1[0:K, :])
            yield
            pA4 = psA[:, :, 0:2 * N1].rearrange("p c (h k) -> p c h k", h=2)
            kb.tt("dve", X1[:, :, :, 0:N1], pA4, TWc2, ALU.mult)
            kb.tt("dve", X2[:, :, 0, 0:N1], pA4[:, :, 1, :], TWs, ALU.mult)
            kb.tt("dve", X2[:, :, 1, 0:N1], pA4[:, :, 0, :], nTWs, ALU.mult)
            kb.tt("dve", Ap[:, :, :, 0:N1], X1[:, :, :, 0:N1], X2[:, :, :, 0:N1], ALU.add)
            yield
            rr, ri = Ap[:, :, 0, 0:N1], Ap[:, :, 1, 0:N1]
            kb.mm(ur, Cm, rr, start=True, stop=False)
            kb.mm(ur, Sm, ri, start=False, stop=True)
            kb.mm(ui, Cm, ri, start=True, stop=False)
            kb.mm(ui, nSm, rr, start=False, stop=True)
            yield

        def conv_block(o, kfblk, ublk, gateblk, dst, sk, t):
            Kf, Y, Bp, XB1, XB2, tp = t["Kf"], t["Y"], t["Bp"], t["XB1"], t["XB2"], t["tp"]
            yield from fwd_fft(kfblk, N1, t)
            kb.copy("act", Kf[:, :, 0, 0:N1], ur)
            kb.copy("act", Kf[:, :, 1, 0:N1], ui)
            yield
            yield from fwd_fft(ublk, NH, t)
            Kr, Ki = Kf[:, :, 0, 0:N1], Kf[:, :, 1, 0:N1]
            tt_ = [x[:, :, 0:N1] for x in t["T"]]
            kb.tt("dve", tt_[0], ur, Kr, ALU.mult)
            kb.tt("dve", tt_[1], ui, Ki, ALU.mult)
            kb.tt("dve", tt_[2], ur, Ki, ALU.mult)
            kb.tt("dve", tt_[3], ui, Kr, ALU.mult)
            kb.tt("dve", Y[:, :, 0, 0:N1], tt_[0], tt_[1], ALU.subtract)
            kb.tt("dve", Y[:, :, 1, 0:N1], tt_[2], tt_[3], ALU.add)
            yield
            for c in range(4):
                kb.mm(psB[0:N1, c, :], Y[:, c, 0, 0:N1], CS, start=True, stop=False)
                kb.mm(psB[0:N1, c, :], Y[:, c, 1, 0:N1], nSC, start=False, stop=True)
            yield
            pB4 = psB[0:N1].rearrange("p c (h k) -> p c h k", h=2)
            kb.tt("dve", XB1[0:N1], pB4, iTWc2, ALU.mult)
            kb.tt("dve", XB2[0:N1, :, 0, :], pB4[:, :, 1, :], inTWs, ALU.mult)
            kb.tt("dve", XB2[0:N1, :, 1, :], pB4[:, :, 0, :], iTWs, ALU.mult)
            kb.tt("dve", Bp[0:N1], XB1[0:N1], XB2[0:N1], ALU.add)
            yield
            kb.mm(yv, C1n, Bp[0:N1, :, 0, :], start=True, stop=False)
            kb.mm(yv, nS1n, Bp[0:N1, :, 1, :], start=False, stop=True)
            yield
            kb.tt("dve", tp[0:NH], ublk[0:NH], sk.unsqueeze(2).bcast([NH, 4, 128]), ALU.mult)
            kb.tt("dve", tp[0:NH], tp[0:NH], yv, ALU.add)
            kb.tt("dve", dst, tp[0:NH], gateblk[0:NH], ALU.mult)
            yield

        def fftv(src_rows, c0, nrow):
            return src_rows[c0:c0 + Cg, :].rearrange("c (a b) -> a c b", b=128)

        gt = {}

        def load_group(g):
            c0 = g * Cg
            vin, x1in, x2in = vin_r[g % 2], x1_r[g % 2], x2_r[g % 2]
            kfs = [kf_r[0][g % 2], kf_r[1][g % 2]]
            kb.dma("sp", vin[0:NH], fftv(st.zT[0:256], c0, NH))
            kb.dma("sp", x1in[0:NH], fftv(st.zT[256:512], c0, NH))
            kb.dma("sp", x2in[0:NH], fftv(st.zT[512:768], c0, NH))
            for o in range(2):
                kb.dma("sp", kfs[o][0:N1], fftv(st.kfT[o * 256:(o + 1) * 256], c0, N1))
            gt[g] = (vin, x1in, x2in, kfs, hyst_r[g % 2])

        def chain(g, b, t):
            if b == 0 and g + 1 < NG:
                load_group(g + 1)
            vin, x1in, x2in, kfs, hyst = gt[g]
            c0 = g * Cg
            bs = slice(4 * b, 4 * b + 4)
            u2 = t["u2"]
            cc = c0 + 4 * b
            yield from conv_block(0, kfs[0][:, bs, :], vin[:, bs, :], x1in[:, bs, :], u2[0:NH],
                                  skipb[0:NH, cc:cc + 4], t)
            yield from conv_block(1, kfs[1][:, bs, :], u2, x2in[:, bs, :], hyst[0:NH, bs, :],
                                  skipb[0:NH, 256 + cc:256 + cc + 4], t)
            gdone[g] += 1
            if gdone[g] == Cg // 4:
                kb.dma("sp", fftv(st.hyT, c0, NH), hyst[0:NH])

        NCHAIN = 1
        gdone = {g: 0 for g in range(NG)}
        load_group(0)
        todo = [(g, b) for g in range(NG) for b in range(Cg // 4)]
        active = []
        nstart = 0
        while todo or active:
            while todo and len(active) < NCHAIN:
                g, b = todo.pop(0)
                gen = chain(g, b, tmps[nstart % 2])
                nstart += 1
                if active:
                    pass
                active.append(gen)
            for gen in list(active):
                try:
                    next(gen)
                except StopIteration:
                    active.remove(gen)
        kb.pop()

    def phase_B(l, streams):
        kb.push()
        kT = kb.tile("kT", [128, 2, NKEY], BF16, multi=True)
        va = kb.tile("va", [128, NKEY // 128, 2, 66], BF16, multi=True)
        nld = 6
        per = (NKEY // 128) // nld
        for i in range(nld):
            kb.dma("sp", kT[:, :, i * per * 128:(i + 1) * per * 128], kT2d[:, :, i * per * 128:(i + 1) * per * 128])
            kb.dma("sp", va[:, i * per:(i + 1) * per], vaugd[i * per:(i + 1) * per].rearrange("c p k d -> p c k d"))
        qt_r = kb.ring("qz", 2, [128, 8, 512], BF16)
        for q_ in qt_r:
            kb.memset("dve", q_, 0.0)
        pt_r = kb.ring("pt", 3, [128, 2, 512], BF16)
        ost_r = kb.ring("ost", 2, [128, 4, 512], BF16)
        osb_r = kb.ring("osb", 2, [64, 512], F32)
        otmp = kb.tile("otmp", [64, 512], BF16)
        rden_r = kb.ring("rden", 2, [128, 512], F32)
        ps_s = [kb.psum(f"ps_s{i}", [128, 2, 512]) for i in range(2)]
        ps_o = [kb.psum(f"ps_o{i}", [128, 512]) for i in range(2)]
        ps_bc = kb.psum("ps_bc", [64, 512])

        items = []
        for st in streams:
            nk = CTX if st.name == "ctx" else NKEY
            pairs = nk // 256
            for i in range(st.NT):
                for hd in range(8):
                    for cp in range(pairs):
                        items.append((st, i, hd, cp, pairs))
        qcur = {}
        tord = {}
        for (st_, i_, _h, _c, _p) in items:
            tord.setdefault((st_.name, i_), len(tord))

        def get_q(st, i):
            key = (st.name, i)
            if key not in qcur:
                qt = qt_r[tord[key] % 2]
                for hf_ in range(2):
                    P_ = slice(hf_ * 64, hf_ * 64 + 64)
                    kb.dma("sp", qt[P_, hf_:8:2, :st.T], st.qT[P_, :, i * st.T:(i + 1) * st.T])
                qcur[key] = qt
            return qcur[key]

        def emit_qk(n):
            st, i, hd, cp, pairs = items[n]
            qt = get_q(st, i)
            j, hf, kv = hd // 2, hd % 2, hd // 4
            P = slice(hf * 64, hf * 64 + 64)
            pss = ps_s[n % 2]
            for u in range(2):
                ch = 2 * cp + u
                kb.mm(pss[:, u, :st.T], kT[:, kv, ch * 128:(ch + 1) * 128], qt[:, hd, :st.T])

        pending = []

        def finalize2(st, i, hd, po, rden, osb, ost):
            NQ = st.T
            kb.mm(ps_bc[:, :NQ], onesf[64:65, 0:64], rden[64:65, :NQ])
            if hd % 2 == 0:
                kb.tt("dve", ost[0:64, hd // 2, :NQ], osb[:, :NQ], ps_bc[:, :NQ], ALU.mult)
            else:
                kb.tt("dve", otmp[0:64, :NQ], osb[:, :NQ], ps_bc[:, :NQ], ALU.mult)
                kb.copy("dve", ost[64:128, hd // 2, :NQ], otmp[0:64, :NQ])
            if hd == 7:
                kb.dma("sp", st.attnT[:, :, i * NQ:(i + 1) * NQ], ost[:, :, :NQ])

        emit_qk(0)
        hcount = 0
        for n in range(len(items)):
            st, i, hd, cp, pairs = items[n]
            NQ = st.T
            kv = hd // 4
            if n + 1 < len(items):
                emit_qk(n + 1)
            pt = pt_r[n % 3]
            kb.act(pt[:, :, :NQ], ps_s[n % 2][:, :, :NQ], AF.Exp, scale=0.125)
            if cp == 0:
                po = ps_o[hcount % 2]
            for u in range(2):
                ch = 2 * cp + u
                kb.mm(po[0:65, :NQ], va[:, ch, kv, 0:65], pt[:, u, :NQ], start=(cp == 0 and u == 0),
                      stop=(cp == pairs - 1 and u == 1))
            while pending:
                finalize2(*pending.pop(0))
            if cp == pairs - 1:
                rden = rden_r[hcount % 2]
                osb = osb_r[hcount % 2]
                ost = ost_r[tord[(st.name, i)] % 2]
                kb.recip(rden[64:65, :NQ], po[64:65, :NQ])
                kb.copy("act", osb[:, :NQ], po[0:64, :NQ])
                pending.append((st, i, hd, po, rden, osb, ost))
                hcount += 1
        while pending:
            finalize2(*pending.pop(0))
        kb.pop()

    def prep_C(st, i, TT, which, xt, sq, srt, rstd, h, ps_ss):
        t0 = i * TT
        xsrc = st.x.rearrange("(c p) t -> p c t", p=128)
        kb.dma("sp", xt[:, :, :TT], xsrc[:, :, t0:t0 + TT])
        ai = norm_mod(st, xt, sq, h, ps_ss, srt, rstd, which, TT)
        norm_apply(st, xt, xt, h, rstd, ai, TT)

    def phase_C1(l, streams, TT):
        kb.push()
        wG = kb.tile("wG", [128, 8, 4096], BF16, multi=True)
        wBA = kb.tile("wBA", [128, 4, 1024], BF16, multi=True)
        wBH = kb.tile("wBH", [128, 2, 1024], BF16, multi=True)
        wBP = kb.tile("wBP", [128, 2, 1024], BF16, multi=True)
        wBG = kb.tile("wBG", [128, 2, 1024], BF16, multi=True)
        wO = kb.tile("wO", [128, 8, 1024], BF16, multi=True)
        wsrc = w_in[l].rearrange("(kc p) n -> p kc n", p=128)
        for kc in range(8):
            kb.dma("pool", wG[:, kc, :], wsrc[:, kc, 2304:6400])
        for c in range(4):
            kb.dma("pool", wBA[:, c, :], w_br_attn[l, c * 128:(c + 1) * 128, :])
        for (wt, src) in ((wBH, w_br_hy), (wBP, w_br_pool), (wBG, w_br_gm)):
            for c in range(2):
                kb.dma("pool", wt[:, c, :], src[l, c * 128:(c + 1) * 128, :])
        for kc in range(8):
            kb.dma("pool", wO[:, kc, :], w_out[l, kc * 128:(kc + 1) * 128, :])
        xt = kb.tile("xt", [128, 8, TT], F32)
        sq = kb.tile("sq", [128, 8, TT], BF16)
        srt = kb.tile("srt", [128, TT], F32)
        rstd = kb.tile("rstd", [128, TT], F32)
        h_r = kb.ring("h", 2, [128, 8, TT], BF16)
        at = kb.tile("at", [128, 4, TT], BF16)
        hyt_r = kb.ring("hyt", 2, [128, 2, TT], BF16)
        plt_r = kb.ring("plt", 2, [128, 2, TT], BF16)
        gmt_r = kb.ring("gmt", 2, [128, 2, TT], BF16)
        g_r = kb.ring("g", 2, [128, TT], BF16)
        tmp_r = kb.ring("tmp", 2, [128, TT], F32)
        acc = kb.tile("acc", [128, TT], F32)
        mg = kb.tile("mg", [128, 8, TT], BF16)
        xr_r = kb.ring("xr", 2, [128, TT], F32)
        xo_r = kb.ring("xo", 2, [128, TT], F32)
        ps_g = [kb.psum(f"ps_g{i}", [128, 512]) for i in range(2)]
        ps_b = [kb.psum(f"ps_b{i}", [128, 512]) for i in range(2)]
        ps_o = [kb.psum(f"ps_o{i}", [128, 512]) for i in range(2)]
        ps_ss = kb.psum("ps_ss", [128, 512])
        tiles = [(st, i) for st in streams for i in range(st.S // min(TT, st.S))]
        cnt = {"h": 0, "n": 0, "o": 0}
        hmap = {}

        def prep(k):
            st, i = tiles[k]
            T_ = min(TT, st.S)
            h = h_r[cnt["h"] % 2]
            cnt["h"] += 1
            prep_C(st, i, T_, 1, xt, sq, srt, rstd, h, ps_ss)
            hmap[k] = h

        prep(0)
        for k, (st, i) in enumerate(tiles):
            T_ = min(TT, st.S)
            t0 = i * T_
            h = hmap.pop(k)
            hyt, plt, gmt = hyt_r[k % 2], plt_r[k % 2], gmt_r[k % 2]
            for (dst, src) in ((hyt, st.hyT), (plt, st.plT), (gmt, st.gmT)):
                kb.dma("sp", dst[:, :, :T_], src.rearrange("(c p) t -> p c t", p=128)[:, :, t0:t0 + T_])
            kb.dma("sp", at[:, :, :T_], st.attnT[:, :, t0:t0 + T_])
            branches = ((1, wBH, hyt, 2, 128), (2, wBP, plt, 2, 128), (3, wBG, gmt, 2, 128), (0, wBA, at, 4, 128))
            for m in range(8):
                if m == 4 and k + 1 < len(tiles):
                    prep(k + 1)
                ms = slice(m * 128, (m + 1) * 128)
                for bi, (gi, wB, src, nk, kp) in enumerate(branches):
                    pg = ps_g[cnt["n"] % 2]
                    pb = ps_b[cnt["n"] % 2]
                    g = g_r[cnt["n"] % 2]
                    tmp = tmp_r[cnt["n"] % 2]
                    cnt["n"] += 1
                    for kc in range(8):
                        kb.mm(pg[:, :T_], wG[:, kc, gi * 1024 + m * 128:gi * 1024 + (m + 1) * 128], h[:, kc, :T_],
                              start=(kc == 0), stop=(kc == 7))
                    for c in range(nk):
                        kb.mm(pb[:, :T_], wB[0:kp, c, ms], src[0:kp, c, :T_], start=(c == 0), stop=(c == nk - 1))
                    kb.act(g[:, :T_], pg[:, :T_], AF.Sigmoid)
                    if bi == 0:
                        kb.tt("dve", acc[:, :T_], g[:, :T_], pb[:, :T_], ALU.mult)
                    else:
                        kb.tt("dve", tmp[:, :T_], g[:, :T_], pb[:, :T_], ALU.mult)
                        dst = acc[:, :T_] if bi < 3 else mg[:, m, :T_]
                        kb.tt("dve", dst, acc[:, :T_], tmp[:, :T_], ALU.add)
            xsrc = st.x.rearrange("(c p) t -> p c t", p=128)
            xdst = st.xmid.rearrange("(c p) t -> p c t", p=128)
            for m in range(8):
                po = ps_o[cnt["o"] % 2]
                xr = xr_r[cnt["o"] % 2]
                xo = xo_r[cnt["o"] % 2]
                cnt["o"] += 1
                kb.dma("sp", xr[:, :T_], xsrc[:, m, t0:t0 + T_])
                for kc in range(8):
                    kb.mm(po[:, :T_], wO[:, kc, m * 128:(m + 1) * 128], mg[:, kc, :T_], start=(kc == 0), stop=(kc == 7))
                kb.stt(xo[:, :T_], po[:, :T_], modp[:, st.si, 2, m:m + 1], xr[:, :T_], ALU.mult, ALU.add)
                kb.dma("sp", xdst[:, m, t0:t0 + T_], xo[:, :T_])
        kb.pop()

    def phase_C2(l, streams, TT, final):
        kb.push()
        wg = kb.tile("wg", [128, 8, HID], BF16, multi=True)
        wu = kb.tile("wu", [128, 8, HID], BF16, multi=True)
        wd = kb.tile("wd", [128, 22, D], BF16, multi=True)
        for kc in range(8):
            kb.dma("pool", wg[:, kc, :], ffn_g[l, kc * 128:(kc + 1) * 128, :])
            kb.dma("pool", wu[:, kc, :], ffn_u[l, kc * 128:(kc + 1) * 128, :])
        for j in range(22):
            kb.dma("pool", wd[:, j, :], ffn_d[l, j * 128:(j + 1) * 128, :])
        xt = kb.tile("xt", [128, 8, TT], F32)
        sq = kb.tile("sq", [128, 8, TT], BF16)
        srt = kb.tile("srt", [128, TT], F32)
        rstd = kb.tile("rstd", [128, TT], F32)
        h_r = kb.ring("h", 2, [128, 8, TT], BF16)
        a_t = kb.tile("a", [128, 22, TT], BF16)
        sl_r = kb.ring("sl", 2, [128, TT], F32)
        xr_r = kb.ring("xr", 2, [128, TT], F32)
        xo_r = kb.ring("xo", 2, [128, TT], F32)
        if final:
            xf = kb.tile("xf", [128, 8, TT], F32)
            fsq = kb.tile("fsq", [128, 8, TT], BF16)
            fnw = sp_col("final_norm_w")
        ps_g = [kb.psum(f"ps_g{i}", [128, 512]) for i in range(2)]
        ps_u = [kb.psum(f"ps_u{i}", [128, 512]) for i in range(2)]
        ps_o = [kb.psum(f"ps_o{i}", [128, 512]) for i in range(2)]
        ps_ss = kb.psum("ps_ss", [128, 512])
        tiles = [(st, i) for st in streams for i in range(st.S // min(TT, st.S))]
        cnt = {"h": 0, "n": 0, "o": 0}
        hmap = {}

        def prep(k):
            st, i = tiles[k]
            T_ = min(TT, st.S)
            h = h_r[cnt["h"] % 2]
            cnt["h"] += 1
            prep_C(st, i, T_, 2, xt, sq, srt, rstd, h, ps_ss)
            hmap[k] = h

        prep(0)
        for k, (st, i) in enumerate(tiles):
            T_ = min(TT, st.S)
            t0 = i * T_
            h = hmap.pop(k)
            for j in range(22):
                if j == 12 and k + 1 < len(tiles):
                    prep(k + 1)
                pg = ps_g[cnt["n"] % 2]
                pu = ps_u[cnt["n"] % 2]
                sl = sl_r[cnt["n"] % 2]
                cnt["n"] += 1
                for kc in range(8):
                    kb.mm(pg[:, :T_], wg[:, kc, j * 128:(j + 1) * 128], h[:, kc, :T_], start=(kc == 0), stop=(kc == 7))
                for kc in range(8):
                    kb.mm(pu[:, :T_], wu[:, kc, j * 128:(j + 1) * 128], h[:, kc, :T_], start=(kc == 0), stop=(kc == 7))
                kb.act(sl[:, :T_], pg[:, :T_], AF.Silu)
                kb.tt("dve", a_t[:, j, :T_], sl[:, :T_], pu[:, :T_], ALU.mult)
            xsrc = st.x.rearrange("(c p) t -> p c t", p=128)
            xdst = st.xnext.rearrange("(c p) t -> p c t", p=128)
            dofinal = final and st.name == "lat"
            for m in range(8):
                po = ps_o[cnt["o"] % 2]
                xr = xr_r[cnt["o"] % 2]
                xo = xo_r[cnt["o"] % 2]
                cnt["o"] += 1
                kb.dma("sp", xr[:, :T_], xsrc[:, m, t0:t0 + T_])
                for j in range(22):
                    kb.mm(po[:, :T_], wd[:, j, m * 128:(m + 1) * 128], a_t[:, j, :T_], start=(j == 0), stop=(j == 21))
                if dofinal:
                    kb.stt(xf[:, m, :T_], po[:, :T_], modp[:, st.si, 5, m:m + 1], xr[:, :T_], ALU.mult, ALU.add)
                else:
                    kb.stt(xo[:, :T_], po[:, :T_], modp[:, st.si, 5, m:m + 1], xr[:, :T_], ALU.mult, ALU.add)
                    kb.dma("sp", xdst[:, m, t0:t0 + T_], xo[:, :T_])
            if dofinal:
                kb.act(fsq[:, :, :T_], xf[:, :, :T_], AF.Square)
                for c in range(8):
                    kb.mm(ps_ss[:, :T_], ones1024, fsq[:, c, :T_], start=(c == 0), stop=(c == 7))
                kb.act(srt[:, :T_], ps_ss[:, :T_], AF.Ln, bias=epst)
                kb.act(rstd[:, :T_], srt[:, :T_], AF.Exp, scale=-0.5)
                kb.tt("dve", xf[:, :, :T_], xf[:, :, :T_], rstd[:, :T_].unsqueeze(1).bcast([128, 8, T_]), ALU.mult)
                odst = outT.rearrange("(c p) t -> p c t", p=128)
                for c in range(8):
                    xo = xo_r[cnt["o"] % 2]
                    cnt["o"] += 1
                    kb.act(xo[:, :T_], xf[:, c, :T_], AF.Identity, scale=fnw[:, c:c + 1])
                    kb.dma("sp", odst[:, c, t0:t0 + T_], xo[:, :T_])
        kb.pop()

    TC1 = 512
    TC2 = 256
    for l in range(nlayers):
        last = (l == DEPTH - 1)
        streams = [lat] if last else [cxs, lat]
        lat.xmid, cxs.xmid = xa, ca
        lat.xnext, cxs.xnext = xb, cb
        phase_M(l)
        if stop_phase == ("M", l):
            break
        phase_A(l)
        if stop_phase == ("A", l):
            break
        if "H" not in skip:
            for st in streams:
                phase_H(l, st)
        if stop_phase == ("H", l):
            break
        phase_B(l, streams)
        if stop_phase == ("B", l):
            break
        phase_C1(l, streams, TC1)
        for st in streams:
            st.x = st.xmid
        if stop_phase == ("C1", l):
            break
        phase_C2(l, streams, TC2, final=last)
        for st in streams:
            st.x = st.xnext
        if stop_phase == ("C2", l):
            break

    kb.barrier()
    return nc


SP_SPEC = [("norm1_w", 16), ("norm2_w", 16), ("b_mod", 96), ("final_norm_w", 8), ("q_norm_w", 2), ("k_norm_w", 2),
           ("hy_conv_w", 36), ("hy_conv_b", 12), ("pool_scale", 4), ("gm_bs", 8), ("hy_b1", 2), ("hy_freq", 2),
           ("hy_b2", 4), ("hy_decay", 16)]
SP_OFF = {}
_o = 0
for _n, _c in SP_SPEC:
    SP_OFF[_n] = (_o, _c)
    _o += _c
NSP = _o


def _pack_small(inp):
    sp = np.zeros((128, NSP), np.float32)

    def put(name, arr):
        o, n = SP_OFF[name]
        assert arr.shape == (128, n), (name, arr.shape, n)
        sp[:, o:o + n] = arr

    def fm(a, nch):
        L = a.shape[0]
        return a.reshape(L, nch, 128).transpose(2, 0, 1).reshape(128, L * nch)

    put("norm1_w", fm(inp["norm1_w"], 8))
    put("norm2_w", fm(inp["norm2_w"], 8))
    put("b_mod", fm(inp["b_mod"], 48))
    put("final_norm_w", inp["final_norm_w"].reshape(8, 128).T)
    put("q_norm_w", np.tile(inp["q_norm_w"].T, (2, 1)))
    put("k_norm_w", np.tile(inp["k_norm_w"].T, (2, 1)))
    put("hy_conv_w", inp["hy_conv_w"].reshape(DEPTH, 3, 6, 128).transpose(3, 0, 1, 2).reshape(128, DEPTH * 18))
    put("hy_conv_b", fm(inp["hy_conv_b"], 6))
    put("pool_scale", fm(inp["pool_scale"], 2))
    put("gm_bs", inp["gm_bs"].transpose(2, 0, 1).reshape(128, DEPTH * 4))
    h64 = lambda a: np.concatenate([a, np.zeros_like(a)], axis=0)
    put("hy_b1", h64(inp["hy_b1"].T))
    put("hy_freq", h64(inp["hy_freq"].T))
    put("hy_b2", h64(inp["hy_b2"].transpose(2, 0, 1).reshape(64, DEPTH * 2)))
    put("hy_decay", fm(inp["hy_decay"], 8))
    return sp


_CONST_CACHE = {}


def _consts():
    if _CONST_CACHE:
        return _CONST_CACHE
    cbf = np.zeros((128, 3, 128), np.float32)
    cbf[:, 0, :] = np.eye(128)
    k = np.arange(128)
    cbf[k, 1, k ^ 1] = 1.0
    cosF, sinS = _rope_tables()
    _CONST_CACHE.update(dict(
        cbf=_bf(cbf), ropeT=np.stack([cosF, sinS]).astype(np.float32),
        invcnt_l=_pool_invcnt(SEQ), invcnt_c=_pool_invcnt(CTX)))
    for nm, n in (("lat", SEQ), ("ctx", CTX)):
        zz, tr = _hy_tables(n)
        tf, tb = _dft_tables(2 * n // 128)
        _CONST_CACHE["zpos_" + nm] = zz
        _CONST_CACHE["trow_" + nm] = tr
        _CONST_CACHE["dftf_" + nm] = tf
        _CONST_CACHE["dftb_" + nm] = tb
    return _CONST_CACHE


def make_in_maps(inp):
    inp = {k: np.asarray(v) for k, v in inp.items()}
    shared = dict(_consts())
    for k in ("w_mod", "w_in", "w_br_attn", "w_br_hyena", "w_br_pool", "w_br_gmlp", "w_out", "ffn_w_gate",
              "ffn_w_up", "ffn_w_down", "gm_norm_w", "pool_w", "hy_w1", "hy_w2", "hy_w3", "hy_skip"):
        shared[k] = np.ascontiguousarray(inp[k], dtype=np.float32)
    shared["smallp"] = _pack_small(inp)
    shared["gm_wsT"] = np.ascontiguousarray(inp["gm_ws"].transpose(0, 1, 3, 2))
    shared["hy_skip"] = np.ascontiguousarray(inp["hy_skip"].reshape(DEPTH, 512), dtype=np.float32)
    maps = []
    for b in range(NCORE):
        m = dict(shared)
        m["xT"] = np.ascontiguousarray(inp["x"][b].T)
        m["ctxT"] = np.ascontiguousarray(inp["ctx"][b].T)
        cv = np.stack([inp["c"][b].reshape(8, 128).T, inp["c_ctx"].reshape(8, 128).T], axis=-1)
        m["cvec"] = np.ascontiguousarray(cv, dtype=np.float32)
        maps.append(m)
    return maps


def kernel(**inputs):
    nc = build_program()
    maps = make_in_maps(inputs)
    res = run_bass_kernel_spmd(nc, maps, core_ids=list(range(NCORE)))
    out = np.stack([np.ascontiguousarray(r["outT"].T) for r in res.results], axis=0)
    return out.astype(np.float32)
```

```python
import numpy as np
import ml_dtypes
from contextlib import ExitStack
import concourse.bass as bass
import concourse.mybir as mybir
from concourse.bass_utils import run_bass_kernel_spmd

F32 = mybir.dt.float32
BF16 = mybir.dt.bfloat16
AF = mybir.ActivationFunctionType
ALU = mybir.AluOpType

D = 1024
SEQ = 8192
CTX = 256
DEPTH = 2
NCORE = 8
HID = 2816
EPS = 1e-6
NKEY = CTX + SEQ
PADC = 8


class Res:
    __slots__ = ("name", "readers", "writers", "multi", "sem")

    def __init__(self, name, multi=False):
        self.name = name
        self.readers = {}
        self.writers = {}
        self.multi = multi
        self.sem = None


class V:
    __slots__ = ("ap", "res", "sb")

    def __init__(self, ap, res, sb=True):
        self.ap = ap
        self.res = res
        self.sb = sb

    def __getitem__(self, k):
        return V(self.ap[k], self.res, self.sb)

    def rearrange(self, s, **kw):
        return V(self.ap.rearrange(s, **kw), self.res, self.sb)

    def bcast(self, shape):
        return V(self.ap.broadcast_to(list(shape)), self.res, self.sb)

    def unsqueeze(self, a):
        return V(self.ap.unsqueeze(a), self.res, self.sb)

    def pbcast(self, n):
        return V(self.ap.partition_broadcast(n), self.res, self.sb)

    def wr(self, res):
        return V(self.ap, res, self.sb)


class KB:
    def __init__(self, nc):
        self.nc = nc
        self.eng = {"pe": nc.tensor, "act": nc.scalar, "dve": nc.vector, "pool": nc.gpsimd, "sp": nc.sync}
        self.psem = {}
        self.pcnt = {}
        self.root = ExitStack()
        for e in ("pe", "act", "dve", "pool"):
            self.psem[e] = self.root.enter_context(nc.semaphore(f"prog_{e}"))
            self.pcnt[e] = 0
        self.waited = {e: {} for e in self.eng}
        self.dma_all = []
        self.dma_free = []
        self.phase_res = []
        self.stacks = [self.root]
        self.uid = 0
        self.dram_res = {}

    def _name(self, n):
        self.uid += 1
        return f"{n}_{self.uid}"

    def tile(self, name, shape, dt, multi=False):
        t = self.stacks[-1].enter_context(self.nc.sbuf_tensor(self._name(name), list(shape), dt))
        r = Res(name, multi)
        self.phase_res.append(r)
        return V(t[tuple(slice(None) for _ in shape)], r, True)

    def ring(self, name, n, shape, dt, multi=False):
        return [self.tile(f"{name}{i}", shape, dt, multi) for i in range(n)]

    def psum(self, name, shape, dt=F32):
        t = self.stacks[-1].enter_context(self.nc.psum_tensor(self._name(name), list(shape), dt))
        r = Res(name)
        self.phase_res.append(r)
        return V(t[tuple(slice(None) for _ in shape)], r, True)

    def dram(self, name, shape, dt, kind="Internal"):
        t = self.nc.dram_tensor(name, list(shape), dt, kind=kind)
        return V(t.ap(), Res(name, True), False)

    def dres(self, v, key):
        k = (v.res.name, key)
        if k not in self.dram_res:
            self.dram_res[k] = Res(f"{v.res.name}:{key}", True)
        return v.wr(self.dram_res[k])

    def push(self):
        self.stacks.append(ExitStack())
        self.phase_res = []

    def pop(self):
        self.barrier()
        for r in self.phase_res:
            if r.sem is not None:
                if "hw" in r.sem:
                    self.dma_free.append(r.sem["hw"])
                r.sem = None
        self.phase_res = []
        self.stacks.pop().close()

    def _deps(self, outs, ins):
        d = {}

        def add(tokd):
            for k, (s, c) in tokd.items():
                if k not in d or d[k][1] < c:
                    d[k] = (s, c)

        for v in ins:
            add(v.res.writers)
        for v in outs:
            add(v.res.readers)
            if not v.res.multi:
                add(v.res.writers)
        return d

    def _wait(self, e, deps, skip_own=True):
        own = self.psem[e].name if (skip_own and e == "pe") else None
        w = self.waited[e]
        for k, (s, c) in deps.items():
            if k == own:
                continue
            if w.get(k, 0) >= c:
                continue
            self.eng[e].wait_ge(s, c)
            w[k] = c

    def _commit(self, outs, ins, key, tok):
        for v in ins:
            r = v.res.readers
            if key not in r or r[key][1] < tok[1]:
                r[key] = tok
        for v in outs:
            if v.res.multi:
                v.res.writers[key] = tok
                v.res.readers = {}
            else:
                v.res.writers = {key: tok}
                v.res.readers = {}

    def op(self, e, fn, outs, ins):
        outs = [o for o in outs if isinstance(o, V)]
        ins = [i for i in ins if isinstance(i, V)]
        self._wait(e, self._deps(outs, ins))
        inst = fn()
        self.pcnt[e] += 1
        inst.then_inc(self.psem[e], 1)
        self._commit(outs, ins, self.psem[e].name, (self.psem[e], self.pcnt[e]))
        return inst

    def dma(self, q, out, in_, extra_in=(), extra_out=()):
        sbv = out if out.sb else in_
        outs = [out] + list(extra_out)
        ins = [in_] + list(extra_in)
        self._wait(q, self._deps(outs, ins), skip_own=False)
        res = sbv.res
        qt = "sw" if q == "pool" else "hw"
        if res.sem is None:
            res.sem = {}
        if qt not in res.sem:
            if qt == "hw" and self.dma_free:
                res.sem[qt] = self.dma_free.pop()
            else:
                s = self.root.enter_context(self.nc.semaphore(self._name("dma" + qt)))
                res.sem[qt] = [s, 0]
                self.dma_all.append(res.sem[qt])
        sm = res.sem[qt]
        inst = self.eng[q].dma_start(out=out.ap, in_=in_.ap)
        sm[1] += 16
        inst.then_inc(sm[0], 16)
        self._commit(outs, ins, sm[0].name, (sm[0], sm[1]))
        return inst

    def barrier(self):
        toks = {}
        for e in self.psem:
            toks[self.psem[e].name] = (self.psem[e], self.pcnt[e])
        for s in self.dma_all:
            if s[1] > 0:
                toks[s[0].name] = (s[0], s[1])
        for e in self.eng:
            self._wait(e, toks, skip_own=True)

    @staticmethod
    def _a(x):
        return x.ap if isinstance(x, V) else x

    def mm(self, out, lhsT, rhs, start=True, stop=True):
        return self.op("pe", lambda: self.nc.tensor.matmul(out.ap, lhsT=lhsT.ap, rhs=rhs.ap, start=start, stop=stop),
                       [out], [lhsT, rhs])

    def transpose(self, out, in_, ident):
        return self.op("pe", lambda: self.nc.tensor.transpose(out.ap, in_.ap, ident.ap), [out], [in_, ident])

    def act(self, out, in_, func, bias=0.0, scale=1.0, accum=None):
        a = self._a
        kw = {}
        if accum is not None:
            kw["accum_out"] = accum.ap
        return self.op("act", lambda: self.nc.scalar.activation(out=out.ap, in_=in_.ap, func=func, bias=a(bias),
                                                                scale=a(scale), **kw),
                       [out, accum], [in_, bias, scale])

    def tt(self, e, out, in0, in1, op):
        return self.op(e, lambda: self.eng[e].tensor_tensor(out=out.ap, in0=in0.ap, in1=in1.ap, op=op),
                       [out], [in0, in1])

    def ts(self, e, out, in0, s1, op0, s2=None, op1=None):
        a = self._a
        if op1 is None:
            f = lambda: self.eng[e].tensor_scalar(out=out.ap, in0=in0.ap, scalar1=a(s1), scalar2=None, op0=op0)
        else:
            f = lambda: self.eng[e].tensor_scalar(out=out.ap, in0=in0.ap, scalar1=a(s1), scalar2=a(s2), op0=op0,
                                                  op1=op1)
        return self.op(e, f, [out], [in0, s1, s2])

    def stt(self, out, in0, scalar, in1, op0, op1):
        a = self._a
        return self.op("dve", lambda: self.nc.vector.scalar_tensor_tensor(out=out.ap, in0=in0.ap, scalar=a(scalar),
                                                                          in1=in1.ap, op0=op0, op1=op1),
                       [out], [in0, scalar, in1])

    def copy(self, e, out, in_):
        if e == "act":
            return self.op("act", lambda: self.nc.scalar.copy(out=out.ap, in_=in_.ap), [out], [in_])
        return self.op(e, lambda: self.eng[e].tensor_copy(out=out.ap, in_=in_.ap), [out], [in_])

    def memset(self, e, out, val):
        return self.op(e, lambda: self.eng[e].memset(out.ap, val), [out], [])

    def recip(self, out, in_):
        return self.op("dve", lambda: self.nc.vector.reciprocal(out=out.ap, in_=in_.ap), [out], [in_])


def _bf(a):
    return np.asarray(a, dtype=np.float32).astype(ml_dtypes.bfloat16)


def _rope_tables():
    rows = SEQ // 64
    r, col = np.meshgrid(np.arange(rows, dtype=np.float32), np.arange(64, dtype=np.float32), indexing="ij")
    inv = (np.float32(10000.0) ** (-np.arange(0, 32, 2, dtype=np.float32) / np.float32(32))).astype(np.float32)
    ang = np.concatenate([r.reshape(-1, 1) * inv, col.reshape(-1, 1) * inv], axis=-1).astype(np.float32)
    cos = np.cos(ang).astype(np.float32)
    sin = np.sin(ang).astype(np.float32)
    p = np.arange(128)
    i = (p % 64) // 2
    cosF = cos[:, i].T.copy()
    sgn = np.where(p % 2 == 0, -1.0, 1.0).astype(np.float32)
    sinS = (sin[:, i].T * sgn[:, None]).astype(np.float32)
    return np.ascontiguousarray(cosF), np.ascontiguousarray(sinS)


def _hy_pos(n):
    pos = np.arange(n, dtype=np.float32)
    t = (pos / np.float32(n - 1)).astype(np.float32)
    bands = np.linspace(1e-4, 15, 16, dtype=np.float32)
    ang = (np.float32(2.0 * np.pi / n) * pos[:, None] * bands).astype(np.float32)
    z = np.concatenate([t[:, None], np.cos(ang), np.sin(ang)], axis=-1).astype(np.float32)
    return z, t


def _pool_invcnt(n):
    out = np.zeros((128, 2, n), np.float32)
    pos = np.arange(n)
    for g, win in enumerate((2, 4, 8, 16)):
        lo = np.clip(pos - win // 2, 0, n)
        hi = np.clip(pos + win - win // 2, 0, n)
        ic = (1.0 / (hi - lo).astype(np.float32)).astype(np.float32)
        ch, half = g // 2, g % 2
        out[half * 64:(half + 1) * 64, ch, :] = ic[None, :]
    return out


def _hy_tables(n):
    z, t = _hy_pos(n)
    zf = z.T.copy()
    zr = np.empty_like(zf)
    zr[:, 0] = zf[:, 0]
    zr[:, 1:] = zf[:, :0:-1]
    tr = np.zeros((2, n), np.float32)
    tr[0] = t
    tr[1, 1:] = t[:0:-1]
    return np.stack([zf, zr]).astype(np.float32), tr


def _dft_tables(n1):
    N = n1 * 128
    nh = n1 // 2
    a = np.arange(128, dtype=np.float64)
    b = np.arange(n1, dtype=np.float64)
    C = np.cos(2 * np.pi * np.outer(a, a) / 128.0)
    S_ = np.sin(2 * np.pi * np.outer(a, a) / 128.0)
    C1 = np.cos(2 * np.pi * np.outer(b, b) / n1)
    S1 = np.sin(2 * np.pi * np.outer(b, b) / n1)
    twc = np.cos(2 * np.pi * np.outer(a, b) / N)
    tws = np.sin(2 * np.pi * np.outer(a, b) / N)
    FC = 4 * n1 + 512
    tf = np.zeros((128, FC), np.float32)
    tf[:, 0:n1] = twc
    tf[:, n1:2 * n1] = twc
    tf[:, 2 * n1:3 * n1] = tws
    tf[:, 3 * n1:4 * n1] = -tws
    o = 4 * n1
    tf[0:n1, o:o + 128] = twc.T
    tf[0:n1, o + 128:o + 256] = twc.T
    tf[0:n1, o + 256:o + 384] = tws.T
    tf[0:n1, o + 384:o + 512] = -tws.T
    BC = 2 * n1 + 128 * 3 + 512 + 2 * nh
    tb = np.zeros((128, BC), np.float32)
    tb[0:n1, 0:n1] = C1
    tb[0:n1, n1:2 * n1] = -S1
    o = 2 * n1
    tb[:, o:o + 128] = C
    tb[:, o + 128:o + 256] = S_
    tb[:, o + 256:o + 384] = -S_
    o += 384
    tb[:, o:o + 128] = C
    tb[:, o + 128:o + 256] = S_
    tb[:, o + 256:o + 384] = -S_
    tb[:, o + 384:o + 512] = C
    o += 512
    tb[0:n1, o:o + nh] = C1[:, 0:nh] / N
    tb[0:n1, o + nh:o + 2 * nh] = -S1[:, 0:nh] / N
    return tf, _bf(tb)


class Stream:
    pass


def build_program(dbg=None, nlayers=DEPTH, stop_phase=None, skip=()):
    nc = bass.Bass("TRN2", target_bir_lowering=False)
    kb = KB(nc)
    dbg = dbg or []

    def din(name, shape, dt=F32):
        return kb.dram(name, shape, dt, kind="ExternalInput")

    def dsc(name, shape, dt):
        return kb.dram(name, shape, dt, kind=("ExternalOutput" if name in dbg else "Internal"))

    xT_in = din("xT", [D, SEQ])
    ctxT_in = din("ctxT", [D, CTX])
    cvec = din("cvec", [128, 8, 2])
    w_mod = din("w_mod", [DEPTH, D, 6 * D])
    w_in = din("w_in", [DEPTH, D, 6400])
    w_br_attn = din("w_br_attn", [DEPTH, 512, D])
    w_br_hy = din("w_br_hyena", [DEPTH, 256, D])
    w_br_pool = din("w_br_pool", [DEPTH, 256, D])
    w_br_gm = din("w_br_gmlp", [DEPTH, 256, D])
    w_out = din("w_out", [DEPTH, D, D])
    ffn_g = din("ffn_w_gate", [DEPTH, D, HID])
    ffn_u = din("ffn_w_up", [DEPTH, D, HID])
    ffn_d = din("ffn_w_down", [DEPTH, HID, D])
    smallp_d = din("smallp", [128, NSP])
    gm_wsT = din("gm_wsT", [DEPTH, 4, 128, 128])
    gm_nw = din("gm_norm_w", [DEPTH, 256])
    pool_w = din("pool_w", [DEPTH, 4, 64, 64])
    hy_w1 = din("hy_w1", [DEPTH, 33, 64])
    hy_w2 = din("hy_w2", [DEPTH, 2, 64, 64])
    hy_w3 = din("hy_w3", [DEPTH, 64, 1024])
    hy_skip = din("hy_skip", [DEPTH, 512])
    cbf = din("cbf", [128, 3, 128], BF16)
    ropeT = din("ropeT", [2, 128, SEQ])
    invcnt_l = din("invcnt_l", [128, 2, SEQ])
    invcnt_c = din("invcnt_c", [128, 2, CTX])
    outT = kb.dram("outT", [D, SEQ], F32, kind="ExternalOutput")

    xa = dsc("xa", [D, SEQ], F32)
    xb = dsc("xb", [D, SEQ], F32)
    ca = dsc("ca", [D, CTX], F32)
    cb = dsc("cb", [D, CTX], F32)
    kT2d = dsc("kT2d", [128, 2, NKEY], BF16)
    vaugd = dsc("vaugd", [NKEY // 128, 128, 2, 66], BF16)

    def mk_stream(name, S, T, xin, koff):
        st = Stream()
        st.name, st.S, st.T, st.NT, st.koff = name, S, T, S // T, koff
        st.x = xin
        st.qT = dsc(f"qT_{name}", [128, 4, S], BF16)
        st.pT = dsc(f"pT_{name}", [D, S + 2 * PADC], BF16)
        st.zT = dsc(f"zT_{name}", [768, S], BF16)
        st.plT = dsc(f"plT_{name}", [256, S], BF16)
        st.gmT = dsc(f"gmT_{name}", [256, S], BF16)
        st.hyT = dsc(f"hyT_{name}", [256, S], BF16)
        st.attnT = dsc(f"attnT_{name}", [128, 4, S], BF16)
        st.rope = (name == "lat")
        st.si = 0 if name == "lat" else 1
        st.invcnt = invcnt_l if name == "lat" else invcnt_c
        n1 = 2 * S // 128
        st.zpos = din(f"zpos_{name}", [2, 33, S])
        st.trow = din(f"trow_{name}", [2, S])
        st.dftf = din(f"dftf_{name}", [128, 4 * n1 + 512])
        st.dftb = din(f"dftb_{name}", [128, 2 * n1 + 384 + 512 + n1], BF16)
        st.kfT = dsc(f"kfT_{name}", [512, 2 * S], BF16)
        return st

    lat = mk_stream("lat", SEQ, 512, xT_in, CTX)
    cxs = mk_stream("ctx", CTX, 256, ctxT_in, 0)

    identb = kb.tile("identb", [128, 128], BF16)
    permb = kb.tile("permb", [128, 128], BF16)
    ones1024 = kb.tile("ones1024", [128, 128], BF16)
    blk64 = kb.tile("blk64", [128, 128], BF16)
    onesf = kb.tile("onesf", [128, 64], F32)
    epst = kb.tile("epst", [128, 1], F32)
    smallp = kb.tile("smallp", [128, NSP], F32)
    modp = kb.tile("modp", [128, 2, 6, 8], F32)
    actbf = kb.tile("actbf", [128, 8, 2], BF16)
    zero_bf = kb.tile("zero_bf", [128, 8, PADC], BF16)

    kb.dma("sp", identb, cbf[:, 0, :])
    kb.dma("sp", permb, cbf[:, 1, :])
    kb.dma("sp", smallp, smallp_d)
    kb.memset("dve", ones1024, 1.0 / 1024.0)
    kb.memset("dve", blk64, 0.0)
    kb.memset("dve", blk64[0:64, 0:64], 1.0 / 64.0)
    kb.memset("dve", blk64[64:128, 64:128], 1.0 / 64.0)
    kb.memset("dve", onesf, 1.0)
    kb.memset("dve", epst, EPS)
    kb.memset("dve", zero_bf, 0.0)
    for st in (lat, cxs):
        pv = st.pT.rearrange("(c p) t -> p c t", p=128)
        kb.dma("sp", kb.dres(pv[:, :, 0:PADC], "padlo"), zero_bf)
        kb.dma("sp", kb.dres(pv[:, :, PADC + st.S:2 * PADC + st.S], "padhi"), zero_bf)
    kb.push()
    cv = kb.tile("cv", [128, 8, 2], F32)
    kb.dma("sp", cv, cvec)
    kb.act(actbf, cv, AF.Silu)
    kb.pop()

    def sp_col(name, l=None):
        o, n = SP_OFF[name]
        if l is not None:
            per = n // DEPTH
            return smallp[:, o + l * per:o + (l + 1) * per]
        return smallp[:, o:o + n]

    def phase_M(l):
        kb.push()
        wm = kb.ring("wm", 2, [128, 8, 1536], BF16)
        pm = kb.psum("pm", [128, 48, 2])
        modt = kb.tile("modt", [128, 48, 2], F32)
        wsrc = w_mod[l].rearrange("(kc p) n -> p kc n", p=128)
        for blk in range(4):
            w = wm[blk % 2]
            for kc in range(8):
                kb.dma("pool", w[:, kc, :], wsrc[:, kc, blk * 1536:(blk + 1) * 1536])
            for j in range(12):
                oc = blk * 12 + j
                for kc in range(8):
                    kb.mm(pm[:, oc, :], w[:, kc, j * 128:(j + 1) * 128], actbf[:, kc, :], start=(kc == 0),
                          stop=(kc == 7))
        bm = sp_col("b_mod", l)
        kb.tt("dve", modt, pm, bm.unsqueeze(2).bcast([128, 48, 2]), ALU.add)
        n1 = sp_col("norm1_w", l)
        n2 = sp_col("norm2_w", l)
        for s in range(2):
            kb.stt(modp[:, s, 0, :], modt[:, 8:16, s], 1.0, n1, ALU.add, ALU.mult)
            kb.copy("dve", modp[:, s, 1, :], modt[:, 0:8, s])
            kb.copy("dve", modp[:, s, 2, :], modt[:, 16:24, s])
            kb.stt(modp[:, s, 3, :], modt[:, 32:40, s], 1.0, n2, ALU.add, ALU.mult)
            kb.copy("dve", modp[:, s, 4, :], modt[:, 24:32, s])
            kb.copy("dve", modp[:, s, 5, :], modt[:, 40:48, s])
        kb.pop()

    def norm_mod(st, xt, sq, h, ps_ss, srt, rstd, which, TT):
        ai = 0 if which == 1 else 3
        kb.act(sq[:, :, :TT], xt[:, :, :TT], AF.Square)
        for c in range(8):
            kb.mm(ps_ss[:, :TT], ones1024, sq[:, c, :TT], start=(c == 0), stop=(c == 7))
        kb.act(srt[:, :TT], ps_ss[:, :TT], AF.Ln, bias=epst)
        kb.act(rstd[:, :TT], srt[:, :TT], AF.Exp, scale=-0.5)
        return ai

    def norm_apply(st, xt, xs, h, rstd, ai, TT):
        kb.tt("dve", xs[:, :, :TT], xt[:, :, :TT], rstd[:, :TT].unsqueeze(1).bcast([128, 8, TT]), ALU.mult)
        for c in range(8):
            kb.act(h[:, c, :TT], xs[:, c, :TT], AF.Identity, bias=modp[:, st.si, ai + 1, c:c + 1],
                   scale=modp[:, st.si, ai, c:c + 1])

    def phase_A(l):
        kb.push()
        wA = kb.tile("wA", [128, 8, 2432], BF16, multi=True)
        wsrc = w_in[l].rearrange("(kc p) n -> p kc n", p=128)
        for (d0, s0, n) in ((0, 0, 512), (512, 512, 64), (576, 512, 64), (640, 576, 64), (704, 576, 64),
                            (768, 768, 768), (1536, 1536, 256), (1792, 640, 128), (1920, 1792, 512)):
            for kc in range(8):
                kb.dma("pool", wA[:, kc, d0:d0 + n], wsrc[:, kc, s0:s0 + n])
        poolW = kb.tile("poolW", [128, 2, 128], BF16)
        kb.memset("dve", poolW, 0.0)
        for g in range(4):
            ch, hf = g // 2, g % 2
            kb.dma("pool", poolW[hf * 64:(hf + 1) * 64, ch, hf * 64:(hf + 1) * 64], pool_w[l, g])
        wsT = kb.tile("wsT", [128, 4, 128], BF16, multi=True)
        for g in range(4):
            kb.dma("pool", wsT[:, g, :], gm_wsT[l, g])
        gnw = kb.tile("gnw", [128, 256], F32)
        kb.dma("sp", gnw, gm_nw[l, :].pbcast(128))

        xt_r = kb.ring("xt", 1, [128, 8, 512], F32)
        sq = kb.tile("sq", [128, 8, 512], BF16)
        srt = kb.tile("srt", [128, 512], F32)
        rstd = kb.tile("rstd", [128, 512], F32)
        h_r = kb.ring("h", 2, [128, 8, 512], BF16)
        sqq_r = kb.ring("sqq", 2, [128, 512], BF16)
        srq_r = kb.ring("srq", 2, [128, 512], F32)
        rq_r = kb.ring("rq", 2, [128, 512], F32)
        qn_r = kb.ring("qn", 2, [128, 512], BF16)
        t1_r = kb.ring("t1", 2, [128, 512], F32)
        t2_r = kb.ring("t2", 2, [128, 512], F32)
        rope_r = kb.ring("rope", 2, [128, 2, 512], F32)
        qst_r = kb.ring("qst", 1, [128, 4, 512], BF16)
        kst_r = kb.ring("kst", 2, [128, 2, 512], BF16)
        vst_r = kb.ring("vst", 2, [128, 4, 2, 66], BF16)
        pst_r = kb.ring("pst", 1, [128, 8, 512], BF16)
        usb_r = kb.ring("usb", 2, [128, 256], F32)
        vnn = kb.tile("vnn", [128, 256], F32)
        vn_r = kb.ring("vn", 2, [128, 256], BF16)
        gmt_r = kb.ring("gmt", 2, [128, 256], BF16)
        stat = kb.tile("stat", [128, 8], F32)
        junk = kb.tile("junk", [128, 256], F32)
        gmst_r = kb.ring("gmst", 2, [128, 2, 512], BF16)
        pp_r = kb.ring("pp", 2, [128, 8, 512 + 2 * PADC], BF16)
        zst_r = kb.ring("zst", 1, [128, 6, 512], BF16)
        pa = [kb.tile(f"pa{i}", [128, 2, 512 + 2 * PADC], F32) for i in range(2)]
        icn_r = kb.ring("icn", 1, [128, 2, 512], F32)
        dpl = kb.tile("dpl", [128, 2, 512], F32)
        dpb = kb.tile("dpb", [128, 2, 512], BF16)
        plst_r = kb.ring("plst", 2, [128, 2, 512], BF16)

        ps_main = [kb.psum(f"ps_main{i}", [128, 512]) for i in range(2)]
        ps_qk = [kb.psum(f"ps_qk{i}", [128, 512]) for i in range(2)]
        ps_ss = ps_qk[0]
        ps_tok = [kb.psum(f"ps_tok{i}", [128, 512]) for i in range(2)]
        ps_gs = kb.psum("ps_gs", [128, 256])
        ps_gt_t = kb.stacks[-1].enter_context(nc.psum_tensor(kb._name("ps_gt"), [128, 2, 128], BF16))
        ps_gt = V(ps_gt_t[:, :, :], Res("ps_gt"), True)

        for r in vst_r:
            kb.memset("dve", r[:, :, :, 64:66], 1.0)

        wq = sp_col("q_norm_w", l)
        wk = sp_col("k_norm_w", l)
        hw = sp_col("hy_conv_w", l)
        hb = sp_col("hy_conv_b", l)
        pscale = sp_col("pool_scale", l)
        bsT = sp_col("gm_bs", l)

        cnt = {"main": 0, "qk": 0, "tok": 0, "h": 0}
        dg = kb.tile("dg", [128, 18, 128], BF16)
        for k in range(18):
            kb.ts("dve", dg[:, k, :], identb, hw[:, k:k + 1], ALU.mult)

        hcur = {}

        def prep_A(st, i):
            TT = st.T
            t0 = i * TT
            xt = xt_r[0]
            h = h_r[cnt["h"] % 2]
            cnt["h"] += 1
            xsrc = st.x.rearrange("(c p) t -> p c t", p=128)
            kb.dma("sp", xt[:, :, :TT], kb.dres(xsrc[:, :, t0:t0 + TT], i))
            rp = None
            if st.rope:
                rp = rope_r[i % 2]
                kb.dma("sp", rp[:, :, :TT], ropeT[:, :, t0:t0 + TT].rearrange("a p t -> p a t"))
            ai = norm_mod(st, xt, sq, h, ps_ss, srt, rstd, 1, TT)
            norm_apply(st, xt, xt, h, rstd, ai, TT)
            hcur[(st.name, i)] = (h, rp)

        def tile_A(st, i, full, mid_hook=None):
            TT = st.T
            t0 = i * TT
            h, rp = hcur.pop((st.name, i))
            qst = qst_r[0]
            kst = kst_r[i % 2]
            vst = vst_r[i % 2]
            pst = pst_r[0]
            gmst = gmst_r[i % 2]
            chunks = []
            if full:
                chunks += [("q", j, j * 128) for j in range(4)]
            chunks += [("k", 0, 512), ("k", 1, 640)]
            if full:
                chunks += [("p", j, 768 + j * 128) for j in range(8)]
            pss = {}

            def main_mm(ci):
                kind, idx, c0 = chunks[ci]
                ps = ps_main[cnt["main"] % 2]
                cnt["main"] += 1
                for kc in range(8):
                    kb.mm(ps[:, :TT], wA[:, kc, c0:c0 + 128], h[:, kc, :TT], start=(kc == 0), stop=(kc == 7))
                pss[ci] = ps

            def post(ci):
                kind, idx, c0 = chunks[ci]
                ps = pss.pop(ci)
                if kind == "p":
                    kb.copy("act", pst[:, idx, :TT], ps[:, :TT])
                    return
                k2 = cnt["qk"] % 2
                sqq = sqq_r[k2]
                qn = qn_r[k2]
                srq, rq, t1, t2 = srq_r[k2], rq_r[k2], t1_r[k2], t2_r[k2]
                psq = ps_qk[0]
                psw = ps_qk[1]
                cnt["qk"] += 1
                kb.act(sqq[:, :TT], ps[:, :TT], AF.Square)
                kb.mm(psq[:, :TT], blk64, sqq[:, :TT])
                kb.act(srq[:, :TT], psq[:, :TT], AF.Ln, bias=epst)
                kb.act(rq[:, :TT], srq[:, :TT], AF.Exp, scale=-0.5)
                dst = qst[:, idx, :TT] if kind == "q" else kst[:, idx, :TT]
                wn = wq if kind == "q" else wk
                if st.rope:
                    kb.stt(qn[:, :TT], ps[:, :TT], wn[:, 0:1], rq[:, :TT], ALU.mult, ALU.mult)
                    kb.mm(psw[:, :TT], permb, qn[:, :TT])
                    kb.tt("dve", t1[:, :TT], qn[:, :TT], rp[:, 0, :TT], ALU.mult)
                    kb.tt("dve", t2[:, :TT], psw[:, :TT], rp[:, 1, :TT], ALU.mult)
                    kb.tt("dve", dst, t1[:, :TT], t2[:, :TT], ALU.add)
                else:
                    kb.stt(dst, ps[:, :TT], wn[:, 0:1], rq[:, :TT], ALU.mult, ALU.mult)

            main_mm(0)
            for ci in range(len(chunks)):
                if mid_hook is not None and ci == min(8, len(chunks) - 1):
                    mid_hook()
                if ci + 1 < len(chunks):
                    main_mm(ci + 1)
                post(ci)
            nsub = TT // 128
            for sub in range(nsub):
                ts_ = slice(sub * 128, (sub + 1) * 128)
                ps = ps_tok[cnt["tok"] % 2]
                cnt["tok"] += 1
                for kc in range(8):
                    kb.mm(ps[:, 0:128], h[:, kc, ts_], wA[:, kc, 1792:1920], start=(kc == 0), stop=(kc == 7))
                kb.copy("act", vst[:, sub, :, 0:64], ps[:, 0:128].rearrange("p (k d) -> p k d", k=2))
                if not full:
                    continue
                ps = ps_tok[cnt["tok"] % 2]
                cnt["tok"] += 1
                for kc in range(8):
                    kb.mm(ps[:, 0:512], h[:, kc, ts_], wA[:, kc, 1920:2432], start=(kc == 0), stop=(kc == 7))
                usb = usb_r[sub % 2]
                vn = vn_r[sub % 2]
                gmt = gmt_r[sub % 2]
                kb.copy("act", usb, ps[:, 0:256])
                kb.memset("dve", stat[:, 0:2], 0.0)
                kb.act(junk, ps[:, 256:512], AF.Identity, accum=stat[:, 0:1])
                kb.act(junk, ps[:, 256:512], AF.Square, accum=stat[:, 1:2])
                kb.ts("dve", stat[:, 2:3], stat[:, 0:1], 1.0 / 256.0, ALU.mult)
                kb.tt("dve", stat[:, 3:4], stat[:, 2:3], stat[:, 2:3], ALU.mult)
                kb.stt(stat[:, 4:5], stat[:, 1:2], 1.0 / 256.0, stat[:, 3:4], ALU.mult, ALU.subtract)
                kb.act(stat[:, 5:6], stat[:, 4:5], AF.Ln, bias=epst)
                kb.act(stat[:, 6:7], stat[:, 5:6], AF.Exp, scale=-0.5)
                kb.stt(stat[:, 7:8], stat[:, 2:3], -1.0, stat[:, 6:7], ALU.mult, ALU.mult)
                kb.act(vnn, ps[:, 256:512], AF.Identity, bias=stat[:, 7:8], scale=stat[:, 6:7])
                kb.tt("dve", vn, vnn, gnw, ALU.mult)
                for g in range(4):
                    gs = slice(g * 64, (g + 1) * 64)
                    kb.mm(ps_gs[:, gs], wsT[:, g, :], vn[:, gs])
                for g in range(4):
                    gs = slice(g * 64, (g + 1) * 64)
                    kb.stt(gmt[:, gs], ps_gs[:, gs], bsT[:, g:g + 1], usb[:, gs], ALU.add, ALU.mult)
                for hf in range(2):
                    kb.transpose(ps_gt[:, hf, :], gmt[:, hf * 128:(hf + 1) * 128], identb)
                kb.copy("dve", gmst[:, :, ts_], ps_gt)
            if full:
                kb.dma("sp", kb.dres(st.qT[:, :, t0:t0 + TT], i), qst[:, :, :TT])
                pv = st.pT.rearrange("(c p) t -> p c t", p=128)
                kb.dma("sp", kb.dres(pv[:, :, PADC + t0:PADC + t0 + TT], i), pst[:, :, :TT])
                gv = st.gmT.rearrange("(c p) t -> p c t", p=128)
                kb.dma("sp", kb.dres(gv[:, :, t0:t0 + TT], i), gmst[:, :, :TT])
            kb.dma("sp", kb.dres(kT2d[:, :, st.koff + t0:st.koff + t0 + TT], (st.name, i)), kst[:, :, :TT])
            c0 = (st.koff + t0) // 128
            kb.dma("sp", kb.dres(vaugd[c0:c0 + nsub].rearrange("c p k d -> p c k d"), (st.name, i)),
                   vst[:, :nsub])

        def tile_A2(st, i):
            TT = st.T
            t0 = i * TT
            W = TT + 2 * PADC
            pp = pp_r[i % 2]
            icn = icn_r[0]
            zst = zst_r[0]
            plst = plst_r[i % 2]
            pv = st.pT.rearrange("(c p) t -> p c t", p=128)
            extra = [kb.dres(pv, k) for k in (i - 1, i + 1) if 0 <= k < st.NT]
            extra += [kb.dres(pv, "padlo"), kb.dres(pv, "padhi")]
            kb.dma("sp", pp[:, :, :W], kb.dres(pv[:, :, t0:t0 + W], i), extra_in=extra)
            kb.dma("sp", icn[:, :, :TT], st.invcnt[:, :, t0:t0 + TT])
            for c in range(6):
                ps = ps_main[cnt["main"] % 2]
                cnt["main"] += 1
                for tap in range(3):
                    kb.mm(ps[:, :TT], dg[:, tap * 6 + c, :], pp[:, c, PADC - 1 + tap:PADC - 1 + tap + TT],
                          start=(tap == 0), stop=(tap == 2))
                kb.act(zst[:, c, :TT], ps[:, :TT], AF.Identity, bias=hb[:, c:c + 1])
            zv = st.zT.rearrange("(c p) t -> p c t", p=128)
            kb.dma("sp", kb.dres(zv[:, :, t0:t0 + TT], i), zst[:, :, :TT])
            z = pp[:, 6:8, :]
            A_, B_ = pa[0], pa[1]
            kb.tt("dve", A_[:, :, 1:W], z[:, :, 1:W], z[:, :, 0:W - 1], ALU.add)
            kb.tt("dve", B_[:, :, 3:W], A_[:, :, 3:W], A_[:, :, 1:W - 2], ALU.add)

            def sel(hf, ch, arr, sh):
                ps_ = slice(hf * 64, (hf + 1) * 64)
                kb.tt("dve", dpl[ps_, ch, :TT], arr[ps_, ch, PADC + sh:PADC + sh + TT], icn[ps_, ch, :TT], ALU.mult)
                kb.tt("dve", dpb[ps_, ch, :TT], dpl[ps_, ch, :TT], pp[ps_, 6 + ch, PADC:PADC + TT], ALU.subtract)

            sel(0, 0, A_, 0)
            sel(1, 0, B_, 1)
            kb.tt("dve", A_[:, 1, 7:W], B_[:, 1, 7:W], B_[:, 1, 3:W - 4], ALU.add)
            kb.tt("dve", B_[:, 1, 15:W], A_[:, 1, 15:W], A_[:, 1, 7:W - 8], ALU.add)
            sel(0, 1, A_, 3)
            sel(1, 1, B_, 7)
            for ch in range(2):
                ps = ps_main[cnt["main"] % 2]
                cnt["main"] += 1
                kb.mm(ps[:, :TT], poolW[:, ch, :], dpb[:, ch, :TT])
                kb.act(plst[:, ch, :TT], ps[:, :TT], AF.Identity, scale=pscale[:, ch:ch + 1])
            plv = st.plT.rearrange("(c p) t -> p c t", p=128)
            kb.dma("sp", kb.dres(plv[:, :, t0:t0 + TT], i), plst[:, :, :TT])

        fullc = (l < DEPTH - 1)
        prep_A(cxs, 0)
        tile_A(cxs, 0, fullc, mid_hook=lambda: prep_A(lat, 0))
        if fullc:
            tile_A2(cxs, 0)
        for i in range(lat.NT):
            hook = (lambda j=i: prep_A(lat, j + 1)) if i + 1 < lat.NT else None
            tile_A(lat, i, True, mid_hook=hook)
            if i >= 1:
                tile_A2(lat, i - 1)
        tile_A2(lat, lat.NT - 1)
        kb.pop()

    def phase_H(l, st):
        S = st.S
        N1 = 2 * S // 128
        NH = N1 // 2
        PI = float(np.pi)
        kb.push()
        SLH = min(2048, S)
        SLW = min(512, S)
        NSL = S // SLW
        w1 = kb.tile("w1", [33, 64], F32)
        kb.dma("sp", w1, hy_w1[l])
        w2 = kb.tile("w2", [64, 2, 64], F32, multi=True)
        for i in range(2):
            kb.dma("sp", w2[:, i, :], hy_w2[l, i])
        w3 = kb.tile("w3", [64, 1024], F32)
        kb.dma("sp", w3, hy_w3[l])
        fr = sp_col("hy_freq", l)[0:64]
        b1 = sp_col("hy_b1", l)[0:64]
        b2 = sp_col("hy_b2", l)[0:64]
        fb = kb.tile("fb", [64, 3], F32)
        kb.tt("dve", fb[:, 0:1], fr, b1, ALU.mult)
        kb.tt("dve", fb[:, 1:3], b2, fr.bcast([64, 2]), ALU.mult)
        negdec = kb.tile("negdec", [128, 8], F32)
        kb.act(negdec, sp_col("hy_decay", l), AF.Abs)
        kb.ts("dve", negdec, negdec, -1.0, ALU.mult)
        hd3 = kb.tile("hd3", [64, 2, S], F32, multi=True)
        zt_r = kb.ring("zt", 2, [33, SLH], F32)
        ha = kb.tile("ha", [64, SLH], F32)
        m1 = kb.tile("m1", [64, SLH], F32)
        hb_r = kb.ring("hb", 2, [64, SLH], F32)
        ps_h = kb.psum("ps_h", [64, SLH])
        n = 0
        for d in range(2):
            for sl in range(S // SLH):
                cs = slice(sl * SLH, (sl + 1) * SLH)
                zt = zt_r[n % 2]
                n += 1
                kb.dma("sp", zt, st.zpos[d, :, cs])
                cur, Kc = zt, 33
                for layer in range(3):
                    lhsT = w1 if layer == 0 else w2[:, layer - 1, :]
                    for q in range(max(1, SLH // 512)):
                        qs = slice(q * 512, min((q + 1) * 512, SLH))
                        kb.mm(ps_h[:, qs], lhsT, cur[0:Kc, qs])
                    kb.act(ha, ps_h, AF.Identity, scale=fr, bias=fb[:, layer:layer + 1])
                    kb.ts("dve", m1, ha, PI, ALU.is_gt, -2.0 * PI, ALU.mult)
                    kb.tt("dve", ha, ha, m1, ALU.add)
                    kb.ts("dve", m1, ha, -PI, ALU.is_lt, 2.0 * PI, ALU.mult)
                    kb.tt("dve", ha, ha, m1, ALU.add)
                    dst = hd3[:, d, cs] if layer == 2 else hb_r[layer % 2]
                    kb.act(dst, ha, AF.Sin)
                    cur, Kc = dst, 64
        tr_r = kb.ring("tr", 2, [128, 2, SLW], F32)
        win_r = kb.ring("win", 2, [128, SLW], F32)
        f_r = kb.ring("f", 2, [128, SLW], F32)
        junk = kb.tile("junk", [128, SLW], F32)
        kst_r = kb.ring("kst", 2, [128, SLW], BF16)
        part = kb.tile("part", [128, 8, NSL], F32)
        nrm8 = kb.tile("nrm8", [128, 8], F32)
        rinv = kb.tile("rinv", [128, 4], F32)
        ps_f = [kb.psum(f"ps_f{i}", [128, 512]) for i in range(2)]
        kb.memset("dve", part, 0.0)
        n = 0
        for pas in (1, 2):
            for sl in range(NSL):
                cs = slice(sl * SLW, (sl + 1) * SLW)
                tr = tr_r[(pas * NSL + sl) % 2]
                kb.dma("sp", tr, st.trow[:, cs].pbcast(128))
                for oc in range(8):
                    o, d, wh = oc // 4, (oc % 4) // 2, oc % 2
                    ps = ps_f[n % 2]
                    win = win_r[n % 2]
                    f = f_r[n % 2]
                    kst = kst_r[n % 2]
                    n += 1
                    kb.mm(ps[:, :SLW], w3[:, oc * 128:(oc + 1) * 128], hd3[:, d, cs])
                    kb.act(win, tr[:, d, :], AF.Exp, scale=negdec[:, oc:oc + 1])
                    kb.tt("dve", f, ps[:, :SLW], win, ALU.mult)
                    if d == 1 and sl == 0:
                        kb.memset("dve", f[:, 0:1], 0.0)
                    if pas == 1:
                        kb.act(junk, f, AF.Abs, accum=part[:, oc, sl:sl + 1])
                    else:
                        kb.ts("dve", kst, f, rinv[:, 2 * o + wh:2 * o + wh + 1], ALU.mult)
                        r0 = (2 * o + wh) * 128
                        kb.dma("sp", st.kfT[r0:r0 + 128, d * S + sl * SLW:d * S + (sl + 1) * SLW], kst)
            if pas == 1:
                kb.op("dve", lambda: nc.vector.reduce_sum(out=nrm8.ap, in_=part.ap, axis=mybir.AxisListType.X),
                      [nrm8], [part])
                n4 = nrm8.rearrange("p (o d w) -> p o d w", o=2, d=2)
                kb.tt("dve", rinv.rearrange("p (o w) -> p o w", o=2), n4[:, :, 0, :], n4[:, :, 1, :], ALU.add)
                kb.recip(rinv, rinv)
        kb.pop()

        kb.push()
        tabf = kb.tile("tabf", [128, 4 * N1 + 512], F32)
        tabb = kb.tile("tabb", [128, 2 * N1 + 896 + N1], BF16)
        kb.dma("sp", tabf, st.dftf)
        kb.dma("sp", tabb, st.dftb)
        skipb = kb.tile("skipb", [128, 512], F32)
        kb.dma("sp", skipb[0:NH, :], hy_skip[l, :].pbcast(NH))
        TWc2 = tabf[:, 0:2 * N1].rearrange("p (h k) -> p h k", h=2).unsqueeze(1).bcast([128, 4, 2, N1])
        TWs = tabf[:, 2 * N1:3 * N1].unsqueeze(1).bcast([128, 4, N1])
        nTWs = tabf[:, 3 * N1:4 * N1].unsqueeze(1).bcast([128, 4, N1])
        o_ = 4 * N1
        iTWc2 = tabf[0:N1, o_:o_ + 256].rearrange("p (h k) -> p h k", h=2).unsqueeze(1).bcast([N1, 4, 2, 128])
        iTWs = tabf[0:N1, o_ + 256:o_ + 384].unsqueeze(1).bcast([N1, 4, 128])
        inTWs = tabf[0:N1, o_ + 384:o_ + 512].unsqueeze(1).bcast([N1, 4, 128])
        F1 = tabb[:, 0:2 * N1]
        o_ = 2 * N1
        Cm, Sm, nSm = tabb[:, o_:o_ + 128], tabb[:, o_ + 128:o_ + 256], tabb[:, o_ + 256:o_ + 384]
        CS, nSC = tabb[:, o_ + 384:o_ + 640], tabb[:, o_ + 640:o_ + 896]
        o_ += 896
        C1n, nS1n = tabb[0:N1, o_:o_ + NH], tabb[0:N1, o_ + NH:o_ + 2 * NH]
        Cg = 16
        NG = 256 // Cg
        vin_r = kb.ring("vin", 2, [128, Cg, 128], BF16)
        x1_r = kb.ring("x1in", 2, [128, Cg, 128], BF16)
        x2_r = kb.ring("x2in", 2, [128, Cg, 128], BF16)
        kf_r = [kb.ring(f"kf{o}", 2, [128, Cg, 128], BF16) for o in range(2)]
        hyst_r = kb.ring("hyst", 2, [128, Cg, 128], BF16)
        def mk_tmp(i):
            t = {}
            t["X1"] = kb.tile(f"X1_{i}", [128, 4, 2, 128], F32)
            t["X2"] = kb.tile(f"X2_{i}", [128, 4, 2, 128], F32)
            t["Ap"] = kb.tile(f"Ap_{i}", [128, 4, 2, 128], BF16)
            t["Kf"] = kb.tile(f"Kf_{i}", [128, 4, 2, 128], F32)
            t["T"] = [kb.tile(f"T{j}_{i}", [128, 4, 128], F32) for j in range(4)]
            t["Y"] = kb.tile(f"Y_{i}", [128, 4, 2, 128], BF16)
            t["XB1"] = kb.tile(f"XB1_{i}", [128, 4, 2, 128], F32)
            t["XB2"] = kb.tile(f"XB2_{i}", [128, 4, 2, 128], F32)
            t["Bp"] = kb.tile(f"Bp_{i}", [128, 4, 2, 128], BF16)
            t["tp"] = kb.tile(f"tp_{i}", [128, 4, 128], F32)
            t["u2"] = kb.tile(f"u2_{i}", [128, 4, 128], BF16)
            return t

        tmps = [mk_tmp(0), mk_tmp(1)]
        psA = kb.psum("psA", [128, 4, 256])
        psUr = kb.psum("psUr", [128, 512])
        psUi = kb.psum("psUi", [128, 512])
        psB = kb.psum("psB", [128, 4, 256])
        psY = kb.psum("psY", [128, 512])
        ur = psUr[:, 0:4 * N1].rearrange("p (c k) -> p c k", c=4)
        ui = psUi[:, 0:4 * N1].rearrange("p (c k) -> p c k", c=4)
        yv = psY[0:NH, 0:512].rearrange("p (c k) -> p c k", c=4)

        def fwd_fft(src, K, t):
            X1, X2, Ap = t["X1"], t["X2"], t["Ap"]
            yield ("W", "A")
            for c in range(4):
                kb.mm(psA[:, c, 0:2 * N1], src[0:K, c, :], F1[0:K, :])
            yield ("R", "A")
            pA4 = psA[:, :, 0:2 * N1].rearrange("p c (h k) -> p c h k", h=2)
            kb.tt("dve", X1[:, :, :, 0:N1], pA4, TWc2, ALU.mult)
            kb.tt("dve", X2[:, :, 0, 0:N1], pA4[:, :, 1, :], TWs, ALU.mult)
            kb.tt("dve", X2[:, :, 1, 0:N1], pA4[:, :, 0, :], nTWs, ALU.mult)
            kb.tt("dve", Ap[:, :, :, 0:N1], X1[:, :, :, 0:N1], X2[:, :, :, 0:N1], ALU.add)
            yield ("W", "U")
            rr, ri = Ap[:, :, 0, 0:N1], Ap[:, :, 1, 0:N1]
            kb.mm(ur, Cm, rr, start=True, stop=False)
            kb.mm(ur, Sm, ri, start=False, stop=True)
            kb.mm(ui, Cm, ri, start=True, stop=False)
            kb.mm(ui, nSm, rr, start=False, stop=True)

        def conv_block(o, kfblk, ublk, gateblk, dst, sk, t):
            Kf, Y, Bp, XB1, XB2, tp = t["Kf"], t["Y"], t["Bp"], t["XB1"], t["XB2"], t["tp"]
            yield from fwd_fft(kfblk, N1, t)
            yield ("R", "U")
            kb.copy("act", Kf[:, :, 0, 0:N1], ur)
            kb.copy("act", Kf[:, :, 1, 0:N1], ui)
            yield from fwd_fft(ublk, NH, t)
            yield ("R", "U")
            Kr, Ki = Kf[:, :, 0, 0:N1], Kf[:, :, 1, 0:N1]
            tt_ = [x[:, :, 0:N1] for x in t["T"]]
            kb.tt("dve", tt_[0], ur, Kr, ALU.mult)
            kb.tt("dve", tt_[1], ui, Ki, ALU.mult)
            kb.tt("dve", tt_[2], ur, Ki, ALU.mult)
            kb.tt("dve", tt_[3], ui, Kr, ALU.mult)
            kb.tt("dve", Y[:, :, 0, 0:N1], tt_[0], tt_[1], ALU.subtract)
            kb.tt("dve", Y[:, :, 1, 0:N1], tt_[2], tt_[3], ALU.add)
            yield ("W", "B")
            for c in range(4):
                kb.mm(psB[0:N1, c, :], Y[:, c, 0, 0:N1], CS, start=True, stop=False)
                kb.mm(psB[0:N1, c, :], Y[:, c, 1, 0:N1], nSC, start=False, stop=True)
            yield ("R", "B")
            pB4 = psB[0:N1].rearrange("p c (h k) -> p c h k", h=2)
            kb.tt("dve", XB1[0:N1], pB4, iTWc2, ALU.mult)
            kb.tt("dve", XB2[0:N1, :, 0, :], pB4[:, :, 1, :], inTWs, ALU.mult)
            kb.tt("dve", XB2[0:N1, :, 1, :], pB4[:, :, 0, :], iTWs, ALU.mult)
            kb.tt("dve", Bp[0:N1], XB1[0:N1], XB2[0:N1], ALU.add)
            yield ("W", "Y")
            kb.mm(yv, C1n, Bp[0:N1, :, 0, :], start=True, stop=False)
            kb.mm(yv, nS1n, Bp[0:N1, :, 1, :], start=False, stop=True)
            yield ("R", "Y")
            kb.tt("dve", tp[0:NH], ublk[0:NH], sk.unsqueeze(2).bcast([NH, 4, 128]), ALU.mult)
            kb.tt("dve", tp[0:NH], tp[0:NH], yv, ALU.add)
            kb.tt("dve", dst, tp[0:NH], gateblk[0:NH], ALU.mult)

        def fftv(src_rows, c0, nrow):
            return src_rows[c0:c0 + Cg, :].rearrange("c (a b) -> a c b", b=128)

        gt = {}

        def load_group(g):
            c0 = g * Cg
            vin, x1in, x2in = vin_r[g % 2], x1_r[g % 2], x2_r[g % 2]
            kfs = [kf_r[0][g % 2], kf_r[1][g % 2]]
            kb.dma("sp", vin[0:NH], fftv(st.zT[0:256], c0, NH))
            kb.dma("sp", x1in[0:NH], fftv(st.zT[256:512], c0, NH))
            kb.dma("sp", x2in[0:NH], fftv(st.zT[512:768], c0, NH))
            for o in range(2):
                kb.dma("sp", kfs[o][0:N1], fftv(st.kfT[o * 256:(o + 1) * 256], c0, N1))
            gt[g] = (vin, x1in, x2in, kfs, hyst_r[g % 2])

        def chain(g, b, t):
            vin, x1in, x2in, kfs, hyst = gt[g]
            c0 = g * Cg
            bs = slice(4 * b, 4 * b + 4)
            u2 = t["u2"]
            cc = c0 + 4 * b
            yield from conv_block(0, kfs[0][:, bs, :], vin[:, bs, :], x1in[:, bs, :], u2[0:NH],
                                  skipb[0:NH, cc:cc + 4], t)
            yield from conv_block(1, kfs[1][:, bs, :], u2, x2in[:, bs, :], hyst[0:NH, bs, :],
                                  skipb[0:NH, 256 + cc:256 + cc + 4], t)
            gdone[g] += 1
            if gdone[g] == Cg // 4:
                kb.dma("sp", fftv(st.hyT, c0, NH), hyst[0:NH])
                if g + 2 < NG:
                    load_group(g + 2)

        NCHAIN = 2
        gdone = {g: 0 for g in range(NG)}
        load_group(0)
        if NG > 1:
            load_group(1)
        todo = [(g, b) for g in range(NG) for b in range(Cg // 4)]
        active = []
        held = {}
        nstart = 0
        while todo or active:
            while todo and len(active) < NCHAIN:
                g, b = todo.pop(0)
                ch = {"id": nstart, "gen": chain(g, b, tmps[nstart % 2]), "done": False}
                nstart += 1
                ch["tok"] = next(ch["gen"])
                active.append(ch)
            progressed = False
            for ch in list(active):
                kind, tile = ch["tok"]
                if kind == "W":
                    if held.get(tile) not in (None, ch["id"]):
                        continue
                    held[tile] = ch["id"]
                try:
                    ch["tok"] = next(ch["gen"])
                except StopIteration:
                    ch["done"] = True
                if kind == "R":
                    held[tile] = None
                progressed = True
                if ch["done"]:
                    active.remove(ch)
            assert progressed, "hyena chain interleave deadlock"
        kb.pop()

    def phase_B(l, streams):
        kb.push()
        kT = kb.tile("kT", [128, 2, NKEY], BF16, multi=True)
        va = kb.tile("va", [128, NKEY // 128, 2, 66], BF16, multi=True)
        nld = 6
        per = (NKEY // 128) // nld
        for i in range(nld):
            kb.dma("sp", kT[:, :, i * per * 128:(i + 1) * per * 128], kT2d[:, :, i * per * 128:(i + 1) * per * 128])
            kb.dma("sp", va[:, i * per:(i + 1) * per], vaugd[i * per:(i + 1) * per].rearrange("c p k d -> p c k d"))
        qt_r = kb.ring("qz", 2, [128, 8, 512], BF16)
        for q_ in qt_r:
            kb.memset("dve", q_, 0.0)
        pt_r = kb.ring("pt", 3, [128, 2, 512], BF16)
        ost_r = kb.ring("ost", 2, [128, 4, 512], BF16)
        osb_r = kb.ring("osb", 2, [64, 512], F32)
        otmp = kb.tile("otmp", [64, 512], BF16)
        rden_r = kb.ring("rden", 2, [128, 512], F32)
        ps_s = [kb.psum(f"ps_s{i}", [128, 2, 512]) for i in range(2)]
        ps_o = [kb.psum(f"ps_o{i}", [128, 512]) for i in range(2)]
        ps_bc = kb.psum("ps_bc", [64, 512])

        items = []
        for st in streams:
            nk = CTX if st.name == "ctx" else NKEY
            pairs = nk // 256
            for i in range(st.NT):
                for hd in range(8):
                    for cp in range(pairs):
                        items.append((st, i, hd, cp, pairs))
        qcur = {}
        tord = {}
        for (st_, i_, _h, _c, _p) in items:
            tord.setdefault((st_.name, i_), len(tord))

        def get_q(st, i):
            key = (st.name, i)
            if key not in qcur:
                qt = qt_r[tord[key] % 2]
                for hf_ in range(2):
                    P_ = slice(hf_ * 64, hf_ * 64 + 64)
                    kb.dma("sp", qt[P_, hf_:8:2, :st.T], st.qT[P_, :, i * st.T:(i + 1) * st.T])
                qcur[key] = qt
            return qcur[key]

        def emit_qk(n):
            st, i, hd, cp, pairs = items[n]
            qt = get_q(st, i)
            j, hf, kv = hd // 2, hd % 2, hd // 4
            P = slice(hf * 64, hf * 64 + 64)
            pss = ps_s[n % 2]
            for u in range(2):
                ch = 2 * cp + u
                kb.mm(pss[:, u, :st.T], kT[:, kv, ch * 128:(ch + 1) * 128], qt[:, hd, :st.T])

        pending = []

        def finalize2(st, i, hd, po, rden, osb, ost):
            NQ = st.T
            kb.mm(ps_bc[:, :NQ], onesf[64:65, 0:64], rden[64:65, :NQ])
            if hd % 2 == 0:
                kb.tt("dve", ost[0:64, hd // 2, :NQ], osb[:, :NQ], ps_bc[:, :NQ], ALU.mult)
            else:
                kb.tt("dve", otmp[0:64, :NQ], osb[:, :NQ], ps_bc[:, :NQ], ALU.mult)
                kb.copy("dve", ost[64:128, hd // 2, :NQ], otmp[0:64, :NQ])
            if hd == 7:
                kb.dma("sp", st.attnT[:, :, i * NQ:(i + 1) * NQ], ost[:, :, :NQ])

        emit_qk(0)
        hcount = 0
        for n in range(len(items)):
            st, i, hd, cp, pairs = items[n]
            NQ = st.T
            kv = hd // 4
            if n + 1 < len(items):
                emit_qk(n + 1)
            pt = pt_r[n % 3]
            kb.act(pt[:, :, :NQ], ps_s[n % 2][:, :, :NQ], AF.Exp, scale=0.125)
            if cp == 0:
                po = ps_o[hcount % 2]
            for u in range(2):
                ch = 2 * cp + u
                kb.mm(po[0:65, :NQ], va[:, ch, kv, 0:65], pt[:, u, :NQ], start=(cp == 0 and u == 0),
                      stop=(cp == pairs - 1 and u == 1))
            while pending:
                finalize2(*pending.pop(0))
            if cp == pairs - 1:
                rden = rden_r[hcount % 2]
                osb = osb_r[hcount % 2]
                ost = ost_r[tord[(st.name, i)] % 2]
                kb.recip(rden[64:65, :NQ], po[64:65, :NQ])
                kb.copy("act", osb[:, :NQ], po[0:64, :NQ])
                pending.append((st, i, hd, po, rden, osb, ost))
                hcount += 1
        while pending:
            finalize2(*pending.pop(0))
        kb.pop()

    def prep_C(st, i, TT, which, xt, sq, srt, rstd, h, ps_ss):
        t0 = i * TT
        xsrc = st.x.rearrange("(c p) t -> p c t", p=128)
        kb.dma("sp", xt[:, :, :TT], xsrc[:, :, t0:t0 + TT])
        ai = norm_mod(st, xt, sq, h, ps_ss, srt, rstd, which, TT)
        norm_apply(st, xt, xt, h, rstd, ai, TT)

    def phase_C1(l, streams, TT):
        kb.push()
        wG = kb.tile("wG", [128, 8, 4096], BF16, multi=True)
        wBA = kb.tile("wBA", [128, 4, 1024], BF16, multi=True)
        wBH = kb.tile("wBH", [128, 2, 1024], BF16, multi=True)
        wBP = kb.tile("wBP", [128, 2, 1024], BF16, multi=True)
        wBG = kb.tile("wBG", [128, 2, 1024], BF16, multi=True)
        wO = kb.tile("wO", [128, 8, 1024], BF16, multi=True)
        wsrc = w_in[l].rearrange("(kc p) n -> p kc n", p=128)
        for kc in range(8):
            kb.dma("pool", wG[:, kc, :], wsrc[:, kc, 2304:6400])
        for c in range(4):
            kb.dma("pool", wBA[:, c, :], w_br_attn[l, c * 128:(c + 1) * 128, :])
        for (wt, src) in ((wBH, w_br_hy), (wBP, w_br_pool), (wBG, w_br_gm)):
            for c in range(2):
                kb.dma("pool", wt[:, c, :], src[l, c * 128:(c + 1) * 128, :])
        for kc in range(8):
            kb.dma("pool", wO[:, kc, :], w_out[l, kc * 128:(kc + 1) * 128, :])
        xt = kb.tile("xt", [128, 8, TT], F32)
        sq = kb.tile("sq", [128, 8, TT], BF16)
        srt = kb.tile("srt", [128, TT], F32)
        rstd = kb.tile("rstd", [128, TT], F32)
        h_r = kb.ring("h", 2, [128, 8, TT], BF16)
        at = kb.tile("at", [128, 4, TT], BF16)
        hyt_r = kb.ring("hyt", 2, [128, 2, TT], BF16)
        plt_r = kb.ring("plt", 2, [128, 2, TT], BF16)
        gmt_r = kb.ring("gmt", 2, [128, 2, TT], BF16)
        g_r = kb.ring("g", 2, [128, TT], BF16)
        tmp_r = kb.ring("tmp", 2, [128, TT], F32)
        acc = kb.tile("acc", [128, TT], F32)
        mg = kb.tile("mg", [128, 8, TT], BF16)
        xr_r = kb.ring("xr", 2, [128, TT], F32)
        xo_r = kb.ring("xo", 2, [128, TT], F32)
        ps_g = [kb.psum(f"ps_g{i}", [128, 512]) for i in range(2)]
        ps_b = [kb.psum(f"ps_b{i}", [128, 512]) for i in range(2)]
        ps_o = [kb.psum(f"ps_o{i}", [128, 512]) for i in range(2)]
        ps_ss = kb.psum("ps_ss", [128, 512])
        tiles = [(st, i) for st in streams for i in range(st.S // min(TT, st.S))]
        cnt = {"h": 0, "n": 0, "o": 0}
        hmap = {}

        def prep(k):
            st, i = tiles[k]
            T_ = min(TT, st.S)
            h = h_r[cnt["h"] % 2]
            cnt["h"] += 1
            prep_C(st, i, T_, 1, xt, sq, srt, rstd, h, ps_ss)
            hmap[k] = h

        prep(0)
        for k, (st, i) in enumerate(tiles):
            T_ = min(TT, st.S)
            t0 = i * T_
            h = hmap.pop(k)
            hyt, plt, gmt = hyt_r[k % 2], plt_r[k % 2], gmt_r[k % 2]
            for (dst, src) in ((hyt, st.hyT), (plt, st.plT), (gmt, st.gmT)):
                kb.dma("sp", dst[:, :, :T_], src.rearrange("(c p) t -> p c t", p=128)[:, :, t0:t0 + T_])
            kb.dma("sp", at[:, :, :T_], st.attnT[:, :, t0:t0 + T_])
            branches = ((1, wBH, hyt, 2, 128), (2, wBP, plt, 2, 128), (3, wBG, gmt, 2, 128), (0, wBA, at, 4, 128))
            for m in range(8):
                if m == 4 and k + 1 < len(tiles):
                    prep(k + 1)
                ms = slice(m * 128, (m + 1) * 128)
                for bi, (gi, wB, src, nk, kp) in enumerate(branches):
                    pg = ps_g[cnt["n"] % 2]
                    pb = ps_b[cnt["n"] % 2]
                    g = g_r[cnt["n"] % 2]
                    tmp = tmp_r[cnt["n"] % 2]
                    cnt["n"] += 1
                    for kc in range(8):
                        kb.mm(pg[:, :T_], wG[:, kc, gi * 1024 + m * 128:gi * 1024 + (m + 1) * 128], h[:, kc, :T_],
                              start=(kc == 0), stop=(kc == 7))
                    for c in range(nk):
                        kb.mm(pb[:, :T_], wB[0:kp, c, ms], src[0:kp, c, :T_], start=(c == 0), stop=(c == nk - 1))
                    kb.act(g[:, :T_], pg[:, :T_], AF.Sigmoid)
                    if bi == 0:
                        kb.tt("dve", acc[:, :T_], g[:, :T_], pb[:, :T_], ALU.mult)
                    else:
                        kb.tt("dve", tmp[:, :T_], g[:, :T_], pb[:, :T_], ALU.mult)
                        dst = acc[:, :T_] if bi < 3 else mg[:, m, :T_]
                        kb.tt("dve", dst, acc[:, :T_], tmp[:, :T_], ALU.add)
            xsrc = st.x.rearrange("(c p) t -> p c t", p=128)
            xdst = st.xmid.rearrange("(c p) t -> p c t", p=128)
            for m in range(8):
                po = ps_o[cnt["o"] % 2]
                xr = xr_r[cnt["o"] % 2]
                xo = xo_r[cnt["o"] % 2]
                cnt["o"] += 1
                kb.dma("sp", xr[:, :T_], xsrc[:, m, t0:t0 + T_])
                for kc in range(8):
                    kb.mm(po[:, :T_], wO[:, kc, m * 128:(m + 1) * 128], mg[:, kc, :T_], start=(kc == 0), stop=(kc == 7))
                kb.stt(xo[:, :T_], po[:, :T_], modp[:, st.si, 2, m:m + 1], xr[:, :T_], ALU.mult, ALU.add)
                kb.dma("sp", xdst[:, m, t0:t0 + T_], xo[:, :T_])
        kb.pop()

    def phase_C2(l, streams, TT, final):
        kb.push()
        wg = kb.tile("wg", [128, 8, HID], BF16, multi=True)
        wu = kb.tile("wu", [128, 8, HID], BF16, multi=True)
        wd = kb.tile("wd", [128, 22, D], BF16, multi=True)
        for kc in range(8):
            kb.dma("pool", wg[:, kc, :], ffn_g[l, kc * 128:(kc + 1) * 128, :])
            kb.dma("pool", wu[:, kc, :], ffn_u[l, kc * 128:(kc + 1) * 128, :])
        for j in range(22):
            kb.dma("pool", wd[:, j, :], ffn_d[l, j * 128:(j + 1) * 128, :])
        xt = kb.tile("xt", [128, 8, TT], F32)
        sq = kb.tile("sq", [128, 8, TT], BF16)
        srt = kb.tile("srt", [128, TT], F32)
        rstd = kb.tile("rstd", [128, TT], F32)
        h_r = kb.ring("h", 2, [128, 8, TT], BF16)
        a_t = kb.tile("a", [128, 22, TT], BF16)
        sl_r = kb.ring("sl", 2, [128, TT], F32)
        xr_r = kb.ring("xr", 2, [128, TT], F32)
        xo_r = kb.ring("xo", 2, [128, TT], F32)
        if final:
            xf = kb.tile("xf", [128, 8, TT], F32)
            fsq = kb.tile("fsq", [128, 8, TT], BF16)
            fnw = sp_col("final_norm_w")
        ps_g = [kb.psum(f"ps_g{i}", [128, 512]) for i in range(2)]
        ps_u = [kb.psum(f"ps_u{i}", [128, 512]) for i in range(2)]
        ps_o = [kb.psum(f"ps_o{i}", [128, 512]) for i in range(2)]
        ps_ss = kb.psum("ps_ss", [128, 512])
        tiles = [(st, i) for st in streams for i in range(st.S // min(TT, st.S))]
        cnt = {"h": 0, "n": 0, "o": 0}
        hmap = {}

        def prep(k):
            st, i = tiles[k]
            T_ = min(TT, st.S)
            h = h_r[cnt["h"] % 2]
            cnt["h"] += 1
            prep_C(st, i, T_, 2, xt, sq, srt, rstd, h, ps_ss)
            hmap[k] = h

        prep(0)
        for k, (st, i) in enumerate(tiles):
            T_ = min(TT, st.S)
            t0 = i * T_
            h = hmap.pop(k)
            for j in range(22):
                if j == 12 and k + 1 < len(tiles):
                    prep(k + 1)
                pg = ps_g[cnt["n"] % 2]
                pu = ps_u[cnt["n"] % 2]
                sl = sl_r[cnt["n"] % 2]
                cnt["n"] += 1
                for kc in range(8):
                    kb.mm(pg[:, :T_], wg[:, kc, j * 128:(j + 1) * 128], h[:, kc, :T_], start=(kc == 0), stop=(kc == 7))
                for kc in range(8):
                    kb.mm(pu[:, :T_], wu[:, kc, j * 128:(j + 1) * 128], h[:, kc, :T_], start=(kc == 0), stop=(kc == 7))
                kb.act(sl[:, :T_], pg[:, :T_], AF.Silu)
                kb.tt("dve", a_t[:, j, :T_], sl[:, :T_], pu[:, :T_], ALU.mult)
            xsrc = st.x.rearrange("(c p) t -> p c t", p=128)
            xdst = st.xnext.rearrange("(c p) t -> p c t", p=128)
            dofinal = final and st.name == "lat"
            for m in range(8):
                po = ps_o[cnt["o"] % 2]
                xr = xr_r[cnt["o"] % 2]
                xo = xo_r[cnt["o"] % 2]
                cnt["o"] += 1
                kb.dma("sp", xr[:, :T_], xsrc[:, m, t0:t0 + T_])
                for j in range(22):
                    kb.mm(po[:, :T_], wd[:, j, m * 128:(m + 1) * 128], a_t[:, j, :T_], start=(j == 0), stop=(j == 21))
                if dofinal:
                    kb.stt(xf[:, m, :T_], po[:, :T_], modp[:, st.si, 5, m:m + 1], xr[:, :T_], ALU.mult, ALU.add)
                else:
                    kb.stt(xo[:, :T_], po[:, :T_], modp[:, st.si, 5, m:m + 1], xr[:, :T_], ALU.mult, ALU.add)
                    kb.dma("sp", xdst[:, m, t0:t0 + T_], xo[:, :T_])
            if dofinal:
                kb.act(fsq[:, :, :T_], xf[:, :, :T_], AF.Square)
                for c in range(8):
                    kb.mm(ps_ss[:, :T_], ones1024, fsq[:, c, :T_], start=(c == 0), stop=(c == 7))
                kb.act(srt[:, :T_], ps_ss[:, :T_], AF.Ln, bias=epst)
                kb.act(rstd[:, :T_], srt[:, :T_], AF.Exp, scale=-0.5)
                kb.tt("dve", xf[:, :, :T_], xf[:, :, :T_], rstd[:, :T_].unsqueeze(1).bcast([128, 8, T_]), ALU.mult)
                odst = outT.rearrange("(c p) t -> p c t", p=128)
                for c in range(8):
                    xo = xo_r[cnt["o"] % 2]
                    cnt["o"] += 1
                    kb.act(xo[:, :T_], xf[:, c, :T_], AF.Identity, scale=fnw[:, c:c + 1])
                    kb.dma("sp", odst[:, c, t0:t0 + T_], xo[:, :T_])
        kb.pop()

    TC1 = 512
    TC2 = 256
    for l in range(nlayers):
        last = (l == DEPTH - 1)
        streams = [lat] if last else [cxs, lat]
        lat.xmid, cxs.xmid = xa, ca
        lat.xnext, cxs.xnext = xb, cb
        phase_M(l)
        if stop_phase == ("M", l):
            break
        phase_A(l)
        if stop_phase == ("A", l):
            break
        if "H" not in skip:
            for st in streams:
                phase_H(l, st)
        if stop_phase == ("H", l):
            break
        phase_B(l, streams)
        if stop_phase == ("B", l):
            break
        phase_C1(l, streams, TC1)
        for st in streams:
            st.x = st.xmid
        if stop_phase == ("C1", l):
            break
        phase_C2(l, streams, TC2, final=last)
        for st in streams:
            st.x = st.xnext
        if stop_phase == ("C2", l):
            break

    kb.barrier()
    return nc


SP_SPEC = [("norm1_w", 16), ("norm2_w", 16), ("b_mod", 96), ("final_norm_w", 8), ("q_norm_w", 2), ("k_norm_w", 2),
           ("hy_conv_w", 36), ("hy_conv_b", 12), ("pool_scale", 4), ("gm_bs", 8), ("hy_b1", 2), ("hy_freq", 2),
           ("hy_b2", 4), ("hy_decay", 16)]
SP_OFF = {}
_o = 0
for _n, _c in SP_SPEC:
    SP_OFF[_n] = (_o, _c)
    _o += _c
NSP = _o


def _pack_small(inp):
    sp = np.zeros((128, NSP), np.float32)

    def put(name, arr):
        o, n = SP_OFF[name]
        assert arr.shape == (128, n), (name, arr.shape, n)
        sp[:, o:o + n] = arr

    def fm(a, nch):
        L = a.shape[0]
        return a.reshape(L, nch, 128).transpose(2, 0, 1).reshape(128, L * nch)

    put("norm1_w", fm(inp["norm1_w"], 8))
    put("norm2_w", fm(inp["norm2_w"], 8))
    put("b_mod", fm(inp["b_mod"], 48))
    put("final_norm_w", inp["final_norm_w"].reshape(8, 128).T)
    put("q_norm_w", np.tile(inp["q_norm_w"].T, (2, 1)))
    put("k_norm_w", np.tile(inp["k_norm_w"].T, (2, 1)))
    put("hy_conv_w", inp["hy_conv_w"].reshape(DEPTH, 3, 6, 128).transpose(3, 0, 1, 2).reshape(128, DEPTH * 18))
    put("hy_conv_b", fm(inp["hy_conv_b"], 6))
    put("pool_scale", fm(inp["pool_scale"], 2))
    put("gm_bs", inp["gm_bs"].transpose(2, 0, 1).reshape(128, DEPTH * 4))
    h64 = lambda a: np.concatenate([a, np.zeros_like(a)], axis=0)
    put("hy_b1", h64(inp["hy_b1"].T))
    put("hy_freq", h64(inp["hy_freq"].T))
    put("hy_b2", h64(inp["hy_b2"].transpose(2, 0, 1).reshape(64, DEPTH * 2)))
    put("hy_decay", fm(inp["hy_decay"], 8))
    return sp


_CONST_CACHE = {}


def _consts():
    if _CONST_CACHE:
        return _CONST_CACHE
    cbf = np.zeros((128, 3, 128), np.float32)
    cbf[:, 0, :] = np.eye(128)
    k = np.arange(128)
    cbf[k, 1, k ^ 1] = 1.0
    cosF, sinS = _rope_tables()
    _CONST_CACHE.update(dict(
        cbf=_bf(cbf), ropeT=np.stack([cosF, sinS]).astype(np.float32),
        invcnt_l=_pool_invcnt(SEQ), invcnt_c=_pool_invcnt(CTX)))
    for nm, n in (("lat", SEQ), ("ctx", CTX)):
        zz, tr = _hy_tables(n)
        tf, tb = _dft_tables(2 * n // 128)
        _CONST_CACHE["zpos_" + nm] = zz
        _CONST_CACHE["trow_" + nm] = tr
        _CONST_CACHE["dftf_" + nm] = tf
        _CONST_CACHE["dftb_" + nm] = tb
    return _CONST_CACHE


def make_in_maps(inp):
    inp = {k: np.asarray(v) for k, v in inp.items()}
    shared = dict(_consts())
    for k in ("w_mod", "w_in", "w_br_attn", "w_br_hyena", "w_br_pool", "w_br_gmlp", "w_out", "ffn_w_gate",
              "ffn_w_up", "ffn_w_down", "gm_norm_w", "pool_w", "hy_w1", "hy_w2", "hy_w3", "hy_skip"):
        shared[k] = np.ascontiguousarray(inp[k], dtype=np.float32)
    shared["smallp"] = _pack_small(inp)
    shared["gm_wsT"] = np.ascontiguousarray(inp["gm_ws"].transpose(0, 1, 3, 2))
    shared["hy_skip"] = np.ascontiguousarray(inp["hy_skip"].reshape(DEPTH, 512), dtype=np.float32)
    maps = []
    for b in range(NCORE):
        m = dict(shared)
        m["xT"] = np.ascontiguousarray(inp["x"][b].T)
        m["ctxT"] = np.ascontiguousarray(inp["ctx"][b].T)
        cv = np.stack([inp["c"][b].reshape(8, 128).T, inp["c_ctx"].reshape(8, 128).T], axis=-1)
        m["cvec"] = np.ascontiguousarray(cv, dtype=np.float32)
        maps.append(m)
    return maps


def kernel(**inputs):
    nc = build_program()
    maps = make_in_maps(inputs)
    res = run_bass_kernel_spmd(nc, maps, core_ids=list(range(NCORE)))
    out = np.stack([np.ascontiguousarray(r["outT"].T) for r in res.results], axis=0)
    return out.astype(np.float32)
```

```python
import numpy as np
import ml_dtypes
from contextlib import ExitStack
import concourse.bass as bass
import concourse.mybir as mybir
from concourse.bass_utils import run_bass_kernel_spmd

F32 = mybir.dt.float32
BF16 = mybir.dt.bfloat16
AF = mybir.ActivationFunctionType
ALU = mybir.AluOpType

D = 1024
SEQ = 8192
CTX = 256
DEPTH = 2
NCORE = 8
HID = 2816
EPS = 1e-6
NKEY = CTX + SEQ
PADC = 8
STQ = "pool"


class Res:
    __slots__ = ("name", "readers", "writers", "multi", "sem")

    def __init__(self, name, multi=False):
        self.name = name
        self.readers = {}
        self.writers = {}
        self.multi = multi
        self.sem = None


class V:
    __slots__ = ("ap", "res", "sb")

    def __init__(self, ap, res, sb=True):
        self.ap = ap
        self.res = res
        self.sb = sb

    def __getitem__(self, k):
        return V(self.ap[k], self.res, self.sb)

    def rearrange(self, s, **kw):
        return V(self.ap.rearrange(s, **kw), self.res, self.sb)

    def bcast(self, shape):
        return V(self.ap.broadcast_to(list(shape)), self.res, self.sb)

    def unsqueeze(self, a):
        return V(self.ap.unsqueeze(a), self.res, self.sb)

    def pbcast(self, n):
        return V(self.ap.partition_broadcast(n), self.res, self.sb)

    def wr(self, res):
        return V(self.ap, res, self.sb)


class KB:
    def __init__(self, nc):
        self.nc = nc
        self.eng = {"pe": nc.tensor, "act": nc.scalar, "dve": nc.vector, "pool": nc.gpsimd, "sp": nc.sync}
        self.psem = {}
        self.pcnt = {}
        self.root = ExitStack()
        for e in ("pe", "act", "dve", "pool"):
            self.psem[e] = self.root.enter_context(nc.semaphore(f"prog_{e}"))
            self.pcnt[e] = 0
        self.waited = {e: {} for e in self.eng}
        self.dma_all = []
        self.dma_free = []
        self.phase_res = []
        self.stacks = [self.root]
        self.uid = 0
        self.dram_res = {}

    def _name(self, n):
        self.uid += 1
        return f"{n}_{self.uid}"

    def tile(self, name, shape, dt, multi=False):
        t = self.stacks[-1].enter_context(self.nc.sbuf_tensor(self._name(name), list(shape), dt))
        r = Res(name, multi)
        self.phase_res.append(r)
        return V(t[tuple(slice(None) for _ in shape)], r, True)

    def ring(self, name, n, shape, dt, multi=False):
        return [self.tile(f"{name}{i}", shape, dt, multi) for i in range(n)]

    def psum(self, name, shape, dt=F32):
        t = self.stacks[-1].enter_context(self.nc.psum_tensor(self._name(name), list(shape), dt))
        r = Res(name)
        self.phase_res.append(r)
        return V(t[tuple(slice(None) for _ in shape)], r, True)

    def dram(self, name, shape, dt, kind="Internal"):
        t = self.nc.dram_tensor(name, list(shape), dt, kind=kind)
        return V(t.ap(), Res(name, True), False)

    def dres(self, v, key):
        k = (v.res.name, key)
        if k not in self.dram_res:
            self.dram_res[k] = Res(f"{v.res.name}:{key}", True)
        return v.wr(self.dram_res[k])

    def push(self):
        self.stacks.append(ExitStack())
        self.phase_res = []

    def pop(self):
        self.barrier()
        for r in self.phase_res:
            if r.sem is not None:
                if "hw" in r.sem:
                    self.dma_free.append(r.sem["hw"])
                r.sem = None
        self.phase_res = []
        self.stacks.pop().close()

    def _deps(self, outs, ins):
        d = {}

        def add(tokd):
            for k, (s, c) in tokd.items():
                if k not in d or d[k][1] < c:
                    d[k] = (s, c)

        for v in ins:
            add(v.res.writers)
        for v in outs:
            add(v.res.readers)
            if not v.res.multi:
                add(v.res.writers)
        return d

    def _wait(self, e, deps, skip_own=True):
        own = self.psem[e].name if (skip_own and e == "pe") else None
        w = self.waited[e]
        for k, (s, c) in deps.items():
            if k == own:
                continue
            if w.get(k, 0) >= c:
                continue
            self.eng[e].wait_ge(s, c)
            w[k] = c

    def _commit(self, outs, ins, key, tok):
        for v in ins:
            r = v.res.readers
            if key not in r or r[key][1] < tok[1]:
                r[key] = tok
        for v in outs:
            if v.res.multi:
                v.res.writers[key] = tok
                v.res.readers = {}
            else:
                v.res.writers = {key: tok}
                v.res.readers = {}

    def op(self, e, fn, outs, ins):
        outs = [o for o in outs if isinstance(o, V)]
        ins = [i for i in ins if isinstance(i, V)]
        self._wait(e, self._deps(outs, ins))
        inst = fn()
        self.pcnt[e] += 1
        inst.then_inc(self.psem[e], 1)
        self._commit(outs, ins, self.psem[e].name, (self.psem[e], self.pcnt[e]))
        return inst

    def dma(self, q, out, in_, extra_in=(), extra_out=()):
        sbv = out if out.sb else in_
        outs = [out] + list(extra_out)
        ins = [in_] + list(extra_in)
        self._wait(q, self._deps(outs, ins), skip_own=False)
        res = sbv.res
        qt = "sw" if q == "pool" else "hw"
        if res.sem is None:
            res.sem = {}
        if qt not in res.sem:
            if qt == "hw" and self.dma_free:
                res.sem[qt] = self.dma_free.pop()
            else:
                s = self.root.enter_context(self.nc.semaphore(self._name("dma" + qt)))
                res.sem[qt] = [s, 0]
                self.dma_all.append(res.sem[qt])
        sm = res.sem[qt]
        inst = self.eng[q].dma_start(out=out.ap, in_=in_.ap)
        sm[1] += 16
        inst.then_inc(sm[0], 16)
        self._commit(outs, ins, sm[0].name, (sm[0], sm[1]))
        return inst

    def barrier(self):
        toks = {}
        for e in self.psem:
            toks[self.psem[e].name] = (self.psem[e], self.pcnt[e])
        for s in self.dma_all:
            if s[1] > 0:
                toks[s[0].name] = (s[0], s[1])
        for e in self.eng:
            self._wait(e, toks, skip_own=True)

    @staticmethod
    def _a(x):
        return x.ap if isinstance(x, V) else x

    def mm(self, out, lhsT, rhs, start=True, stop=True):
        return self.op("pe", lambda: self.nc.tensor.matmul(out.ap, lhsT=lhsT.ap, rhs=rhs.ap, start=start, stop=stop),
                       [out], [lhsT, rhs])

    def transpose(self, out, in_, ident):
        return self.op("pe", lambda: self.nc.tensor.transpose(out.ap, in_.ap, ident.ap), [out], [in_, ident])

    def act(self, out, in_, func, bias=0.0, scale=1.0, accum=None):
        a = self._a
        kw = {}
        if accum is not None:
            kw["accum_out"] = accum.ap
        return self.op("act", lambda: self.nc.scalar.activation(out=out.ap, in_=in_.ap, func=func, bias=a(bias),
                                                                scale=a(scale), **kw),
                       [out, accum], [in_, bias, scale])

    def tt(self, e, out, in0, in1, op):
        return self.op(e, lambda: self.eng[e].tensor_tensor(out=out.ap, in0=in0.ap, in1=in1.ap, op=op),
                       [out], [in0, in1])

    def ts(self, e, out, in0, s1, op0, s2=None, op1=None):
        a = self._a
        if op1 is None:
            f = lambda: self.eng[e].tensor_scalar(out=out.ap, in0=in0.ap, scalar1=a(s1), scalar2=None, op0=op0)
        else:
            f = lambda: self.eng[e].tensor_scalar(out=out.ap, in0=in0.ap, scalar1=a(s1), scalar2=a(s2), op0=op0,
                                                  op1=op1)
        return self.op(e, f, [out], [in0, s1, s2])

    def stt(self, out, in0, scalar, in1, op0, op1):
        a = self._a
        return self.op("dve", lambda: self.nc.vector.scalar_tensor_tensor(out=out.ap, in0=in0.ap, scalar=a(scalar),
                                                                          in1=in1.ap, op0=op0, op1=op1),
                       [out], [in0, scalar, in1])

    def copy(self, e, out, in_):
        if e == "act":
            return self.op("act", lambda: self.nc.scalar.copy(out=out.ap, in_=in_.ap), [out], [in_])
        return self.op(e, lambda: self.eng[e].tensor_copy(out=out.ap, in_=in_.ap), [out], [in_])

    def memset(self, e, out, val):
        return self.op(e, lambda: self.eng[e].memset(out.ap, val), [out], [])

    def recip(self, out, in_):
        return self.op("dve", lambda: self.nc.vector.reciprocal(out=out.ap, in_=in_.ap), [out], [in_])


def _bf(a):
    return np.asarray(a, dtype=np.float32).astype(ml_dtypes.bfloat16)


def _rope_tables():
    rows = SEQ // 64
    r, col = np.meshgrid(np.arange(rows, dtype=np.float32), np.arange(64, dtype=np.float32), indexing="ij")
    inv = (np.float32(10000.0) ** (-np.arange(0, 32, 2, dtype=np.float32) / np.float32(32))).astype(np.float32)
    ang = np.concatenate([r.reshape(-1, 1) * inv, col.reshape(-1, 1) * inv], axis=-1).astype(np.float32)
    cos = np.cos(ang).astype(np.float32)
    sin = np.sin(ang).astype(np.float32)
    p = np.arange(128)
    i = (p % 64) // 2
    cosF = cos[:, i].T.copy()
    sgn = np.where(p % 2 == 0, -1.0, 1.0).astype(np.float32)
    sinS = (sin[:, i].T * sgn[:, None]).astype(np.float32)
    return np.ascontiguousarray(cosF), np.ascontiguousarray(sinS)


def _hy_pos(n):
    pos = np.arange(n, dtype=np.float32)
    t = (pos / np.float32(n - 1)).astype(np.float32)
    bands = np.linspace(1e-4, 15, 16, dtype=np.float32)
    ang = (np.float32(2.0 * np.pi / n) * pos[:, None] * bands).astype(np.float32)
    z = np.concatenate([t[:, None], np.cos(ang), np.sin(ang)], axis=-1).astype(np.float32)
    return z, t


def _pool_invcnt(n):
    out = np.zeros((128, 2, n), np.float32)
    pos = np.arange(n)
    for g, win in enumerate((2, 4, 8, 16)):
        lo = np.clip(pos - win // 2, 0, n)
        hi = np.clip(pos + win - win // 2, 0, n)
        ic = (1.0 / (hi - lo).astype(np.float32)).astype(np.float32)
        ch, half = g // 2, g % 2
        out[half * 64:(half + 1) * 64, ch, :] = ic[None, :]
    return out


def _hy_tables(n):
    z, t = _hy_pos(n)
    zf = z.T.copy()
    zr = np.empty_like(zf)
    zr[:, 0] = zf[:, 0]
    zr[:, 1:] = zf[:, :0:-1]
    tr = np.zeros((2, n), np.float32)
    tr[0] = t
    tr[1, 1:] = t[:0:-1]
    return np.stack([zf, zr]).astype(np.float32), tr


def _dft_tables(n1):
    N = n1 * 128
    nh = n1 // 2
    a = np.arange(128, dtype=np.float64)
    b = np.arange(n1, dtype=np.float64)
    C = np.cos(2 * np.pi * np.outer(a, a) / 128.0)
    S_ = np.sin(2 * np.pi * np.outer(a, a) / 128.0)
    C1 = np.cos(2 * np.pi * np.outer(b, b) / n1)
    S1 = np.sin(2 * np.pi * np.outer(b, b) / n1)
    twc = np.cos(2 * np.pi * np.outer(a, b) / N)
    tws = np.sin(2 * np.pi * np.outer(a, b) / N)
    FC = 4 * n1 + 512
    tf = np.zeros((128, FC), np.float32)
    tf[:, 0:n1] = twc
    tf[:, n1:2 * n1] = twc
    tf[:, 2 * n1:3 * n1] = tws
    tf[:, 3 * n1:4 * n1] = -tws
    o = 4 * n1
    tf[0:n1, o:o + 128] = twc.T
    tf[0:n1, o + 128:o + 256] = twc.T
    tf[0:n1, o + 256:o + 384] = tws.T
    tf[0:n1, o + 384:o + 512] = -tws.T
    BC = 2 * n1 + 128 * 3 + 512 + 2 * nh
    tb = np.zeros((128, BC), np.float32)
    tb[0:n1, 0:n1] = C1
    tb[0:n1, n1:2 * n1] = -S1
    o = 2 * n1
    tb[:, o:o + 128] = C
    tb[:, o + 128:o + 256] = S_
    tb[:, o + 256:o + 384] = -S_
    o += 384
    tb[:, o:o + 128] = C
    tb[:, o + 128:o + 256] = S_
    tb[:, o + 256:o + 384] = -S_
    tb[:, o + 384:o + 512] = C
    o += 512
    tb[0:n1, o:o + nh] = C1[:, 0:nh] / N
    tb[0:n1, o + nh:o + 2 * nh] = -S1[:, 0:nh] / N
    return tf, _bf(tb)


class Stream:
    pass


def build_program(dbg=None, nlayers=DEPTH, stop_phase=None, skip=()):
    nc = bass.Bass("TRN2", target_bir_lowering=False)
    kb = KB(nc)
    dbg = dbg or []

    def din(name, shape, dt=F32):
        return kb.dram(name, shape, dt, kind="ExternalInput")

    def dsc(name, shape, dt):
        return kb.dram(name, shape, dt, kind=("ExternalOutput" if name in dbg else "Internal"))

    xT_in = din("xT", [D, SEQ])
    ctxT_in = din("ctxT", [D, CTX])
    cvec = din("cvec", [128, 8, 2])
    w_mod = din("w_mod", [DEPTH, D, 6 * D])
    w_in = din("w_in", [DEPTH, D, 6400])
    w_br_attn = din("w_br_attn", [DEPTH, 512, D])
    w_br_hy = din("w_br_hyena", [DEPTH, 256, D])
    w_br_pool = din("w_br_pool", [DEPTH, 256, D])
    w_br_gm = din("w_br_gmlp", [DEPTH, 256, D])
    w_out = din("w_out", [DEPTH, D, D])
    ffn_g = din("ffn_w_gate", [DEPTH, D, HID])
    ffn_u = din("ffn_w_up", [DEPTH, D, HID])
    ffn_d = din("ffn_w_down", [DEPTH, HID, D])
    smallp_d = din("smallp", [128, NSP])
    gm_wsT = din("gm_wsT", [DEPTH, 4, 128, 128])
    gm_nw = din("gm_norm_w", [DEPTH, 256])
    pool_w = din("pool_w", [DEPTH, 4, 64, 64])
    hy_w1 = din("hy_w1", [DEPTH, 33, 64])
    hy_w2 = din("hy_w2", [DEPTH, 2, 64, 64])
    hy_w3 = din("hy_w3", [DEPTH, 64, 1024])
    hy_skip = din("hy_skip", [DEPTH, 512])
    cbf = din("cbf", [128, 3, 128], BF16)
    ropeT = din("ropeT", [2, 128, SEQ])
    invcnt_l = din("invcnt_l", [128, 2, SEQ])
    invcnt_c = din("invcnt_c", [128, 2, CTX])
    outT = kb.dram("outT", [D, SEQ], F32, kind="ExternalOutput")

    xa = dsc("xa", [D, SEQ], F32)
    xb = dsc("xb", [D, SEQ], F32)
    ca = dsc("ca", [D, CTX], F32)
    cb = dsc("cb", [D, CTX], F32)
    kT2d = dsc("kT2d", [128, 2, NKEY], BF16)
    vaugd = dsc("vaugd", [NKEY // 128, 128, 2, 66], BF16)

    def mk_stream(name, S, T, xin, koff):
        st = Stream()
        st.name, st.S, st.T, st.NT, st.koff = name, S, T, S // T, koff
        st.x = xin
        st.qT = dsc(f"qT_{name}", [128, 4, S], BF16)
        st.pT = dsc(f"pT_{name}", [D, S + 2 * PADC], BF16)
        st.zT = dsc(f"zT_{name}", [768, S], BF16)
        st.plT = dsc(f"plT_{name}", [256, S], BF16)
        st.gmT = dsc(f"gmT_{name}", [256, S], BF16)
        st.hyT = dsc(f"hyT_{name}", [256, S], BF16)
        st.attnT = dsc(f"attnT_{name}", [128, 4, S], BF16)
        st.rope = (name == "lat")
        st.si = 0 if name == "lat" else 1
        st.invcnt = invcnt_l if name == "lat" else invcnt_c
        n1 = 2 * S // 128
        st.zpos = din(f"zpos_{name}", [2, 33, S])
        st.trow = din(f"trow_{name}", [2, S])
        st.dftf = din(f"dftf_{name}", [128, 4 * n1 + 512])
        st.dftb = din(f"dftb_{name}", [128, 2 * n1 + 384 + 512 + n1], BF16)
        st.kfT = dsc(f"kfT_{name}", [512, 2 * S], BF16)
        return st

    lat = mk_stream("lat", SEQ, 512, xT_in, CTX)
    cxs = mk_stream("ctx", CTX, 256, ctxT_in, 0)

    identb = kb.tile("identb", [128, 128], BF16)
    permb = kb.tile("permb", [128, 128], BF16)
    ones1024 = kb.tile("ones1024", [128, 128], BF16)
    blk64 = kb.tile("blk64", [128, 128], BF16)
    onesf = kb.tile("onesf", [128, 64], F32)
    epst = kb.tile("epst", [128, 1], F32)
    smallp = kb.tile("smallp", [128, NSP], F32)
    modp = kb.tile("modp", [128, 2, 6, 8], F32)
    actbf = kb.tile("actbf", [128, 8, 2], BF16)
    zero_bf = kb.tile("zero_bf", [128, 8, PADC], BF16)

    kb.dma("sp", identb, cbf[:, 0, :])
    kb.dma("sp", permb, cbf[:, 1, :])
    kb.dma("sp", smallp, smallp_d)
    kb.memset("dve", ones1024, 1.0 / 1024.0)
    kb.memset("dve", blk64, 0.0)
    kb.memset("dve", blk64[0:64, 0:64], 1.0 / 64.0)
    kb.memset("dve", blk64[64:128, 64:128], 1.0 / 64.0)
    kb.memset("dve", onesf, 1.0)
    kb.memset("dve", epst, EPS)
    kb.memset("dve", zero_bf, 0.0)
    for st in (lat, cxs):
        pv = st.pT.rearrange("(c p) t -> p c t", p=128)
        kb.dma("sp", kb.dres(pv[:, :, 0:PADC], "padlo"), zero_bf)
        kb.dma("sp", kb.dres(pv[:, :, PADC + st.S:2 * PADC + st.S], "padhi"), zero_bf)
    kb.push()
    cv = kb.tile("cv", [128, 8, 2], F32)
    kb.dma("sp", cv, cvec)
    kb.act(actbf, cv, AF.Silu)
    kb.pop()

    def sp_col(name, l=None):
        o, n = SP_OFF[name]
        if l is not None:
            per = n // DEPTH
            return smallp[:, o + l * per:o + (l + 1) * per]
        return smallp[:, o:o + n]

    def phase_M(l):
        kb.push()
        wm = kb.ring("wm", 2, [128, 8, 1536], BF16)
        pm = kb.psum("pm", [128, 48, 2])
        modt = kb.tile("modt", [128, 48, 2], F32)
        wsrc = w_mod[l].rearrange("(kc p) n -> p kc n", p=128)
        for blk in range(4):
            w = wm[blk % 2]
            for kc in range(8):
                kb.dma("pool", w[:, kc, :], wsrc[:, kc, blk * 1536:(blk + 1) * 1536])
            for j in range(12):
                oc = blk * 12 + j
                for kc in range(8):
                    kb.mm(pm[:, oc, :], w[:, kc, j * 128:(j + 1) * 128], actbf[:, kc, :], start=(kc == 0),
                          stop=(kc == 7))
        bm = sp_col("b_mod", l)
        kb.tt("dve", modt, pm, bm.unsqueeze(2).bcast([128, 48, 2]), ALU.add)
        n1 = sp_col("norm1_w", l)
        n2 = sp_col("norm2_w", l)
        for s in range(2):
            kb.stt(modp[:, s, 0, :], modt[:, 8:16, s], 1.0, n1, ALU.add, ALU.mult)
            kb.copy("dve", modp[:, s, 1, :], modt[:, 0:8, s])
            kb.copy("dve", modp[:, s, 2, :], modt[:, 16:24, s])
            kb.stt(modp[:, s, 3, :], modt[:, 32:40, s], 1.0, n2, ALU.add, ALU.mult)
            kb.copy("dve", modp[:, s, 4, :], modt[:, 24:32, s])
            kb.copy("dve", modp[:, s, 5, :], modt[:, 40:48, s])
        kb.pop()

    def norm_mod(st, xt, sq, h, ps_ss, srt, rstd, which, TT):
        ai = 0 if which == 1 else 3
        kb.act(sq[:, :, :TT], xt[:, :, :TT], AF.Square)
        for c in range(8):
            kb.mm(ps_ss[:, :TT], ones1024, sq[:, c, :TT], start=(c == 0), stop=(c == 7))
        kb.act(srt[:, :TT], ps_ss[:, :TT], AF.Ln, bias=epst)
        kb.act(rstd[:, :TT], srt[:, :TT], AF.Exp, scale=-0.5)
        return ai

    def norm_apply(st, xt, xs, h, rstd, ai, TT):
        kb.tt("dve", xs[:, :, :TT], xt[:, :, :TT], rstd[:, :TT].unsqueeze(1).bcast([128, 8, TT]), ALU.mult)
        for c in range(8):
            kb.act(h[:, c, :TT], xs[:, c, :TT], AF.Identity, bias=modp[:, st.si, ai + 1, c:c + 1],
                   scale=modp[:, st.si, ai, c:c + 1])

    def phase_A(l):
        kb.push()
        wA = kb.tile("wA", [128, 8, 2432], BF16, multi=True)
        wsrc = w_in[l].rearrange("(kc p) n -> p kc n", p=128)
        for (d0, s0, n) in ((0, 0, 512), (512, 512, 64), (576, 512, 64), (640, 576, 64), (704, 576, 64),
                            (768, 768, 768), (1536, 1536, 256), (1792, 640, 128), (1920, 1792, 512)):
            for kc in range(8):
                kb.dma("pool", wA[:, kc, d0:d0 + n], wsrc[:, kc, s0:s0 + n])
        poolW = kb.tile("poolW", [128, 2, 128], BF16)
        kb.memset("dve", poolW, 0.0)
        for g in range(4):
            ch, hf = g // 2, g % 2
            kb.dma("pool", poolW[hf * 64:(hf + 1) * 64, ch, hf * 64:(hf + 1) * 64], pool_w[l, g])
        wsT = kb.tile("wsT", [128, 4, 128], BF16, multi=True)
        for g in range(4):
            kb.dma("pool", wsT[:, g, :], gm_wsT[l, g])
        gnw = kb.tile("gnw", [128, 256], F32)
        kb.dma("sp", gnw, gm_nw[l, :].pbcast(128))

        xt_r = kb.ring("xt", 1, [128, 8, 512], F32)
        sq = kb.tile("sq", [128, 8, 512], BF16)
        srt = kb.tile("srt", [128, 512], F32)
        rstd = kb.tile("rstd", [128, 512], F32)
        h_r = kb.ring("h", 2, [128, 8, 512], BF16)
        sqq_r = kb.ring("sqq", 2, [128, 512], BF16)
        srq_r = kb.ring("srq", 2, [128, 512], F32)
        rq_r = kb.ring("rq", 2, [128, 512], F32)
        qn_r = kb.ring("qn", 2, [128, 512], BF16)
        t1_r = kb.ring("t1", 2, [128, 512], F32)
        t2_r = kb.ring("t2", 2, [128, 512], F32)
        rope_r = kb.ring("rope", 2, [128, 2, 512], F32)
        qst_r = kb.ring("qst", 1, [128, 4, 512], BF16)
        kst_r = kb.ring("kst", 2, [128, 2, 512], BF16)
        vst_r = kb.ring("vst", 2, [128, 4, 2, 66], BF16)
        pst_r = kb.ring("pst", 1, [128, 8, 512], BF16)
        usb_r = kb.ring("usb", 2, [128, 256], F32)
        vnn = kb.tile("vnn", [128, 256], F32)
        vn_r = kb.ring("vn", 2, [128, 256], BF16)
        gmt_r = kb.ring("gmt", 2, [128, 256], BF16)
        stat = kb.tile("stat", [128, 8], F32)
        junk = kb.tile("junk", [128, 256], F32)
        gmst_r = kb.ring("gmst", 2, [128, 2, 512], BF16)
        pp_r = kb.ring("pp", 2, [128, 8, 512 + 2 * PADC], BF16)
        zst_r = kb.ring("zst", 1, [128, 6, 512], BF16)
        pa = [kb.tile(f"pa{i}", [128, 2, 512 + 2 * PADC], F32) for i in range(2)]
        icn_r = kb.ring("icn", 1, [128, 2, 512], F32)
        dpl = kb.tile("dpl", [128, 2, 512], F32)
        dpb = kb.tile("dpb", [128, 2, 512], BF16)
        plst_r = kb.ring("plst", 2, [128, 2, 512], BF16)

        ps_main = [kb.psum(f"ps_main{i}", [128, 512]) for i in range(2)]
        ps_qk = [kb.psum(f"ps_qk{i}", [128, 512]) for i in range(2)]
        ps_ss = ps_qk[0]
        ps_tok = [kb.psum(f"ps_tok{i}", [128, 512]) for i in range(2)]
        ps_gs = kb.psum("ps_gs", [128, 256])
        ps_gt_t = kb.stacks[-1].enter_context(nc.psum_tensor(kb._name("ps_gt"), [128, 2, 128], BF16))
        ps_gt = V(ps_gt_t[:, :, :], Res("ps_gt"), True)

        for r in vst_r:
            kb.memset("dve", r[:, :, :, 64:66], 1.0)

        wq = sp_col("q_norm_w", l)
        wk = sp_col("k_norm_w", l)
        hw = sp_col("hy_conv_w", l)
        hb = sp_col("hy_conv_b", l)
        pscale = sp_col("pool_scale", l)
        bsT = sp_col("gm_bs", l)

        cnt = {"main": 0, "qk": 0, "tok": 0, "h": 0}
        dg = kb.tile("dg", [128, 18, 128], BF16)
        for k in range(18):
            kb.ts("dve", dg[:, k, :], identb, hw[:, k:k + 1], ALU.mult)

        hcur = {}

        def prep_A(st, i):
            TT = st.T
            t0 = i * TT
            xt = xt_r[0]
            h = h_r[cnt["h"] % 2]
            cnt["h"] += 1
            xsrc = st.x.rearrange("(c p) t -> p c t", p=128)
            kb.dma("sp", xt[:, :, :TT], kb.dres(xsrc[:, :, t0:t0 + TT], i))
            rp = None
            if st.rope:
                rp = rope_r[i % 2]
                kb.dma("sp", rp[:, :, :TT], ropeT[:, :, t0:t0 + TT].rearrange("a p t -> p a t"))
            ai = norm_mod(st, xt, sq, h, ps_ss, srt, rstd, 1, TT)
            norm_apply(st, xt, xt, h, rstd, ai, TT)
            hcur[(st.name, i)] = (h, rp)

        def tile_A(st, i, full, mid_hook=None):
            TT = st.T
            t0 = i * TT
            h, rp = hcur.pop((st.name, i))
            qst = qst_r[0]
            kst = kst_r[i % 2]
            vst = vst_r[i % 2]
            pst = pst_r[0]
            gmst = gmst_r[i % 2]
            chunks = []
            if full:
                chunks += [("q", j, j * 128) for j in range(4)]
            chunks += [("k", 0, 512), ("k", 1, 640)]
            if full:
                chunks += [("p", j, 768 + j * 128) for j in range(8)]
            pss = {}

            def main_mm(ci):
                kind, idx, c0 = chunks[ci]
                ps = ps_main[cnt["main"] % 2]
                cnt["main"] += 1
                for kc in range(8):
                    kb.mm(ps[:, :TT], wA[:, kc, c0:c0 + 128], h[:, kc, :TT], start=(kc == 0), stop=(kc == 7))
                pss[ci] = ps

            def post(ci):
                kind, idx, c0 = chunks[ci]
                ps = pss.pop(ci)
                if kind == "p":
                    kb.copy("act", pst[:, idx, :TT], ps[:, :TT])
                    return
                k2 = cnt["qk"] % 2
                sqq = sqq_r[k2]
                qn = qn_r[k2]
                srq, rq, t1, t2 = srq_r[k2], rq_r[k2], t1_r[k2], t2_r[k2]
                psq = ps_qk[0]
                psw = ps_qk[1]
                cnt["qk"] += 1
                kb.act(sqq[:, :TT], ps[:, :TT], AF.Square)
                kb.mm(psq[:, :TT], blk64, sqq[:, :TT])
                kb.act(srq[:, :TT], psq[:, :TT], AF.Ln, bias=epst)
                kb.act(rq[:, :TT], srq[:, :TT], AF.Exp, scale=-0.5)
                dst = qst[:, idx, :TT] if kind == "q" else kst[:, idx, :TT]
                wn = wq if kind == "q" else wk
                if st.rope:
                    kb.stt(qn[:, :TT], ps[:, :TT], wn[:, 0:1], rq[:, :TT], ALU.mult, ALU.mult)
                    kb.mm(psw[:, :TT], permb, qn[:, :TT])
                    kb.tt("dve", t1[:, :TT], qn[:, :TT], rp[:, 0, :TT], ALU.mult)
                    kb.tt("dve", t2[:, :TT], psw[:, :TT], rp[:, 1, :TT], ALU.mult)
                    kb.tt("dve", dst, t1[:, :TT], t2[:, :TT], ALU.add)
                else:
                    kb.stt(dst, ps[:, :TT], wn[:, 0:1], rq[:, :TT], ALU.mult, ALU.mult)

            main_mm(0)
            for ci in range(len(chunks)):
                if mid_hook is not None and ci == min(8, len(chunks) - 1):
                    mid_hook()
                if ci + 1 < len(chunks):
                    main_mm(ci + 1)
                post(ci)
            nsub = TT // 128
            for sub in range(nsub):
                ts_ = slice(sub * 128, (sub + 1) * 128)
                ps = ps_tok[cnt["tok"] % 2]
                cnt["tok"] += 1
                for kc in range(8):
                    kb.mm(ps[:, 0:128], h[:, kc, ts_], wA[:, kc, 1792:1920], start=(kc == 0), stop=(kc == 7))
                kb.copy("act", vst[:, sub, :, 0:64], ps[:, 0:128].rearrange("p (k d) -> p k d", k=2))
                if not full:
                    continue
                ps = ps_tok[cnt["tok"] % 2]
                cnt["tok"] += 1
                for kc in range(8):
                    kb.mm(ps[:, 0:512], h[:, kc, ts_], wA[:, kc, 1920:2432], start=(kc == 0), stop=(kc == 7))
                usb = usb_r[sub % 2]
                vn = vn_r[sub % 2]
                gmt = gmt_r[sub % 2]
                kb.copy("act", usb, ps[:, 0:256])
                kb.memset("dve", stat[:, 0:2], 0.0)
                kb.act(junk, ps[:, 256:512], AF.Identity, accum=stat[:, 0:1])
                kb.act(junk, ps[:, 256:512], AF.Square, accum=stat[:, 1:2])
                kb.ts("dve", stat[:, 2:3], stat[:, 0:1], 1.0 / 256.0, ALU.mult)
                kb.tt("dve", stat[:, 3:4], stat[:, 2:3], stat[:, 2:3], ALU.mult)
                kb.stt(stat[:, 4:5], stat[:, 1:2], 1.0 / 256.0, stat[:, 3:4], ALU.mult, ALU.subtract)
                kb.act(stat[:, 5:6], stat[:, 4:5], AF.Ln, bias=epst)
                kb.act(stat[:, 6:7], stat[:, 5:6], AF.Exp, scale=-0.5)
                kb.stt(stat[:, 7:8], stat[:, 2:3], -1.0, stat[:, 6:7], ALU.mult, ALU.mult)
                kb.act(vnn, ps[:, 256:512], AF.Identity, bias=stat[:, 7:8], scale=stat[:, 6:7])
                kb.tt("dve", vn, vnn, gnw, ALU.mult)
                for g in range(4):
                    gs = slice(g * 64, (g + 1) * 64)
                    kb.mm(ps_gs[:, gs], wsT[:, g, :], vn[:, gs])
                for g in range(4):
                    gs = slice(g * 64, (g + 1) * 64)
                    kb.stt(gmt[:, gs], ps_gs[:, gs], bsT[:, g:g + 1], usb[:, gs], ALU.add, ALU.mult)
                for hf in range(2):
                    kb.transpose(ps_gt[:, hf, :], gmt[:, hf * 128:(hf + 1) * 128], identb)
                kb.copy("dve", gmst[:, :, ts_], ps_gt)
            if full:
                kb.dma(STQ, kb.dres(st.qT[:, :, t0:t0 + TT], i), qst[:, :, :TT])
                pv = st.pT.rearrange("(c p) t -> p c t", p=128)
                kb.dma(STQ, kb.dres(pv[:, :, PADC + t0:PADC + t0 + TT], i), pst[:, :, :TT])
                gv = st.gmT.rearrange("(c p) t -> p c t", p=128)
                kb.dma(STQ, kb.dres(gv[:, :, t0:t0 + TT], i), gmst[:, :, :TT])
            kb.dma(STQ, kb.dres(kT2d[:, :, st.koff + t0:st.koff + t0 + TT], (st.name, i)), kst[:, :, :TT])
            c0 = (st.koff + t0) // 128
            kb.dma(STQ, kb.dres(vaugd[c0:c0 + nsub].rearrange("c p k d -> p c k d"), (st.name, i)),
                   vst[:, :nsub])

        def tile_A2(st, i):
            TT = st.T
            t0 = i * TT
            W = TT + 2 * PADC
            pp = pp_r[i % 2]
            icn = icn_r[0]
            zst = zst_r[0]
            plst = plst_r[i % 2]
            pv = st.pT.rearrange("(c p) t -> p c t", p=128)
            extra = [kb.dres(pv, k) for k in (i - 1, i + 1) if 0 <= k < st.NT]
            extra += [kb.dres(pv, "padlo"), kb.dres(pv, "padhi")]
            kb.dma("sp", pp[:, :, :W], kb.dres(pv[:, :, t0:t0 + W], i), extra_in=extra)
            kb.dma("sp", icn[:, :, :TT], st.invcnt[:, :, t0:t0 + TT])
            for c in range(6):
                ps = ps_main[cnt["main"] % 2]
                cnt["main"] += 1
                for tap in range(3):
                    kb.mm(ps[:, :TT], dg[:, tap * 6 + c, :], pp[:, c, PADC - 1 + tap:PADC - 1 + tap + TT],
                          start=(tap == 0), stop=(tap == 2))
                kb.act(zst[:, c, :TT], ps[:, :TT], AF.Identity, bias=hb[:, c:c + 1])
            zv = st.zT.rearrange("(c p) t -> p c t", p=128)
            kb.dma(STQ, kb.dres(zv[:, :, t0:t0 + TT], i), zst[:, :, :TT])
            z = pp[:, 6:8, :]
            A_, B_ = pa[0], pa[1]
            kb.tt("dve", A_[:, :, 1:W], z[:, :, 1:W], z[:, :, 0:W - 1], ALU.add)
            kb.tt("dve", B_[:, :, 3:W], A_[:, :, 3:W], A_[:, :, 1:W - 2], ALU.add)

            def sel(hf, ch, arr, sh):
                ps_ = slice(hf * 64, (hf + 1) * 64)
                kb.tt("dve", dpl[ps_, ch, :TT], arr[ps_, ch, PADC + sh:PADC + sh + TT], icn[ps_, ch, :TT], ALU.mult)
                kb.tt("dve", dpb[ps_, ch, :TT], dpl[ps_, ch, :TT], pp[ps_, 6 + ch, PADC:PADC + TT], ALU.subtract)

            sel(0, 0, A_, 0)
            sel(1, 0, B_, 1)
            kb.tt("dve", A_[:, 1, 7:W], B_[:, 1, 7:W], B_[:, 1, 3:W - 4], ALU.add)
            kb.tt("dve", B_[:, 1, 15:W], A_[:, 1, 15:W], A_[:, 1, 7:W - 8], ALU.add)
            sel(0, 1, A_, 3)
            sel(1, 1, B_, 7)
            for ch in range(2):
                ps = ps_main[cnt["main"] % 2]
                cnt["main"] += 1
                kb.mm(ps[:, :TT], poolW[:, ch, :], dpb[:, ch, :TT])
                kb.act(plst[:, ch, :TT], ps[:, :TT], AF.Identity, scale=pscale[:, ch:ch + 1])
            plv = st.plT.rearrange("(c p) t -> p c t", p=128)
            kb.dma(STQ, kb.dres(plv[:, :, t0:t0 + TT], i), plst[:, :, :TT])

        fullc = (l < DEPTH - 1)
        prep_A(cxs, 0)
        tile_A(cxs, 0, fullc, mid_hook=lambda: prep_A(lat, 0))
        if fullc:
            tile_A2(cxs, 0)
        for i in range(lat.NT):
            hook = (lambda j=i: prep_A(lat, j + 1)) if i + 1 < lat.NT else None
            tile_A(lat, i, True, mid_hook=hook)
            if i >= 1:
                tile_A2(lat, i - 1)
        tile_A2(lat, lat.NT - 1)
        kb.pop()

    def phase_H(l, st):
        S = st.S
        N1 = 2 * S // 128
        NH = N1 // 2
        PI = float(np.pi)
        kb.push()
        SLH = min(2048, S)
        SLW = min(512, S)
        NSL = S // SLW
        w1 = kb.tile("w1", [33, 64], F32)
        kb.dma("sp", w1, hy_w1[l])
        w2 = kb.tile("w2", [64, 2, 64], F32, multi=True)
        for i in range(2):
            kb.dma("sp", w2[:, i, :], hy_w2[l, i])
        w3 = kb.tile("w3", [64, 1024], F32)
        kb.dma("sp", w3, hy_w3[l])
        fr = sp_col("hy_freq", l)[0:64]
        b1 = sp_col("hy_b1", l)[0:64]
        b2 = sp_col("hy_b2", l)[0:64]
        fb = kb.tile("fb", [64, 3], F32)
        kb.tt("dve", fb[:, 0:1], fr, b1, ALU.mult)
        kb.tt("dve", fb[:, 1:3], b2, fr.bcast([64, 2]), ALU.mult)
        negdec = kb.tile("negdec", [128, 8], F32)
        kb.act(negdec, sp_col("hy_decay", l), AF.Abs)
        kb.ts("dve", negdec, negdec, -1.0, ALU.mult)
        hd3 = kb.tile("hd3", [64, 2, S], F32, multi=True)
        zt_r = kb.ring("zt", 2, [33, SLH], F32)
        ha = kb.tile("ha", [64, SLH], F32)
        m1 = kb.tile("m1", [64, SLH], F32)
        hb_r = kb.ring("hb", 2, [64, SLH], F32)
        ps_h = kb.psum("ps_h", [64, SLH])
        n = 0
        for d in range(2):
            for sl in range(S // SLH):
                cs = slice(sl * SLH, (sl + 1) * SLH)
                zt = zt_r[n % 2]
                n += 1
                kb.dma("sp", zt, st.zpos[d, :, cs])
                cur, Kc = zt, 33
                for layer in range(3):
                    lhsT = w1 if layer == 0 else w2[:, layer - 1, :]
                    for q in range(max(1, SLH // 512)):
                        qs = slice(q * 512, min((q + 1) * 512, SLH))
                        kb.mm(ps_h[:, qs], lhsT, cur[0:Kc, qs])
                    kb.act(ha, ps_h, AF.Identity, scale=fr, bias=fb[:, layer:layer + 1])
                    kb.ts("dve", m1, ha, PI, ALU.is_gt, -2.0 * PI, ALU.mult)
                    kb.tt("dve", ha, ha, m1, ALU.add)
                    kb.ts("dve", m1, ha, -PI, ALU.is_lt, 2.0 * PI, ALU.mult)
                    kb.tt("dve", ha, ha, m1, ALU.add)
                    dst = hd3[:, d, cs] if layer == 2 else hb_r[layer % 2]
                    kb.act(dst, ha, AF.Sin)
                    cur, Kc = dst, 64
        tr_r = kb.ring("tr", 2, [128, 2, SLW], F32)
        win_r = kb.ring("win", 2, [128, SLW], F32)
        f_r = kb.ring("f", 2, [128, SLW], F32)
        junk = kb.tile("junk", [128, SLW], F32)
        kst_r = kb.ring("kst", 2, [128, SLW], BF16)
        part = kb.tile("part", [128, 8, NSL], F32)
        nrm8 = kb.tile("nrm8", [128, 8], F32)
        rinv = kb.tile("rinv", [128, 4], F32)
        ps_f = [kb.psum(f"ps_f{i}", [128, 512]) for i in range(2)]
        kb.memset("dve", part, 0.0)
        n = 0
        for pas in (1, 2):
            for sl in range(NSL):
                cs = slice(sl * SLW, (sl + 1) * SLW)
                tr = tr_r[(pas * NSL + sl) % 2]
                kb.dma("sp", tr, st.trow[:, cs].pbcast(128))
                for oc in range(8):
                    o, d, wh = oc // 4, (oc % 4) // 2, oc % 2
                    ps = ps_f[n % 2]
                    win = win_r[n % 2]
                    f = f_r[n % 2]
                    kst = kst_r[n % 2]
                    n += 1
                    kb.mm(ps[:, :SLW], w3[:, oc * 128:(oc + 1) * 128], hd3[:, d, cs])
                    kb.act(win, tr[:, d, :], AF.Exp, scale=negdec[:, oc:oc + 1])
                    kb.tt("dve", f, ps[:, :SLW], win, ALU.mult)
                    if d == 1 and sl == 0:
                        kb.memset("dve", f[:, 0:1], 0.0)
                    if pas == 1:
                        kb.act(junk, f, AF.Abs, accum=part[:, oc, sl:sl + 1])
                    else:
                        kb.ts("dve", kst, f, rinv[:, 2 * o + wh:2 * o + wh + 1], ALU.mult)
                        r0 = (2 * o + wh) * 128
                        kb.dma(STQ, st.kfT[r0:r0 + 128, d * S + sl * SLW:d * S + (sl + 1) * SLW], kst)
            if pas == 1:
                kb.op("dve", lambda: nc.vector.reduce_sum(out=nrm8.ap, in_=part.ap, axis=mybir.AxisListType.X),
                      [nrm8], [part])
                n4 = nrm8.rearrange("p (o d w) -> p o d w", o=2, d=2)
                kb.tt("dve", rinv.rearrange("p (o w) -> p o w", o=2), n4[:, :, 0, :], n4[:, :, 1, :], ALU.add)
                kb.recip(rinv, rinv)
        kb.pop()

        kb.push()
        tabf = kb.tile("tabf", [128, 4 * N1 + 512], F32)
        tabb = kb.tile("tabb", [128, 2 * N1 + 896 + N1], BF16)
        kb.dma("sp", tabf, st.dftf)
        kb.dma("sp", tabb, st.dftb)
        skipb = kb.tile("skipb", [128, 512], F32)
        kb.dma("sp", skipb[0:NH, :], hy_skip[l, :].pbcast(NH))
        TWc2 = tabf[:, 0:2 * N1].rearrange("p (h k) -> p h k", h=2).unsqueeze(1).bcast([128, 4, 2, N1])
        TWs = tabf[:, 2 * N1:3 * N1].unsqueeze(1).bcast([128, 4, N1])
        nTWs = tabf[:, 3 * N1:4 * N1].unsqueeze(1).bcast([128, 4, N1])
        o_ = 4 * N1
        iTWc2 = tabf[0:N1, o_:o_ + 256].rearrange("p (h k) -> p h k", h=2).unsqueeze(1).bcast([N1, 4, 2, 128])
        iTWs = tabf[0:N1, o_ + 256:o_ + 384].unsqueeze(1).bcast([N1, 4, 128])
        inTWs = tabf[0:N1, o_ + 384:o_ + 512].unsqueeze(1).bcast([N1, 4, 128])
        F1 = tabb[:, 0:2 * N1]
        o_ = 2 * N1
        Cm, Sm, nSm = tabb[:, o_:o_ + 128], tabb[:, o_ + 128:o_ + 256], tabb[:, o_ + 256:o_ + 384]
        CS, nSC = tabb[:, o_ + 384:o_ + 640], tabb[:, o_ + 640:o_ + 896]
        o_ += 896
        C1n, nS1n = tabb[0:N1, o_:o_ + NH], tabb[0:N1, o_ + NH:o_ + 2 * NH]
        Cg = 16
        NG = 256 // Cg
        vin_r = kb.ring("vin", 2, [128, Cg, 128], BF16)
        x1_r = kb.ring("x1in", 2, [128, Cg, 128], BF16)
        x2_r = kb.ring("x2in", 2, [128, Cg, 128], BF16)
        kf_r = [kb.ring(f"kf{o}", 2, [128, Cg, 128], BF16) for o in range(2)]
        hyst_r = kb.ring("hyst", 2, [128, Cg, 128], BF16)
        def mk_tmp(i):
            t = {}
            t["X1"] = kb.tile(f"X1_{i}", [128, 4, 2, 128], F32)
            t["X2"] = kb.tile(f"X2_{i}", [128, 4, 2, 128], F32)
            t["Ap"] = kb.tile(f"Ap_{i}", [128, 4, 2, 128], BF16)
            t["Kf"] = kb.tile(f"Kf_{i}", [128, 4, 2, 128], F32)
            t["T"] = [kb.tile(f"T{j}_{i}", [128, 4, 128], F32) for j in range(4)]
            t["Y"] = kb.tile(f"Y_{i}", [128, 4, 2, 128], BF16)
            t["XB1"] = kb.tile(f"XB1_{i}", [128, 4, 2, 128], F32)
            t["XB2"] = kb.tile(f"XB2_{i}", [128, 4, 2, 128], F32)
            t["Bp"] = kb.tile(f"Bp_{i}", [128, 4, 2, 128], BF16)
            t["tp"] = kb.tile(f"tp_{i}", [128, 4, 128], F32)
            t["u2"] = kb.tile(f"u2_{i}", [128, 4, 128], BF16)
            return t

        tmps = [mk_tmp(0), mk_tmp(1)]
        psA = kb.psum("psA", [128, 4, 256])
        psUr = kb.psum("psUr", [128, 512])
        psUi = kb.psum("psUi", [128, 512])
        psB = kb.psum("psB", [128, 4, 256])
        psY = kb.psum("psY", [128, 512])
        ur = psUr[:, 0:4 * N1].rearrange("p (c k) -> p c k", c=4)
        ui = psUi[:, 0:4 * N1].rearrange("p (c k) -> p c k", c=4)
        yv = psY[0:NH, 0:512].rearrange("p (c k) -> p c k", c=4)

        def fwd_fft(src, K, t):
            X1, X2, Ap = t["X1"], t["X2"], t["Ap"]
            yield ("W", "A")
            for c in range(4):
                kb.mm(psA[:, c, 0:2 * N1], src[0:K, c, :], F1[0:K, :])
            yield ("R", "A")
            pA4 = psA[:, :, 0:2 * N1].rearrange("p c (h k) -> p c h k", h=2)
            kb.tt("dve", X1[:, :, :, 0:N1], pA4, TWc2, ALU.mult)
            kb.tt("dve", X2[:, :, 0, 0:N1], pA4[:, :, 1, :], TWs, ALU.mult)
            kb.tt("dve", X2[:, :, 1, 0:N1], pA4[:, :, 0, :], nTWs, ALU.mult)
            kb.tt("dve", Ap[:, :, :, 0:N1], X1[:, :, :, 0:N1], X2[:, :, :, 0:N1], ALU.add)
            yield ("W", "U")
            rr, ri = Ap[:, :, 0, 0:N1], Ap[:, :, 1, 0:N1]
            kb.mm(ur, Cm, rr, start=True, stop=False)
            kb.mm(ur, Sm, ri, start=False, stop=True)
            kb.mm(ui, Cm, ri, start=True, stop=False)
            kb.mm(ui, nSm, rr, start=False, stop=True)

        def conv_block(o, kfblk, ublk, gateblk, dst, sk, t):
            Kf, Y, Bp, XB1, XB2, tp = t["Kf"], t["Y"], t["Bp"], t["XB1"], t["XB2"], t["tp"]
            yield from fwd_fft(kfblk, N1, t)
            yield ("R", "U")
            kb.copy("act", Kf[:, :, 0, 0:N1], ur)
            kb.copy("act", Kf[:, :, 1, 0:N1], ui)
            yield from fwd_fft(ublk, NH, t)
            yield ("R", "U")
            Kr, Ki = Kf[:, :, 0, 0:N1], Kf[:, :, 1, 0:N1]
            tt_ = [x[:, :, 0:N1] for x in t["T"]]
            kb.tt("dve", tt_[0], ur, Kr, ALU.mult)
            kb.tt("dve", tt_[1], ui, Ki, ALU.mult)
            kb.tt("dve", tt_[2], ur, Ki, ALU.mult)
            kb.tt("dve", tt_[3], ui, Kr, ALU.mult)
            kb.tt("dve", Y[:, :, 0, 0:N1], tt_[0], tt_[1], ALU.subtract)
            kb.tt("dve", Y[:, :, 1, 0:N1], tt_[2], tt_[3], ALU.add)
            yield ("W", "B")
            for c in range(4):
                kb.mm(psB[0:N1, c, :], Y[:, c, 0, 0:N1], CS, start=True, stop=False)
                kb.mm(psB[0:N1, c, :], Y[:, c, 1, 0:N1], nSC, start=False, stop=True)
            yield ("R", "B")
            pB4 = psB[0:N1].rearrange("p c (h k) -> p c h k", h=2)
            kb.tt("dve", XB1[0:N1], pB4, iTWc2, ALU.mult)
            kb.tt("dve", XB2[0:N1, :, 0, :], pB4[:, :, 1, :], inTWs, ALU.mult)
            kb.tt("dve", XB2[0:N1, :, 1, :], pB4[:, :, 0, :], iTWs, ALU.mult)
            kb.tt("dve", Bp[0:N1], XB1[0:N1], XB2[0:N1], ALU.add)
            yield ("W", "Y")
            kb.mm(yv, C1n, Bp[0:N1, :, 0, :], start=True, stop=False)
            kb.mm(yv, nS1n, Bp[0:N1, :, 1, :], start=False, stop=True)
            yield ("R", "Y")
            kb.tt("dve", tp[0:NH], ublk[0:NH], sk.unsqueeze(2).bcast([NH, 4, 128]), ALU.mult)
            kb.tt("dve", tp[0:NH], tp[0:NH], yv, ALU.add)
            kb.tt("dve", dst, tp[0:NH], gateblk[0:NH], ALU.mult)

        def fftv(src_rows, c0, nrow):
            return src_rows[c0:c0 + Cg, :].rearrange("c (a b) -> a c b", b=128)

        gt = {}

        def load_group(g):
            c0 = g * Cg
            vin, x1in, x2in = vin_r[g % 2], x1_r[g % 2], x2_r[g % 2]
            kfs = [kf_r[0][g % 2], kf_r[1][g % 2]]
            kb.dma("sp", vin[0:NH], fftv(st.zT[0:256], c0, NH))
            kb.dma("sp", x1in[0:NH], fftv(st.zT[256:512], c0, NH))
            kb.dma("sp", x2in[0:NH], fftv(st.zT[512:768], c0, NH))
            for o in range(2):
                kb.dma("sp", kfs[o][0:N1], fftv(st.kfT[o * 256:(o + 1) * 256], c0, N1))
            gt[g] = (vin, x1in, x2in, kfs, hyst_r[g % 2])

        def chain(g, b, t):
            vin, x1in, x2in, kfs, hyst = gt[g]
            c0 = g * Cg
            bs = slice(4 * b, 4 * b + 4)
            u2 = t["u2"]
            cc = c0 + 4 * b
            yield from conv_block(0, kfs[0][:, bs, :], vin[:, bs, :], x1in[:, bs, :], u2[0:NH],
                                  skipb[0:NH, cc:cc + 4], t)
            yield from conv_block(1, kfs[1][:, bs, :], u2, x2in[:, bs, :], hyst[0:NH, bs, :],
                                  skipb[0:NH, 256 + cc:256 + cc + 4], t)
            gdone[g] += 1
            if gdone[g] == Cg // 4:
                kb.dma(STQ, fftv(st.hyT, c0, NH), hyst[0:NH])
                if g + 2 < NG:
                    load_group(g + 2)

        NCHAIN = 2
        gdone = {g: 0 for g in range(NG)}
        load_group(0)
        if NG > 1:
            load_group(1)
        todo = [(g, b) for g in range(NG) for b in range(Cg // 4)]
        active = []
        held = {}
        nstart = 0
        while todo or active:
            while todo and len(active) < NCHAIN:
                g, b = todo.pop(0)
                ch = {"id": nstart, "gen": chain(g, b, tmps[nstart % 2]), "done": False}
                nstart += 1
                ch["tok"] = next(ch["gen"])
                active.append(ch)
            progressed = False
            for ch in list(active):
                kind, tile = ch["tok"]
                if kind == "W":
                    if held.get(tile) not in (None, ch["id"]):
                        continue
                    held[tile] = ch["id"]
                try:
                    ch["tok"] = next(ch["gen"])
                except StopIteration:
                    ch["done"] = True
                if kind == "R":
                    held[tile] = None
                progressed = True
                if ch["done"]:
                    active.remove(ch)
            assert progressed, "hyena chain interleave deadlock"
        kb.pop()

    def phase_B(l, streams):
        kb.push()
        kT = kb.tile("kT", [128, 2, NKEY], BF16, multi=True)
        va = kb.tile("va", [128, NKEY // 128, 2, 66], BF16, multi=True)
        nld = 6
        per = (NKEY // 128) // nld
        for i in range(nld):
            kb.dma("sp", kT[:, :, i * per * 128:(i + 1) * per * 128], kT2d[:, :, i * per * 128:(i + 1) * per * 128])
            kb.dma("sp", va[:, i * per:(i + 1) * per], vaugd[i * per:(i + 1) * per].rearrange("c p k d -> p c k d"))
        qt_r = kb.ring("qz", 2, [128, 8, 512], BF16)
        for q_ in qt_r:
            kb.memset("dve", q_, 0.0)
        pt_r = kb.ring("pt", 3, [128, 2, 512], BF16)
        ost_r = kb.ring("ost", 2, [128, 4, 512], BF16)
        osb_r = kb.ring("osb", 2, [64, 512], F32)
        otmp = kb.tile("otmp", [64, 512], BF16)
        rden_r = kb.ring("rden", 2, [128, 512], F32)
        ps_s = [kb.psum(f"ps_s{i}", [128, 2, 512]) for i in range(2)]
        ps_o = [kb.psum(f"ps_o{i}", [128, 512]) for i in range(2)]
        ps_bc = kb.psum("ps_bc", [64, 512])

        items = []
        for st in streams:
            nk = CTX if st.name == "ctx" else NKEY
            pairs = nk // 256
            for i in range(st.NT):
                for hd in range(8):
                    for cp in range(pairs):
                        items.append((st, i, hd, cp, pairs))
        qcur = {}
        tord = {}
        for (st_, i_, _h, _c, _p) in items:
            tord.setdefault((st_.name, i_), len(tord))

        def get_q(st, i):
            key = (st.name, i)
            if key not in qcur:
                qt = qt_r[tord[key] % 2]
                for hf_ in range(2):
                    P_ = slice(hf_ * 64, hf_ * 64 + 64)
                    kb.dma("sp", qt[P_, hf_:8:2, :st.T], st.qT[P_, :, i * st.T:(i + 1) * st.T])
                qcur[key] = qt
            return qcur[key]

        def emit_qk(n):
            st, i, hd, cp, pairs = items[n]
            qt = get_q(st, i)
            j, hf, kv = hd // 2, hd % 2, hd // 4
            P = slice(hf * 64, hf * 64 + 64)
            pss = ps_s[n % 2]
            for u in range(2):
                ch = 2 * cp + u
                kb.mm(pss[:, u, :st.T], kT[:, kv, ch * 128:(ch + 1) * 128], qt[:, hd, :st.T])

        pending = []

        def finalize2(st, i, hd, po, rden, osb, ost):
            NQ = st.T
            kb.mm(ps_bc[:, :NQ], onesf[64:65, 0:64], rden[64:65, :NQ])
            if hd % 2 == 0:
                kb.tt("dve", ost[0:64, hd // 2, :NQ], osb[:, :NQ], ps_bc[:, :NQ], ALU.mult)
            else:
                kb.tt("dve", otmp[0:64, :NQ], osb[:, :NQ], ps_bc[:, :NQ], ALU.mult)
                kb.copy("dve", ost[64:128, hd // 2, :NQ], otmp[0:64, :NQ])
            if hd == 7:
                kb.dma(STQ, st.attnT[:, :, i * NQ:(i + 1) * NQ], ost[:, :, :NQ])

        emit_qk(0)
        hcount = 0
        for n in range(len(items)):
            st, i, hd, cp, pairs = items[n]
            NQ = st.T
            kv = hd // 4
            if n + 1 < len(items):
                emit_qk(n + 1)
            pt = pt_r[n % 3]
            kb.act(pt[:, :, :NQ], ps_s[n % 2][:, :, :NQ], AF.Exp, scale=0.125)
            if cp == 0:
                po = ps_o[hcount % 2]
            for u in range(2):
                ch = 2 * cp + u
                kb.mm(po[0:65, :NQ], va[:, ch, kv, 0:65], pt[:, u, :NQ], start=(cp == 0 and u == 0),
                      stop=(cp == pairs - 1 and u == 1))
            while pending:
                finalize2(*pending.pop(0))
            if cp == pairs - 1:
                rden = rden_r[hcount % 2]
                osb = osb_r[hcount % 2]
                ost = ost_r[tord[(st.name, i)] % 2]
                kb.recip(rden[64:65, :NQ], po[64:65, :NQ])
                kb.copy("act", osb[:, :NQ], po[0:64, :NQ])
                pending.append((st, i, hd, po, rden, osb, ost))
                hcount += 1
        while pending:
            finalize2(*pending.pop(0))
        kb.pop()

    def prep_C(st, i, TT, which, xt, sq, srt, rstd, h, ps_ss):
        t0 = i * TT
        xsrc = st.x.rearrange("(c p) t -> p c t", p=128)
        kb.dma("sp", xt[:, :, :TT], xsrc[:, :, t0:t0 + TT])
        ai = norm_mod(st, xt, sq, h, ps_ss, srt, rstd, which, TT)
        norm_apply(st, xt, xt, h, rstd, ai, TT)

    def phase_C1(l, streams, TT):
        kb.push()
        wG = kb.tile("wG", [128, 8, 4096], BF16, multi=True)
        wBA = kb.tile("wBA", [128, 4, 1024], BF16, multi=True)
        wBH = kb.tile("wBH", [128, 2, 1024], BF16, multi=True)
        wBP = kb.tile("wBP", [128, 2, 1024], BF16, multi=True)
        wBG = kb.tile("wBG", [128, 2, 1024], BF16, multi=True)
        wO = kb.tile("wO", [128, 8, 1024], BF16, multi=True)
        wsrc = w_in[l].rearrange("(kc p) n -> p kc n", p=128)
        for kc in range(8):
            kb.dma("pool", wG[:, kc, :], wsrc[:, kc, 2304:6400])
        for c in range(4):
            kb.dma("pool", wBA[:, c, :], w_br_attn[l, c * 128:(c + 1) * 128, :])
        for (wt, src) in ((wBH, w_br_hy), (wBP, w_br_pool), (wBG, w_br_gm)):
            for c in range(2):
                kb.dma("pool", wt[:, c, :], src[l, c * 128:(c + 1) * 128, :])
        for kc in range(8):
            kb.dma("pool", wO[:, kc, :], w_out[l, kc * 128:(kc + 1) * 128, :])
        xt = kb.tile("xt", [128, 8, TT], F32)
        sq = kb.tile("sq", [128, 8, TT], BF16)
        srt = kb.tile("srt", [128, TT], F32)
        rstd = kb.tile("rstd", [128, TT], F32)
        h_r = kb.ring("h", 2, [128, 8, TT], BF16)
        at = kb.tile("at", [128, 4, TT], BF16)
        hyt_r = kb.ring("hyt", 2, [128, 2, TT], BF16)
        plt_r = kb.ring("plt", 2, [128, 2, TT], BF16)
        gmt_r = kb.ring("gmt", 2, [128, 2, TT], BF16)
        g_r = kb.ring("g", 2, [128, TT], BF16)
        tmp_r = kb.ring("tmp", 2, [128, TT], F32)
        acc = kb.tile("acc", [128, TT], F32)
        mg = kb.tile("mg", [128, 8, TT], BF16)
        xr_r = kb.ring("xr", 2, [128, TT], F32)
        xo_r = kb.ring("xo", 2, [128, TT], F32)
        ps_g = [kb.psum(f"ps_g{i}", [128, 512]) for i in range(2)]
        ps_b = [kb.psum(f"ps_b{i}", [128, 512]) for i in range(2)]
        ps_o = [kb.psum(f"ps_o{i}", [128, 512]) for i in range(2)]
        ps_ss = kb.psum("ps_ss", [128, 512])
        tiles = [(st, i) for st in streams for i in range(st.S // min(TT, st.S))]
        cnt = {"h": 0, "n": 0, "o": 0}
        hmap = {}

        def prep(k):
            st, i = tiles[k]
            T_ = min(TT, st.S)
            h = h_r[cnt["h"] % 2]
            cnt["h"] += 1
            prep_C(st, i, T_, 1, xt, sq, srt, rstd, h, ps_ss)
            hmap[k] = h

        prep(0)
        for k, (st, i) in enumerate(tiles):
            T_ = min(TT, st.S)
            t0 = i * T_
            h = hmap.pop(k)
            hyt, plt, gmt = hyt_r[k % 2], plt_r[k % 2], gmt_r[k % 2]
            for (dst, src) in ((hyt, st.hyT), (plt, st.plT), (gmt, st.gmT)):
                kb.dma("sp", dst[:, :, :T_], src.rearrange("(c p) t -> p c t", p=128)[:, :, t0:t0 + T_])
            kb.dma("sp", at[:, :, :T_], st.attnT[:, :, t0:t0 + T_])
            branches = ((1, wBH, hyt, 2, 128), (2, wBP, plt, 2, 128), (3, wBG, gmt, 2, 128), (0, wBA, at, 4, 128))
            for m in range(8):
                if m == 4 and k + 1 < len(tiles):
                    prep(k + 1)
                ms = slice(m * 128, (m + 1) * 128)
                for bi, (gi, wB, src, nk, kp) in enumerate(branches):
                    pg = ps_g[cnt["n"] % 2]
                    pb = ps_b[cnt["n"] % 2]
                    g = g_r[cnt["n"] % 2]
                    tmp = tmp_r[cnt["n"] % 2]
                    cnt["n"] += 1
                    for kc in range(8):
                        kb.mm(pg[:, :T_], wG[:, kc, gi * 1024 + m * 128:gi * 1024 + (m + 1) * 128], h[:, kc, :T_],
                              start=(kc == 0), stop=(kc == 7))
                    for c in range(nk):
                        kb.mm(pb[:, :T_], wB[0:kp, c, ms], src[0:kp, c, :T_], start=(c == 0), stop=(c == nk - 1))
                    kb.act(g[:, :T_], pg[:, :T_], AF.Sigmoid)
                    if bi == 0:
                        kb.tt("dve", acc[:, :T_], g[:, :T_], pb[:, :T_], ALU.mult)
                    else:
                        kb.tt("dve", tmp[:, :T_], g[:, :T_], pb[:, :T_], ALU.mult)
                        dst = acc[:, :T_] if bi < 3 else mg[:, m, :T_]
                        kb.tt("dve", dst, acc[:, :T_], tmp[:, :T_], ALU.add)
            xsrc = st.x.rearrange("(c p) t -> p c t", p=128)
            xdst = st.xmid.rearrange("(c p) t -> p c t", p=128)
            for m in range(8):
                po = ps_o[cnt["o"] % 2]
                xr = xr_r[cnt["o"] % 2]
                xo = xo_r[cnt["o"] % 2]
                cnt["o"] += 1
                kb.dma("sp", xr[:, :T_], xsrc[:, m, t0:t0 + T_])
                for kc in range(8):
                    kb.mm(po[:, :T_], wO[:, kc, m * 128:(m + 1) * 128], mg[:, kc, :T_], start=(kc == 0), stop=(kc == 7))
                kb.stt(xo[:, :T_], po[:, :T_], modp[:, st.si, 2, m:m + 1], xr[:, :T_], ALU.mult, ALU.add)
                kb.dma(STQ, xdst[:, m, t0:t0 + T_], xo[:, :T_])
        kb.pop()

    def phase_C2(l, streams, TT, final):
        kb.push()
        wg = kb.tile("wg", [128, 8, HID], BF16, multi=True)
        wu = kb.tile("wu", [128, 8, HID], BF16, multi=True)
        wd = kb.tile("wd", [128, 22, D], BF16, multi=True)
        for kc in range(8):
            kb.dma("pool", wg[:, kc, :], ffn_g[l, kc * 128:(kc + 1) * 128, :])
            kb.dma("pool", wu[:, kc, :], ffn_u[l, kc * 128:(kc + 1) * 128, :])
        for j in range(22):
            kb.dma("pool", wd[:, j, :], ffn_d[l, j * 128:(j + 1) * 128, :])
        xt = kb.tile("xt", [128, 8, TT], F32)
        sq = kb.tile("sq", [128, 8, TT], BF16)
        srt = kb.tile("srt", [128, TT], F32)
        rstd = kb.tile("rstd", [128, TT], F32)
        h_r = kb.ring("h", 2, [128, 8, TT], BF16)
        a_t = kb.tile("a", [128, 22, TT], BF16)
        sl_r = kb.ring("sl", 2, [128, TT], F32)
        xr_r = kb.ring("xr", 2, [128, TT], F32)
        xo_r = kb.ring("xo", 2, [128, TT], F32)
        if final:
            xf = kb.tile("xf", [128, 8, TT], F32)
            fsq = kb.tile("fsq", [128, 8, TT], BF16)
            fnw = sp_col("final_norm_w")
        ps_g = [kb.psum(f"ps_g{i}", [128, 512]) for i in range(2)]
        ps_u = [kb.psum(f"ps_u{i}", [128, 512]) for i in range(2)]
        ps_o = [kb.psum(f"ps_o{i}", [128, 512]) for i in range(2)]
        ps_ss = kb.psum("ps_ss", [128, 512])
        tiles = [(st, i) for st in streams for i in range(st.S // min(TT, st.S))]
        cnt = {"h": 0, "n": 0, "o": 0}
        hmap = {}

        def prep(k):
            st, i = tiles[k]
            T_ = min(TT, st.S)
            h = h_r[cnt["h"] % 2]
            cnt["h"] += 1
            prep_C(st, i, T_, 2, xt, sq, srt, rstd, h, ps_ss)
            hmap[k] = h

        prep(0)
        for k, (st, i) in enumerate(tiles):
            T_ = min(TT, st.S)
            t0 = i * T_
            h = hmap.pop(k)
            for j in range(22):
                if j == 12 and k + 1 < len(tiles):
                    prep(k + 1)
                pg = ps_g[cnt["n"] % 2]
                pu = ps_u[cnt["n"] % 2]
                sl = sl_r[cnt["n"] % 2]
                cnt["n"] += 1
                for kc in range(8):
                    kb.mm(pg[:, :T_], wg[:, kc, j * 128:(j + 1) * 128], h[:, kc, :T_], start=(kc == 0), stop=(kc == 7))
                for kc in range(8):
                    kb.mm(pu[:, :T_], wu[:, kc, j * 128:(j + 1) * 128], h[:, kc, :T_], start=(kc == 0), stop=(kc == 7))
                kb.act(sl[:, :T_], pg[:, :T_], AF.Silu)
                kb.tt("dve", a_t[:, j, :T_], sl[:, :T_], pu[:, :T_], ALU.mult)
            xsrc = st.x.rearrange("(c p) t -> p c t", p=128)
            xdst = st.xnext.rearrange("(c p) t -> p c t", p=128)
            dofinal = final and st.name == "lat"
            for m in range(8):
                po = ps_o[cnt["o"] % 2]
                xr = xr_r[cnt["o"] % 2]
                xo = xo_r[cnt["o"] % 2]
                cnt["o"] += 1
                kb.dma("sp", xr[:, :T_], xsrc[:, m, t0:t0 + T_])
                for j in range(22):
                    kb.mm(po[:, :T_], wd[:, j, m * 128:(m + 1) * 128], a_t[:, j, :T_], start=(j == 0), stop=(j == 21))
                if dofinal:
                    kb.stt(xf[:, m, :T_], po[:, :T_], modp[:, st.si, 5, m:m + 1], xr[:, :T_], ALU.mult, ALU.add)
                else:
                    kb.stt(xo[:, :T_], po[:, :T_], modp[:, st.si, 5, m:m + 1], xr[:, :T_], ALU.mult, ALU.add)
                    kb.dma(STQ, xdst[:, m, t0:t0 + T_], xo[:, :T_])
            if dofinal:
                kb.act(fsq[:, :, :T_], xf[:, :, :T_], AF.Square)
                for c in range(8):
                    kb.mm(ps_ss[:, :T_], ones1024, fsq[:, c, :T_], start=(c == 0), stop=(c == 7))
                kb.act(srt[:, :T_], ps_ss[:, :T_], AF.Ln, bias=epst)
                kb.act(rstd[:, :T_], srt[:, :T_], AF.Exp, scale=-0.5)
                kb.tt("dve", xf[:, :, :T_], xf[:, :, :T_], rstd[:, :T_].unsqueeze(1).bcast([128, 8, T_]), ALU.mult)
                odst = outT.rearrange("(c p) t -> p c t", p=128)
                for c in range(8):
                    xo = xo_r[cnt["o"] % 2]
                    cnt["o"] += 1
                    kb.act(xo[:, :T_], xf[:, c, :T_], AF.Identity, scale=fnw[:, c:c + 1])
                    kb.dma(STQ, odst[:, c, t0:t0 + T_], xo[:, :T_])
        kb.pop()

    TC1 = 512
    TC2 = 256
    for l in range(nlayers):
        last = (l == DEPTH - 1)
        streams = [lat] if last else [cxs, lat]
        lat.xmid, cxs.xmid = xa, ca
        lat.xnext, cxs.xnext = xb, cb
        phase_M(l)
        if stop_phase == ("M", l):
            break
        phase_A(l)
        if stop_phase == ("A", l):
            break
        if "H" not in skip:
            for st in streams:
                phase_H(l, st)
        if stop_phase == ("H", l):
            break
        phase_B(l, streams)
        if stop_phase == ("B", l):
            break
        phase_C1(l, streams, TC1)
        for st in streams:
            st.x = st.xmid
        if stop_phase == ("C1", l):
            break
        phase_C2(l, streams, TC2, final=last)
        for st in streams:
            st.x = st.xnext
        if stop_phase == ("C2", l):
            break

    kb.barrier()
    return nc


SP_SPEC = [("norm1_w", 16), ("norm2_w", 16), ("b_mod", 96), ("final_norm_w", 8), ("q_norm_w", 2), ("k_norm_w", 2),
           ("hy_conv_w", 36), ("hy_conv_b", 12), ("pool_scale", 4), ("gm_bs", 8), ("hy_b1", 2), ("hy_freq", 2),
           ("hy_b2", 4), ("hy_decay", 16)]
SP_OFF = {}
_o = 0
for _n, _c in SP_SPEC:
    SP_OFF[_n] = (_o, _c)
    _o += _c
NSP = _o


def _pack_small(inp):
    sp = np.zeros((128, NSP), np.float32)

    def put(name, arr):
        o, n = SP_OFF[name]
        assert arr.shape == (128, n), (name, arr.shape, n)
        sp[:, o:o + n] = arr

    def fm(a, nch):
        L = a.shape[0]
        return a.reshape(L, nch, 128).transpose(2, 0, 1).reshape(128, L * nch)

    put("norm1_w", fm(inp["norm1_w"], 8))
    put("norm2_w", fm(inp["norm2_w"], 8))
    put("b_mod", fm(inp["b_mod"], 48))
    put("final_norm_w", inp["final_norm_w"].reshape(8, 128).T)
    put("q_norm_w", np.tile(inp["q_norm_w"].T, (2, 1)))
    put("k_norm_w", np.tile(inp["k_norm_w"].T, (2, 1)))
    put("hy_conv_w", inp["hy_conv_w"].reshape(DEPTH, 3, 6, 128).transpose(3, 0, 1, 2).reshape(128, DEPTH * 18))
    put("hy_conv_b", fm(inp["hy_conv_b"], 6))
    put("pool_scale", fm(inp["pool_scale"], 2))
    put("gm_bs", inp["gm_bs"].transpose(2, 0, 1).reshape(128, DEPTH * 4))
    h64 = lambda a: np.concatenate([a, np.zeros_like(a)], axis=0)
    put("hy_b1", h64(inp["hy_b1"].T))
    put("hy_freq", h64(inp["hy_freq"].T))
    put("hy_b2", h64(inp["hy_b2"].transpose(2, 0, 1).reshape(64, DEPTH * 2)))
    put("hy_decay", fm(inp["hy_decay"], 8))
    return sp


_CONST_CACHE = {}


def _consts():
    if _CONST_CACHE:
        return _CONST_CACHE
    cbf = np.zeros((128, 3, 128), np.float32)
    cbf[:, 0, :] = np.eye(128)
    k = np.arange(128)
    cbf[k, 1, k ^ 1] = 1.0
    cosF, sinS = _rope_tables()
    _CONST_CACHE.update(dict(
        cbf=_bf(cbf), ropeT=np.stack([cosF, sinS]).astype(np.float32),
        invcnt_l=_pool_invcnt(SEQ), invcnt_c=_pool_invcnt(CTX)))
    for nm, n in (("lat", SEQ), ("ctx", CTX)):
        zz, tr = _hy_tables(n)
        tf, tb = _dft_tables(2 * n // 128)
        _CONST_CACHE["zpos_" + nm] = zz
        _CONST_CACHE["trow_" + nm] = tr
        _CONST_CACHE["dftf_" + nm] = tf
        _CONST_CACHE["dftb_" + nm] = tb
    return _CONST_CACHE


def make_in_maps(inp):
    inp = {k: np.asarray(v) for k, v in inp.items()}
    shared = dict(_consts())
    for k in ("w_mod", "w_in", "w_br_attn", "w_br_hyena", "w_br_pool", "w_br_gmlp", "w_out", "ffn_w_gate",
              "ffn_w_up", "ffn_w_down", "gm_norm_w", "pool_w", "hy_w1", "hy_w2", "hy_w3", "hy_skip"):
        shared[k] = np.ascontiguousarray(inp[k], dtype=np.float32)
    shared["smallp"] = _pack_small(inp)
    shared["gm_wsT"] = np.ascontiguousarray(inp["gm_ws"].transpose(0, 1, 3, 2))
    shared["hy_skip"] = np.ascontiguousarray(inp["hy_skip"].reshape(DEPTH, 512), dtype=np.float32)
    maps = []
    for b in range(NCORE):
        m = dict(shared)
        m["xT"] = np.ascontiguousarray(inp["x"][b].T)
        m["ctxT"] = np.ascontiguousarray(inp["ctx"][b].T)
        cv = np.stack([inp["c"][b].reshape(8, 128).T, inp["c_ctx"].reshape(8, 128).T], axis=-1)
        m["cvec"] = np.ascontiguousarray(cv, dtype=np.float32)
        maps.append(m)
    return maps


def kernel(**inputs):
    nc = build_program()
    maps = make_in_maps(inputs)
    res = run_bass_kernel_spmd(nc, maps, core_ids=list(range(NCORE)))
    out = np.stack([np.ascontiguousarray(r["outT"].T) for r in res.results], axis=0)
    return out.astype(np.float32)
```

```python
import numpy as np
import ml_dtypes
from contextlib import ExitStack
import concourse.bass as bass
import concourse.mybir as mybir
from concourse.bass_utils import run_bass_kernel_spmd

F32 = mybir.dt.float32
BF16 = mybir.dt.bfloat16
AF = mybir.ActivationFunctionType
ALU = mybir.AluOpType

D = 1024
SEQ = 8192
CTX = 256
DEPTH = 2
NCORE = 8
HID = 2816
EPS = 1e-6
NKEY = CTX + SEQ
PADC = 8
STQ = "pool"


class Res:
    __slots__ = ("name", "readers", "writers", "multi", "sem")

    def __init__(self, name, multi=False):
        self.name = name
        self.readers = {}
        self.writers = {}
        self.multi = multi
        self.sem = None


class V:
    __slots__ = ("ap", "res", "sb")

    def __init__(self, ap, res, sb=True):
        self.ap = ap
        self.res = res
        self.sb = sb

    def __getitem__(self, k):
        return V(self.ap[k], self.res, self.sb)

    def rearrange(self, s, **kw):
        return V(self.ap.rearrange(s, **kw), self.res, self.sb)

    def bcast(self, shape):
        return V(self.ap.broadcast_to(list(shape)), self.res, self.sb)

    def unsqueeze(self, a):
        return V(self.ap.unsqueeze(a), self.res, self.sb)

    def pbcast(self, n):
        return V(self.ap.partition_broadcast(n), self.res, self.sb)

    def wr(self, res):
        return V(self.ap, res, self.sb)


class KB:
    def __init__(self, nc):
        self.nc = nc
        self.eng = {"pe": nc.tensor, "act": nc.scalar, "dve": nc.vector, "pool": nc.gpsimd, "sp": nc.sync}
        self.psem = {}
        self.pcnt = {}
        self.root = ExitStack()
        for e in ("pe", "act", "dve", "pool"):
            self.psem[e] = self.root.enter_context(nc.semaphore(f"prog_{e}"))
            self.pcnt[e] = 0
        self.waited = {e: {} for e in self.eng}
        self.dma_all = []
        self.dma_free = []
        self.phase_res = []
        self.stacks = [self.root]
        self.uid = 0
        self.dram_res = {}

    def _name(self, n):
        self.uid += 1
        return f"{n}_{self.uid}"

    def tile(self, name, shape, dt, multi=False):
        t = self.stacks[-1].enter_context(self.nc.sbuf_tensor(self._name(name), list(shape), dt))
        r = Res(name, multi)
        self.phase_res.append(r)
        return V(t[tuple(slice(None) for _ in shape)], r, True)

    def ring(self, name, n, shape, dt, multi=False):
        return [self.tile(f"{name}{i}", shape, dt, multi) for i in range(n)]

    def psum(self, name, shape, dt=F32):
        t = self.stacks[-1].enter_context(self.nc.psum_tensor(self._name(name), list(shape), dt))
        r = Res(name)
        self.phase_res.append(r)
        return V(t[tuple(slice(None) for _ in shape)], r, True)

    def dram(self, name, shape, dt, kind="Internal"):
        t = self.nc.dram_tensor(name, list(shape), dt, kind=kind)
        return V(t.ap(), Res(name, True), False)

    def dres(self, v, key):
        k = (v.res.name, key)
        if k not in self.dram_res:
            self.dram_res[k] = Res(f"{v.res.name}:{key}", True)
        return v.wr(self.dram_res[k])

    def push(self):
        self.stacks.append(ExitStack())
        self.phase_res = []

    def pop(self):
        self.barrier()
        for r in self.phase_res:
            if r.sem is not None:
                if "hw" in r.sem:
                    self.dma_free.append(r.sem["hw"])
                r.sem = None
        self.phase_res = []
        self.stacks.pop().close()

    def _deps(self, outs, ins):
        d = {}

        def add(tokd):
            for k, (s, c) in tokd.items():
                if k not in d or d[k][1] < c:
                    d[k] = (s, c)

        for v in ins:
            add(v.res.writers)
        for v in outs:
            add(v.res.readers)
            if not v.res.multi:
                add(v.res.writers)
        return d

    def _wait(self, e, deps, skip_own=True):
        own = self.psem[e].name if (skip_own and e == "pe") else None
        w = self.waited[e]
        for k, (s, c) in deps.items():
            if k == own:
                continue
            if w.get(k, 0) >= c:
                continue
            self.eng[e].wait_ge(s, c)
            w[k] = c

    def _commit(self, outs, ins, key, tok):
        for v in ins:
            r = v.res.readers
            if key not in r or r[key][1] < tok[1]:
                r[key] = tok
        for v in outs:
            if v.res.multi:
                v.res.writers[key] = tok
                v.res.readers = {}
            else:
                v.res.writers = {key: tok}
                v.res.readers = {}

    def op(self, e, fn, outs, ins):
        outs = [o for o in outs if isinstance(o, V)]
        ins = [i for i in ins if isinstance(i, V)]
        self._wait(e, self._deps(outs, ins))
        inst = fn()
        self.pcnt[e] += 1
        inst.then_inc(self.psem[e], 1)
        self._commit(outs, ins, self.psem[e].name, (self.psem[e], self.pcnt[e]))
        return inst

    def dma(self, q, out, in_, extra_in=(), extra_out=()):
        sbv = out if out.sb else in_
        outs = [out] + list(extra_out)
        ins = [in_] + list(extra_in)
        self._wait(q, self._deps(outs, ins), skip_own=False)
        res = sbv.res
        qt = "sw" if q == "pool" else "hw"
        if res.sem is None:
            res.sem = {}
        if qt not in res.sem:
            if qt == "hw" and self.dma_free:
                res.sem[qt] = self.dma_free.pop()
            else:
                s = self.root.enter_context(self.nc.semaphore(self._name("dma" + qt)))
                res.sem[qt] = [s, 0]
                self.dma_all.append(res.sem[qt])
        sm = res.sem[qt]
        inst = self.eng[q].dma_start(out=out.ap, in_=in_.ap)
        sm[1] += 16
        inst.then_inc(sm[0], 16)
        self._commit(outs, ins, sm[0].name, (sm[0], sm[1]))
        return inst

    def barrier(self):
        toks = {}
        for e in self.psem:
            toks[self.psem[e].name] = (self.psem[e], self.pcnt[e])
        for s in self.dma_all:
            if s[1] > 0:
                toks[s[0].name] = (s[0], s[1])
        for e in self.eng:
            self._wait(e, toks, skip_own=True)

    @staticmethod
    def _a(x):
        return x.ap if isinstance(x, V) else x

    def mm(self, out, lhsT, rhs, start=True, stop=True):
        return self.op("pe", lambda: self.nc.tensor.matmul(out.ap, lhsT=lhsT.ap, rhs=rhs.ap, start=start, stop=stop),
                       [out], [lhsT, rhs])

    def transpose(self, out, in_, ident):
        return self.op("pe", lambda: self.nc.tensor.transpose(out.ap, in_.ap, ident.ap), [out], [in_, ident])

    def act(self, out, in_, func, bias=0.0, scale=1.0, accum=None):
        a = self._a
        kw = {}
        if accum is not None:
            kw["accum_out"] = accum.ap
        return self.op("act", lambda: self.nc.scalar.activation(out=out.ap, in_=in_.ap, func=func, bias=a(bias),
                                                                scale=a(scale), **kw),
                       [out, accum], [in_, bias, scale])

    def tt(self, e, out, in0, in1, op):
        return self.op(e, lambda: self.eng[e].tensor_tensor(out=out.ap, in0=in0.ap, in1=in1.ap, op=op),
                       [out], [in0, in1])

    def ts(self, e, out, in0, s1, op0, s2=None, op1=None):
        a = self._a
        if op1 is None:
            f = lambda: self.eng[e].tensor_scalar(out=out.ap, in0=in0.ap, scalar1=a(s1), scalar2=None, op0=op0)
        else:
            f = lambda: self.eng[e].tensor_scalar(out=out.ap, in0=in0.ap, scalar1=a(s1), scalar2=a(s2), op0=op0,
                                                  op1=op1)
        return self.op(e, f, [out], [in0, s1, s2])

    def stt(self, out, in0, scalar, in1, op0, op1):
        a = self._a
        return self.op("dve", lambda: self.nc.vector.scalar_tensor_tensor(out=out.ap, in0=in0.ap, scalar=a(scalar),
                                                                          in1=in1.ap, op0=op0, op1=op1),
                       [out], [in0, scalar, in1])

    def copy(self, e, out, in_):
        if e == "act":
            return self.op("act", lambda: self.nc.scalar.copy(out=out.ap, in_=in_.ap), [out], [in_])
        return self.op(e, lambda: self.eng[e].tensor_copy(out=out.ap, in_=in_.ap), [out], [in_])

    def memset(self, e, out, val):
        return self.op(e, lambda: self.eng[e].memset(out.ap, val), [out], [])

    def recip(self, out, in_):
        return self.op("dve", lambda: self.nc.vector.reciprocal(out=out.ap, in_=in_.ap), [out], [in_])


def _bf(a):
    return np.asarray(a, dtype=np.float32).astype(ml_dtypes.bfloat16)


def _rope_tables():
    rows = SEQ // 64
    r, col = np.meshgrid(np.arange(rows, dtype=np.float32), np.arange(64, dtype=np.float32), indexing="ij")
    inv = (np.float32(10000.0) ** (-np.arange(0, 32, 2, dtype=np.float32) / np.float32(32))).astype(np.float32)
    ang = np.concatenate([r.reshape(-1, 1) * inv, col.reshape(-1, 1) * inv], axis=-1).astype(np.float32)
    cos = np.cos(ang).astype(np.float32)
    sin = np.sin(ang).astype(np.float32)
    p = np.arange(128)
    i = (p % 64) // 2
    cosF = cos[:, i].T.copy()
    sgn = np.where(p % 2 == 0, -1.0, 1.0).astype(np.float32)
    sinS = (sin[:, i].T * sgn[:, None]).astype(np.float32)
    return np.ascontiguousarray(cosF), np.ascontiguousarray(sinS)


def _hy_pos(n):
    pos = np.arange(n, dtype=np.float32)
    t = (pos / np.float32(n - 1)).astype(np.float32)
    bands = np.linspace(1e-4, 15, 16, dtype=np.float32)
    ang = (np.float32(2.0 * np.pi / n) * pos[:, None] * bands).astype(np.float32)
    z = np.concatenate([t[:, None], np.cos(ang), np.sin(ang)], axis=-1).astype(np.float32)
    return z, t


def _pool_invcnt(n):
    out = np.zeros((128, 2, n), np.float32)
    pos = np.arange(n)
    for g, win in enumerate((2, 4, 8, 16)):
        lo = np.clip(pos - win // 2, 0, n)
        hi = np.clip(pos + win - win // 2, 0, n)
        ic = (1.0 / (hi - lo).astype(np.float32)).astype(np.float32)
        ch, half = g // 2, g % 2
        out[half * 64:(half + 1) * 64, ch, :] = ic[None, :]
    return out


def _hy_tables(n):
    z, t = _hy_pos(n)
    zf = z.T.copy()
    zr = np.empty_like(zf)
    zr[:, 0] = zf[:, 0]
    zr[:, 1:] = zf[:, :0:-1]
    tr = np.zeros((2, n), np.float32)
    tr[0] = t
    tr[1, 1:] = t[:0:-1]
    return np.stack([zf, zr]).astype(np.float32), tr


def _dft_tables(n1):
    N = n1 * 128
    nh = n1 // 2
    a = np.arange(128, dtype=np.float64)
    b = np.arange(n1, dtype=np.float64)
    C = np.cos(2 * np.pi * np.outer(a, a) / 128.0)
    S_ = np.sin(2 * np.pi * np.outer(a, a) / 128.0)
    C1 = np.cos(2 * np.pi * np.outer(b, b) / n1)
    S1 = np.sin(2 * np.pi * np.outer(b, b) / n1)
    twc = np.cos(2 * np.pi * np.outer(a, b) / N)
    tws = np.sin(2 * np.pi * np.outer(a, b) / N)
    FC = 4 * n1 + 512
    tf = np.zeros((128, FC), np.float32)
    tf[:, 0:n1] = twc
    tf[:, n1:2 * n1] = twc
    tf[:, 2 * n1:3 * n1] = tws
    tf[:, 3 * n1:4 * n1] = -tws
    o = 4 * n1
    tf[0:n1, o:o + 128] = twc.T
    tf[0:n1, o + 128:o + 256] = twc.T
    tf[0:n1, o + 256:o + 384] = tws.T
    tf[0:n1, o + 384:o + 512] = -tws.T
    BC = 2 * n1 + 128 * 3 + 512 + 2 * nh
    tb = np.zeros((128, BC), np.float32)
    tb[0:n1, 0:n1] = C1
    tb[0:n1, n1:2 * n1] = -S1
    o = 2 * n1
    tb[:, o:o + 128] = C
    tb[:, o + 128:o + 256] = S_
    tb[:, o + 256:o + 384] = -S_
    o += 384
    tb[:, o:o + 128] = C
    tb[:, o + 128:o + 256] = S_
    tb[:, o + 256:o + 384] = -S_
    tb[:, o + 384:o + 512] = C
    o += 512
    tb[0:n1, o:o + nh] = C1[:, 0:nh] / N
    tb[0:n1, o + nh:o + 2 * nh] = -S1[:, 0:nh] / N
    return tf, _bf(tb)


class Stream:
    pass


def build_program(dbg=None, nlayers=DEPTH, stop_phase=None, skip=()):
    nc = bass.Bass("TRN2", target_bir_lowering=False)
    kb = KB(nc)
    dbg = dbg or []

    def din(name, shape, dt=F32):
        return kb.dram(name, shape, dt, kind="ExternalInput")

    def dsc(name, shape, dt):
        return kb.dram(name, shape, dt, kind=("ExternalOutput" if name in dbg else "Internal"))

    xT_in = din("xT", [D, SEQ])
    ctxT_in = din("ctxT", [D, CTX])
    cvec = din("cvec", [128, 8, 2])
    w_mod = din("w_mod", [DEPTH, D, 6 * D])
    w_in = din("w_in", [DEPTH, D, 6400])
    w_br_attn = din("w_br_attn", [DEPTH, 512, D])
    w_br_hy = din("w_br_hyena", [DEPTH, 256, D])
    w_br_pool = din("w_br_pool", [DEPTH, 256, D])
    w_br_gm = din("w_br_gmlp", [DEPTH, 256, D])
    w_out = din("w_out", [DEPTH, D, D])
    ffn_g = din("ffn_w_gate", [DEPTH, D, HID])
    ffn_u = din("ffn_w_up", [DEPTH, D, HID])
    ffn_d = din("ffn_w_down", [DEPTH, HID, D])
    smallp_d = din("smallp", [128, NSP])
    gm_wsT = din("gm_wsT", [DEPTH, 4, 128, 128])
    gm_nw = din("gm_norm_w", [DEPTH, 256])
    pool_w = din("pool_w", [DEPTH, 4, 64, 64])
    hy_w1 = din("hy_w1", [DEPTH, 33, 64])
    hy_w2 = din("hy_w2", [DEPTH, 2, 64, 64])
    hy_w3 = din("hy_w3", [DEPTH, 64, 1024])
    hy_skip = din("hy_skip", [DEPTH, 512])
    cbf = din("cbf", [128, 3, 128], BF16)
    ropeT = din("ropeT", [2, 128, SEQ])
    invcnt_l = din("invcnt_l", [128, 2, SEQ])
    invcnt_c = din("invcnt_c", [128, 2, CTX])
    outT = kb.dram("outT", [D, SEQ], F32, kind="ExternalOutput")

    xa = dsc("xa", [D, SEQ], F32)
    xb = dsc("xb", [D, SEQ], F32)
    ca = dsc("ca", [D, CTX], F32)
    cb = dsc("cb", [D, CTX], F32)
    kT2d = dsc("kT2d", [128, 2, NKEY], BF16)
    vaugd = dsc("vaugd", [NKEY // 128, 128, 2, 66], BF16)

    def mk_stream(name, S, T, xin, koff):
        st = Stream()
        st.name, st.S, st.T, st.NT, st.koff = name, S, T, S // T, koff
        st.x = xin
        st.qT = dsc(f"qT_{name}", [128, 4, S], BF16)
        st.pT = dsc(f"pT_{name}", [D, S + 2 * PADC], BF16)
        st.zT = dsc(f"zT_{name}", [768, S], BF16)
        st.plT = dsc(f"plT_{name}", [256, S], BF16)
        st.gmT = dsc(f"gmT_{name}", [256, S], BF16)
        st.hyT = dsc(f"hyT_{name}", [256, S], BF16)
        st.attnT = dsc(f"attnT_{name}", [128, 4, S], BF16)
        st.rope = (name == "lat")
        st.si = 0 if name == "lat" else 1
        st.invcnt = invcnt_l if name == "lat" else invcnt_c
        n1 = 2 * S // 128
        st.zpos = din(f"zpos_{name}", [2, 33, S])
        st.trow = din(f"trow_{name}", [2, S])
        st.dftf = din(f"dftf_{name}", [128, 4 * n1 + 512])
        st.dftb = din(f"dftb_{name}", [128, 2 * n1 + 384 + 512 + n1], BF16)
        st.kfT = dsc(f"kfT_{name}", [512, 2 * S], BF16)
        return st

    lat = mk_stream("lat", SEQ, 512, xT_in, CTX)
    cxs = mk_stream("ctx", CTX, 256, ctxT_in, 0)

    identb = kb.tile("identb", [128, 128], BF16)
    permb = kb.tile("permb", [128, 128], BF16)
    ones1024 = kb.tile("ones1024", [128, 128], BF16)
    blk64 = kb.tile("blk64", [128, 128], BF16)
    onesf = kb.tile("onesf", [128, 64], F32)
    epst = kb.tile("epst", [128, 1], F32)
    smallp = kb.tile("smallp", [128, NSP], F32)
    modp = kb.tile("modp", [128, 2, 6, 8], F32)
    actbf = kb.tile("actbf", [128, 8, 2], BF16)
    zero_bf = kb.tile("zero_bf", [128, 8, PADC], BF16)

    kb.dma("sp", identb, cbf[:, 0, :])
    kb.dma("sp", permb, cbf[:, 1, :])
    kb.dma("sp", smallp, smallp_d)
    kb.memset("dve", ones1024, 1.0 / 1024.0)
    kb.memset("dve", blk64, 0.0)
    kb.memset("dve", blk64[0:64, 0:64], 1.0 / 64.0)
    kb.memset("dve", blk64[64:128, 64:128], 1.0 / 64.0)
    kb.memset("dve", onesf, 1.0)
    kb.memset("dve", epst, EPS)
    kb.memset("dve", zero_bf, 0.0)
    for st in (lat, cxs):
        pv = st.pT.rearrange("(c p) t -> p c t", p=128)
        kb.dma("sp", kb.dres(pv[:, :, 0:PADC], "padlo"), zero_bf)
        kb.dma("sp", kb.dres(pv[:, :, PADC + st.S:2 * PADC + st.S], "padhi"), zero_bf)
    kb.push()
    cv = kb.tile("cv", [128, 8, 2], F32)
    kb.dma("sp", cv, cvec)
    kb.act(actbf, cv, AF.Silu)
    kb.pop()

    def sp_col(name, l=None):
        o, n = SP_OFF[name]
        if l is not None:
            per = n // DEPTH
            return smallp[:, o + l * per:o + (l + 1) * per]
        return smallp[:, o:o + n]

    def phase_M(l):
        kb.push()
        wm = kb.ring("wm", 2, [128, 8, 1536], BF16)
        pm = kb.psum("pm", [128, 48, 2])
        modt = kb.tile("modt", [128, 48, 2], F32)
        wsrc = w_mod[l].rearrange("(kc p) n -> p kc n", p=128)
        for blk in range(4):
            w = wm[blk % 2]
            for kc in range(8):
                kb.dma("pool", w[:, kc, :], wsrc[:, kc, blk * 1536:(blk + 1) * 1536])
            for j in range(12):
                oc = blk * 12 + j
                for kc in range(8):
                    kb.mm(pm[:, oc, :], w[:, kc, j * 128:(j + 1) * 128], actbf[:, kc, :], start=(kc == 0),
                          stop=(kc == 7))
        bm = sp_col("b_mod", l)
        kb.tt("dve", modt, pm, bm.unsqueeze(2).bcast([128, 48, 2]), ALU.add)
        n1 = sp_col("norm1_w", l)
        n2 = sp_col("norm2_w", l)
        for s in range(2):
            kb.stt(modp[:, s, 0, :], modt[:, 8:16, s], 1.0, n1, ALU.add, ALU.mult)
            kb.copy("dve", modp[:, s, 1, :], modt[:, 0:8, s])
            kb.copy("dve", modp[:, s, 2, :], modt[:, 16:24, s])
            kb.stt(modp[:, s, 3, :], modt[:, 32:40, s], 1.0, n2, ALU.add, ALU.mult)
            kb.copy("dve", modp[:, s, 4, :], modt[:, 24:32, s])
            kb.copy("dve", modp[:, s, 5, :], modt[:, 40:48, s])
        kb.pop()

    def norm_mod(st, xt, sq, h, ps_ss, srt, rstd, which, TT):
        ai = 0 if which == 1 else 3
        kb.act(sq[:, :, :TT], xt[:, :, :TT], AF.Square)
        for c in range(8):
            kb.mm(ps_ss[:, :TT], ones1024, sq[:, c, :TT], start=(c == 0), stop=(c == 7))
        kb.act(srt[:, :TT], ps_ss[:, :TT], AF.Ln, bias=epst)
        kb.act(rstd[:, :TT], srt[:, :TT], AF.Exp, scale=-0.5)
        return ai

    def norm_apply(st, xt, xs, h, rstd, ai, TT):
        kb.tt("dve", xs[:, :, :TT], xt[:, :, :TT], rstd[:, :TT].unsqueeze(1).bcast([128, 8, TT]), ALU.mult)
        for c in range(8):
            kb.act(h[:, c, :TT], xs[:, c, :TT], AF.Identity, bias=modp[:, st.si, ai + 1, c:c + 1],
                   scale=modp[:, st.si, ai, c:c + 1])

    def phase_A(l):
        kb.push()
        wA = kb.tile("wA", [128, 8, 2432], BF16, multi=True)
        wsrc = w_in[l].rearrange("(kc p) n -> p kc n", p=128)
        for (d0, s0, n) in ((0, 0, 512), (512, 512, 64), (576, 512, 64), (640, 576, 64), (704, 576, 64),
                            (768, 768, 768), (1536, 1536, 256), (1792, 640, 128), (1920, 1792, 512)):
            for kc in range(8):
                kb.dma("pool", wA[:, kc, d0:d0 + n], wsrc[:, kc, s0:s0 + n])
        poolW = kb.tile("poolW", [128, 2, 128], BF16)
        kb.memset("dve", poolW, 0.0)
        for g in range(4):
            ch, hf = g // 2, g % 2
            kb.dma("pool", poolW[hf * 64:(hf + 1) * 64, ch, hf * 64:(hf + 1) * 64], pool_w[l, g])
        wsT = kb.tile("wsT", [128, 4, 128], BF16, multi=True)
        for g in range(4):
            kb.dma("pool", wsT[:, g, :], gm_wsT[l, g])
        gnw = kb.tile("gnw", [128, 256], F32)
        kb.dma("sp", gnw, gm_nw[l, :].pbcast(128))

        xt_r = kb.ring("xt", 1, [128, 8, 512], F32)
        sq = kb.tile("sq", [128, 8, 512], BF16)
        srt = kb.tile("srt", [128, 512], F32)
        rstd = kb.tile("rstd", [128, 512], F32)
        h_r = kb.ring("h", 2, [128, 8, 512], BF16)
        sqq_r = kb.ring("sqq", 2, [128, 512], BF16)
        srq_r = kb.ring("srq", 2, [128, 512], F32)
        rq_r = kb.ring("rq", 2, [128, 512], F32)
        qn_r = kb.ring("qn", 2, [128, 512], BF16)
        t1_r = kb.ring("t1", 2, [128, 512], F32)
        t2_r = kb.ring("t2", 2, [128, 512], F32)
        rope_r = kb.ring("rope", 2, [128, 2, 512], F32)
        qst_r = kb.ring("qst", 1, [128, 4, 512], BF16)
        kst_r = kb.ring("kst", 2, [128, 2, 512], BF16)
        vst_r = kb.ring("vst", 2, [128, 4, 2, 66], BF16)
        pst_r = kb.ring("pst", 1, [128, 8, 512], BF16)
        usb_r = kb.ring("usb", 2, [128, 256], F32)
        vnn = kb.tile("vnn", [128, 256], F32)
        vn_r = kb.ring("vn", 2, [128, 256], BF16)
        gmt_r = kb.ring("gmt", 2, [128, 256], BF16)
        stat = kb.tile("stat", [128, 8], F32)
        junk = kb.tile("junk", [128, 256], F32)
        gmst_r = kb.ring("gmst", 2, [128, 2, 512], BF16)
        pp_r = kb.ring("pp", 2, [128, 8, 512 + 2 * PADC], BF16)
        zst_r = kb.ring("zst", 1, [128, 6, 512], BF16)
        pa = [kb.tile(f"pa{i}", [128, 2, 512 + 2 * PADC], F32) for i in range(2)]
        icn_r = kb.ring("icn", 1, [128, 2, 512], F32)
        dpl = kb.tile("dpl", [128, 2, 512], F32)
        dpb = kb.tile("dpb", [128, 2, 512], BF16)
        plst_r = kb.ring("plst", 2, [128, 2, 512], BF16)

        ps_main = [kb.psum(f"ps_main{i}", [128, 512]) for i in range(2)]
        ps_qk = [kb.psum(f"ps_qk{i}", [128, 512]) for i in range(2)]
        ps_ss = ps_qk[0]
        ps_tok = [kb.psum(f"ps_tok{i}", [128, 512]) for i in range(2)]
        ps_gs = kb.psum("ps_gs", [128, 256])
        ps_gt_t = kb.stacks[-1].enter_context(nc.psum_tensor(kb._name("ps_gt"), [128, 2, 128], BF16))
        ps_gt = V(ps_gt_t[:, :, :], Res("ps_gt"), True)

        for r in vst_r:
            kb.memset("dve", r[:, :, :, 64:66], 1.0)

        wq = sp_col("q_norm_w", l)
        wk = sp_col("k_norm_w", l)
        hw = sp_col("hy_conv_w", l)
        hb = sp_col("hy_conv_b", l)
        pscale = sp_col("pool_scale", l)
        bsT = sp_col("gm_bs", l)

        cnt = {"main": 0, "qk": 0, "tok": 0, "h": 0}
        dg = kb.tile("dg", [128, 18, 128], BF16)
        for k in range(18):
            kb.ts("dve", dg[:, k, :], identb, hw[:, k:k + 1], ALU.mult)

        hcur = {}

        def prep_A(st, i):
            TT = st.T
            t0 = i * TT
            xt = xt_r[0]
            h = h_r[cnt["h"] % 2]
            cnt["h"] += 1
            xsrc = st.x.rearrange("(c p) t -> p c t", p=128)
            kb.dma("sp", xt[:, :, :TT], kb.dres(xsrc[:, :, t0:t0 + TT], i))
            rp = None
            if st.rope:
                rp = rope_r[i % 2]
                kb.dma("sp", rp[:, :, :TT], ropeT[:, :, t0:t0 + TT].rearrange("a p t -> p a t"))
            ai = norm_mod(st, xt, sq, h, ps_ss, srt, rstd, 1, TT)
            norm_apply(st, xt, xt, h, rstd, ai, TT)
            hcur[(st.name, i)] = (h, rp)

        def tile_A(st, i, full, mid_hook=None):
            TT = st.T
            t0 = i * TT
            h, rp = hcur.pop((st.name, i))
            qst = qst_r[0]
            kst = kst_r[i % 2]
            vst = vst_r[i % 2]
            pst = pst_r[0]
            gmst = gmst_r[i % 2]
            chunks = []
            if full:
                chunks += [("q", j, j * 128) for j in range(4)]
            chunks += [("k", 0, 512), ("k", 1, 640)]
            if full:
                chunks += [("p", j, 768 + j * 128) for j in range(8)]
            pss = {}

            def main_mm(ci):
                kind, idx, c0 = chunks[ci]
                ps = ps_main[cnt["main"] % 2]
                cnt["main"] += 1
                for kc in range(8):
                    kb.mm(ps[:, :TT], wA[:, kc, c0:c0 + 128], h[:, kc, :TT], start=(kc == 0), stop=(kc == 7))
                pss[ci] = ps

            def post(ci):
                kind, idx, c0 = chunks[ci]
                ps = pss.pop(ci)
                if kind == "p":
                    kb.copy("act", pst[:, idx, :TT], ps[:, :TT])
                    return
                k2 = cnt["qk"] % 2
                sqq = sqq_r[k2]
                qn = qn_r[k2]
                srq, rq, t1, t2 = srq_r[k2], rq_r[k2], t1_r[k2], t2_r[k2]
                psq = ps_qk[0]
                psw = ps_qk[1]
                cnt["qk"] += 1
                kb.act(sqq[:, :TT], ps[:, :TT], AF.Square)
                kb.mm(psq[:, :TT], blk64, sqq[:, :TT])
                kb.act(srq[:, :TT], psq[:, :TT], AF.Ln, bias=epst)
                kb.act(rq[:, :TT], srq[:, :TT], AF.Exp, scale=-0.5)
                dst = qst[:, idx, :TT] if kind == "q" else kst[:, idx, :TT]
                wn = wq if kind == "q" else wk
                if st.rope:
                    kb.stt(qn[:, :TT], ps[:, :TT], wn[:, 0:1], rq[:, :TT], ALU.mult, ALU.mult)
                    kb.mm(psw[:, :TT], permb, qn[:, :TT])
                    kb.tt("dve", t1[:, :TT], qn[:, :TT], rp[:, 0, :TT], ALU.mult)
                    kb.tt("dve", t2[:, :TT], psw[:, :TT], rp[:, 1, :TT], ALU.mult)
                    kb.tt("dve", dst, t1[:, :TT], t2[:, :TT], ALU.add)
                else:
                    kb.stt(dst, ps[:, :TT], wn[:, 0:1], rq[:, :TT], ALU.mult, ALU.mult)

            main_mm(0)
            for ci in range(len(chunks)):
                if mid_hook is not None and ci == min(8, len(chunks) - 1):
                    mid_hook()
                if ci + 1 < len(chunks):
                    main_mm(ci + 1)
                post(ci)
            nsub = TT // 128
            for sub in range(nsub):
                ts_ = slice(sub * 128, (sub + 1) * 128)
                ps = ps_tok[cnt["tok"] % 2]
                cnt["tok"] += 1
                for kc in range(8):
                    kb.mm(ps[:, 0:128], h[:, kc, ts_], wA[:, kc, 1792:1920], start=(kc == 0), stop=(kc == 7))
                kb.copy("act", vst[:, sub, :, 0:64], ps[:, 0:128].rearrange("p (k d) -> p k d", k=2))
                if not full:
                    continue
                ps = ps_tok[cnt["tok"] % 2]
                cnt["tok"] += 1
                for kc in range(8):
                    kb.mm(ps[:, 0:512], h[:, kc, ts_], wA[:, kc, 1920:2432], start=(kc == 0), stop=(kc == 7))
                usb = usb_r[sub % 2]
                vn = vn_r[sub % 2]
                gmt = gmt_r[sub % 2]
                kb.copy("act", usb, ps[:, 0:256])
                kb.memset("dve", stat[:, 0:2], 0.0)
                kb.act(junk, ps[:, 256:512], AF.Identity, accum=stat[:, 0:1])
                kb.act(junk, ps[:, 256:512], AF.Square, accum=stat[:, 1:2])
                kb.ts("dve", stat[:, 2:3], stat[:, 0:1], 1.0 / 256.0, ALU.mult)
                kb.tt("dve", stat[:, 3:4], stat[:, 2:3], stat[:, 2:3], ALU.mult)
                kb.stt(stat[:, 4:5], stat[:, 1:2], 1.0 / 256.0, stat[:, 3:4], ALU.mult, ALU.subtract)
                kb.act(stat[:, 5:6], stat[:, 4:5], AF.Ln, bias=epst)
                kb.act(stat[:, 6:7], stat[:, 5:6], AF.Exp, scale=-0.5)
                kb.stt(stat[:, 7:8], stat[:, 2:3], -1.0, stat[:, 6:7], ALU.mult, ALU.mult)
                kb.act(vnn, ps[:, 256:512], AF.Identity, bias=stat[:, 7:8], scale=stat[:, 6:7])
                kb.tt("dve", vn, vnn, gnw, ALU.mult)
                for g in range(4):
                    gs = slice(g * 64, (g + 1) * 64)
                    kb.mm(ps_gs[:, gs], wsT[:, g, :], vn[:, gs])
                for g in range(4):
                    gs = slice(g * 64, (g + 1) * 64)
                    kb.stt(gmt[:, gs], ps_gs[:, gs], bsT[:, g:g + 1], usb[:, gs], ALU.add, ALU.mult)
                for hf in range(2):
                    kb.transpose(ps_gt[:, hf, :], gmt[:, hf * 128:(hf + 1) * 128], identb)
                kb.copy("dve", gmst[:, :, ts_], ps_gt)
            if full:
                kb.dma(STQ, kb.dres(st.qT[:, :, t0:t0 + TT], i), qst[:, :, :TT])
                pv = st.pT.rearrange("(c p) t -> p c t", p=128)
                kb.dma(STQ, kb.dres(pv[:, :, PADC + t0:PADC + t0 + TT], i), pst[:, :, :TT])
                gv = st.gmT.rearrange("(c p) t -> p c t", p=128)
                kb.dma(STQ, kb.dres(gv[:, :, t0:t0 + TT], i), gmst[:, :, :TT])
            kb.dma(STQ, kb.dres(kT2d[:, :, st.koff + t0:st.koff + t0 + TT], (st.name, i)), kst[:, :, :TT])
            c0 = (st.koff + t0) // 128
            kb.dma(STQ, kb.dres(vaugd[c0:c0 + nsub].rearrange("c p k d -> p c k d"), (st.name, i)),
                   vst[:, :nsub])

        def tile_A2(st, i):
            TT = st.T
            t0 = i * TT
            W = TT + 2 * PADC
            pp = pp_r[i % 2]
            icn = icn_r[0]
            zst = zst_r[0]
            plst = plst_r[i % 2]
            pv = st.pT.rearrange("(c p) t -> p c t", p=128)
            extra = [kb.dres(pv, k) for k in (i - 1, i + 1) if 0 <= k < st.NT]
            extra += [kb.dres(pv, "padlo"), kb.dres(pv, "padhi")]
            kb.dma("sp", pp[:, :, :W], kb.dres(pv[:, :, t0:t0 + W], i), extra_in=extra)
            kb.dma("sp", icn[:, :, :TT], st.invcnt[:, :, t0:t0 + TT])
            for c in range(6):
                ps = ps_main[cnt["main"] % 2]
                cnt["main"] += 1
                for tap in range(3):
                    kb.mm(ps[:, :TT], dg[:, tap * 6 + c, :], pp[:, c, PADC - 1 + tap:PADC - 1 + tap + TT],
                          start=(tap == 0), stop=(tap == 2))
                kb.act(zst[:, c, :TT], ps[:, :TT], AF.Identity, bias=hb[:, c:c + 1])
            zv = st.zT.rearrange("(c p) t -> p c t", p=128)
            kb.dma(STQ, kb.dres(zv[:, :, t0:t0 + TT], i), zst[:, :, :TT])
            z = pp[:, 6:8, :]
            A_, B_ = pa[0], pa[1]
            kb.tt("dve", A_[:, :, 1:W], z[:, :, 1:W], z[:, :, 0:W - 1], ALU.add)
            kb.tt("dve", B_[:, :, 3:W], A_[:, :, 3:W], A_[:, :, 1:W - 2], ALU.add)

            def sel(hf, ch, arr, sh):
                ps_ = slice(hf * 64, (hf + 1) * 64)
                kb.tt("dve", dpl[ps_, ch, :TT], arr[ps_, ch, PADC + sh:PADC + sh + TT], icn[ps_, ch, :TT], ALU.mult)
                kb.tt("dve", dpb[ps_, ch, :TT], dpl[ps_, ch, :TT], pp[ps_, 6 + ch, PADC:PADC + TT], ALU.subtract)

            sel(0, 0, A_, 0)
            sel(1, 0, B_, 1)
            kb.tt("dve", A_[:, 1, 7:W], B_[:, 1, 7:W], B_[:, 1, 3:W - 4], ALU.add)
            kb.tt("dve", B_[:, 1, 15:W], A_[:, 1, 15:W], A_[:, 1, 7:W - 8], ALU.add)
            sel(0, 1, A_, 3)
            sel(1, 1, B_, 7)
            for ch in range(2):
                ps = ps_main[cnt["main"] % 2]
                cnt["main"] += 1
                kb.mm(ps[:, :TT], poolW[:, ch, :], dpb[:, ch, :TT])
                kb.act(plst[:, ch, :TT], ps[:, :TT], AF.Identity, scale=pscale[:, ch:ch + 1])
            plv = st.plT.rearrange("(c p) t -> p c t", p=128)
            kb.dma(STQ, kb.dres(plv[:, :, t0:t0 + TT], i), plst[:, :, :TT])

        fullc = (l < DEPTH - 1)
        prep_A(cxs, 0)
        tile_A(cxs, 0, fullc, mid_hook=lambda: prep_A(lat, 0))
        if fullc:
            tile_A2(cxs, 0)
        for i in range(lat.NT):
            hook = (lambda j=i: prep_A(lat, j + 1)) if i + 1 < lat.NT else None
            tile_A(lat, i, True, mid_hook=hook)
            if i >= 1:
                tile_A2(lat, i - 1)
        tile_A2(lat, lat.NT - 1)
        kb.pop()

    def phase_H(l, st):
        S = st.S
        N1 = 2 * S // 128
        NH = N1 // 2
        PI = float(np.pi)
        kb.push()
        SLH = min(2048, S)
        SLW = min(512, S)
        NSL = S // SLW
        w1 = kb.tile("w1", [33, 64], F32)
        kb.dma("sp", w1, hy_w1[l])
        w2 = kb.tile("w2", [64, 2, 64], F32, multi=True)
        for i in range(2):
            kb.dma("sp", w2[:, i, :], hy_w2[l, i])
        w3 = kb.tile("w3", [64, 1024], F32)
        kb.dma("sp", w3, hy_w3[l])
        fr = sp_col("hy_freq", l)[0:64]
        b1 = sp_col("hy_b1", l)[0:64]
        b2 = sp_col("hy_b2", l)[0:64]
        fb = kb.tile("fb", [64, 3], F32)
        kb.tt("dve", fb[:, 0:1], fr, b1, ALU.mult)
        kb.tt("dve", fb[:, 1:3], b2, fr.bcast([64, 2]), ALU.mult)
        negdec = kb.tile("negdec", [128, 8], F32)
        kb.act(negdec, sp_col("hy_decay", l), AF.Abs)
        kb.ts("dve", negdec, negdec, -1.0, ALU.mult)
        hd3 = kb.tile("hd3", [64, 2, S], F32, multi=True)
        zt_r = kb.ring("zt", 2, [33, SLH], F32)
        ha = kb.tile("ha", [64, SLH], F32)
        m1 = kb.tile("m1", [64, SLH], F32)
        hb_r = kb.ring("hb", 2, [64, SLH], F32)
        ps_h = kb.psum("ps_h", [64, SLH])
        n = 0
        for d in range(2):
            for sl in range(S // SLH):
                cs = slice(sl * SLH, (sl + 1) * SLH)
                zt = zt_r[n % 2]
                n += 1
                kb.dma("sp", zt, st.zpos[d, :, cs])
                cur, Kc = zt, 33
                for layer in range(3):
                    lhsT = w1 if layer == 0 else w2[:, layer - 1, :]
                    for q in range(max(1, SLH // 512)):
                        qs = slice(q * 512, min((q + 1) * 512, SLH))
                        kb.mm(ps_h[:, qs], lhsT, cur[0:Kc, qs])
                    kb.act(ha, ps_h, AF.Identity, scale=fr, bias=fb[:, layer:layer + 1])
                    kb.ts("dve", m1, ha, PI, ALU.is_gt, -2.0 * PI, ALU.mult)
                    kb.tt("dve", ha, ha, m1, ALU.add)
                    kb.ts("dve", m1, ha, -PI, ALU.is_lt, 2.0 * PI, ALU.mult)
                    kb.tt("dve", ha, ha, m1, ALU.add)
                    dst = hd3[:, d, cs] if layer == 2 else hb_r[layer % 2]
                    kb.act(dst, ha, AF.Sin)
                    cur, Kc = dst, 64
        tr_r = kb.ring("tr", 2, [128, 2, SLW], F32)
        win_r = kb.ring("win", 2, [128, SLW], F32)
        f_r = kb.ring("f", 2, [128, SLW], F32)
        junk = kb.tile("junk", [128, SLW], F32)
        kst_r = kb.ring("kst", 2, [128, SLW], BF16)
        part = kb.tile("part", [128, 8, NSL], F32)
        nrm8 = kb.tile("nrm8", [128, 8], F32)
        rinv = kb.tile("rinv", [128, 4], F32)
        ps_f = [kb.psum(f"ps_f{i}", [128, 512]) for i in range(2)]
        kb.memset("dve", part, 0.0)
        n = 0
        for pas in (1, 2):
            for sl in range(NSL):
                cs = slice(sl * SLW, (sl + 1) * SLW)
                tr = tr_r[(pas * NSL + sl) % 2]
                kb.dma("sp", tr, st.trow[:, cs].pbcast(128))
                for oc in range(8):
                    o, d, wh = oc // 4, (oc % 4) // 2, oc % 2
                    ps = ps_f[n % 2]
                    win = win_r[n % 2]
                    f = f_r[n % 2]
                    kst = kst_r[n % 2]
                    n += 1
                    kb.mm(ps[:, :SLW], w3[:, oc * 128:(oc + 1) * 128], hd3[:, d, cs])
                    kb.act(win, tr[:, d, :], AF.Exp, scale=negdec[:, oc:oc + 1])
                    kb.tt("dve", f, ps[:, :SLW], win, ALU.mult)
                    if d == 1 and sl == 0:
                        kb.memset("dve", f[:, 0:1], 0.0)
                    if pas == 1:
                        kb.act(junk, f, AF.Abs, accum=part[:, oc, sl:sl + 1])
                    else:
                        kb.ts("dve", kst, f, rinv[:, 2 * o + wh:2 * o + wh + 1], ALU.mult)
                        r0 = (2 * o + wh) * 128
                        kb.dma(STQ, st.kfT[r0:r0 + 128, d * S + sl * SLW:d * S + (sl + 1) * SLW], kst)
            if pas == 1:
                kb.op("dve", lambda: nc.vector.reduce_sum(out=nrm8.ap, in_=part.ap, axis=mybir.AxisListType.X),
                      [nrm8], [part])
                n4 = nrm8.rearrange("p (o d w) -> p o d w", o=2, d=2)
                kb.tt("dve", rinv.rearrange("p (o w) -> p o w", o=2), n4[:, :, 0, :], n4[:, :, 1, :], ALU.add)
                kb.recip(rinv, rinv)
        kb.pop()

        kb.push()
        tabf = kb.tile("tabf", [128, 4 * N1 + 512], F32)
        tabb = kb.tile("tabb", [128, 2 * N1 + 896 + N1], BF16)
        kb.dma("sp", tabf, st.dftf)
        kb.dma("sp", tabb, st.dftb)
        skipb = kb.tile("skipb", [128, 512], F32)
        kb.dma("sp", skipb[0:NH, :], hy_skip[l, :].pbcast(NH))
        TWc2 = tabf[:, 0:2 * N1].rearrange("p (h k) -> p h k", h=2).unsqueeze(1).bcast([128, 4, 2, N1])
        TWs = tabf[:, 2 * N1:3 * N1].unsqueeze(1).bcast([128, 4, N1])
        nTWs = tabf[:, 3 * N1:4 * N1].unsqueeze(1).bcast([128, 4, N1])
        o_ = 4 * N1
        iTWc2 = tabf[0:N1, o_:o_ + 256].rearrange("p (h k) -> p h k", h=2).unsqueeze(1).bcast([N1, 4, 2, 128])
        iTWs = tabf[0:N1, o_ + 256:o_ + 384].unsqueeze(1).bcast([N1, 4, 128])
        inTWs = tabf[0:N1, o_ + 384:o_ + 512].unsqueeze(1).bcast([N1, 4, 128])
        F1 = tabb[:, 0:2 * N1]
        o_ = 2 * N1
        Cm, Sm, nSm = tabb[:, o_:o_ + 128], tabb[:, o_ + 128:o_ + 256], tabb[:, o_ + 256:o_ + 384]
        CS, nSC = tabb[:, o_ + 384:o_ + 640], tabb[:, o_ + 640:o_ + 896]
        o_ += 896
        C1n, nS1n = tabb[0:N1, o_:o_ + NH], tabb[0:N1, o_ + NH:o_ + 2 * NH]
        Cg = 16
        NG = 256 // Cg
        vin_r = kb.ring("vin", 2, [128, Cg, 128], BF16)
        x1_r = kb.ring("x1in", 2, [128, Cg, 128], BF16)
        x2_r = kb.ring("x2in", 2, [128, Cg, 128], BF16)
        kf_r = [kb.ring(f"kf{o}", 2, [128, Cg, 128], BF16) for o in range(2)]
        hyst_r = kb.ring("hyst", 2, [128, Cg, 128], BF16)
        def mk_tmp(i):
            t = {}
            t["X1"] = kb.tile(f"X1_{i}", [128, 4, 2, 128], F32)
            t["X2"] = kb.tile(f"X2_{i}", [128, 4, 2, 128], F32)
            t["Ap"] = kb.tile(f"Ap_{i}", [128, 4, 2, 128], BF16)
            t["Kf"] = kb.tile(f"Kf_{i}", [128, 4, 2, 128], F32)
            t["T"] = [kb.tile(f"T{j}_{i}", [128, 4, 128], F32) for j in range(4)]
            t["Y"] = kb.tile(f"Y_{i}", [128, 4, 2, 128], BF16)
            t["XB1"] = kb.tile(f"XB1_{i}", [128, 4, 2, 128], F32)
            t["XB2"] = kb.tile(f"XB2_{i}", [128, 4, 2, 128], F32)
            t["Bp"] = kb.tile(f"Bp_{i}", [128, 4, 2, 128], BF16)
            t["tp"] = kb.tile(f"tp_{i}", [128, 4, 128], F32)
            t["u2"] = kb.tile(f"u2_{i}", [128, 4, 128], BF16)
            return t

        tmps = [mk_tmp(0), mk_tmp(1)]
        psA = kb.psum("psA", [128, 4, 256])
        psUr = kb.psum("psUr", [128, 512])
        psUi = kb.psum("psUi", [128, 512])
        psB = kb.psum("psB", [128, 4, 256])
        psY = kb.psum("psY", [128, 512])
        ur = psUr[:, 0:4 * N1].rearrange("p (c k) -> p c k", c=4)
        ui = psUi[:, 0:4 * N1].rearrange("p (c k) -> p c k", c=4)
        yv = psY[0:NH, 0:512].rearrange("p (c k) -> p c k", c=4)

        def fwd_fft(src, K, t):
            X1, X2, Ap = t["X1"], t["X2"], t["Ap"]
            yield ("W", "A")
            for c in range(4):
                kb.mm(psA[:, c, 0:2 * N1], src[0:K, c, :], F1[0:K, :])
            yield ("R", "A")
            pA4 = psA[:, :, 0:2 * N1].rearrange("p c (h k) -> p c h k", h=2)
            kb.tt("dve", X1[:, :, :, 0:N1], pA4, TWc2, ALU.mult)
            kb.tt("dve", X2[:, :, 0, 0:N1], pA4[:, :, 1, :], TWs, ALU.mult)
            kb.tt("dve", X2[:, :, 1, 0:N1], pA4[:, :, 0, :], nTWs, ALU.mult)
            kb.tt("dve", Ap[:, :, :, 0:N1], X1[:, :, :, 0:N1], X2[:, :, :, 0:N1], ALU.add)
            yield ("W", "U")
            rr, ri = Ap[:, :, 0, 0:N1], Ap[:, :, 1, 0:N1]
            kb.mm(ur, Cm, rr, start=True, stop=False)
            kb.mm(ur, Sm, ri, start=False, stop=True)
            kb.mm(ui, Cm, ri, start=True, stop=False)
            kb.mm(ui, nSm, rr, start=False, stop=True)

        def conv_block(o, kfblk, ublk, gateblk, dst, sk, t):
            Kf, Y, Bp, XB1, XB2, tp = t["Kf"], t["Y"], t["Bp"], t["XB1"], t["XB2"], t["tp"]
            yield from fwd_fft(kfblk, N1, t)
            yield ("R", "U")
            kb.copy("act", Kf[:, :, 0, 0:N1], ur)
            kb.copy("act", Kf[:, :, 1, 0:N1], ui)
            yield from fwd_fft(ublk, NH, t)
            yield ("R", "U")
            Kr, Ki = Kf[:, :, 0, 0:N1], Kf[:, :, 1, 0:N1]
            tt_ = [x[:, :, 0:N1] for x in t["T"]]
            kb.tt("dve", tt_[0], ur, Kr, ALU.mult)
            kb.tt("dve", tt_[1], ui, Ki, ALU.mult)
            kb.tt("dve", tt_[2], ur, Ki, ALU.mult)
            kb.tt("dve", tt_[3], ui, Kr, ALU.mult)
            kb.tt("dve", Y[:, :, 0, 0:N1], tt_[0], tt_[1], ALU.subtract)
            kb.tt("dve", Y[:, :, 1, 0:N1], tt_[2], tt_[3], ALU.add)
            yield ("W", "B")
            for c in range(4):
                kb.mm(psB[0:N1, c, :], Y[:, c, 0, 0:N1], CS, start=True, stop=False)
                kb.mm(psB[0:N1, c, :], Y[:, c, 1, 0:N1], nSC, start=False, stop=True)
            yield ("R", "B")
            pB4 = psB[0:N1].rearrange("p c (h k) -> p c h k", h=2)
            kb.tt("dve", XB1[0:N1], pB4, iTWc2, ALU.mult)
            kb.tt("dve", XB2[0:N1, :, 0, :], pB4[:, :, 1, :], inTWs, ALU.mult)
            kb.tt("dve", XB2[0:N1, :, 1, :], pB4[:, :, 0, :], iTWs, ALU.mult)
            kb.tt("dve", Bp[0:N1], XB1[0:N1], XB2[0:N1], ALU.add)
            yield ("W", "Y")
            kb.mm(yv, C1n, Bp[0:N1, :, 0, :], start=True, stop=False)
            kb.mm(yv, nS1n, Bp[0:N1, :, 1, :], start=False, stop=True)
            yield ("R", "Y")
            kb.tt("dve", tp[0:NH], ublk[0:NH], sk.unsqueeze(2).bcast([NH, 4, 128]), ALU.mult)
            kb.tt("dve", tp[0:NH], tp[0:NH], yv, ALU.add)
            kb.tt("dve", dst, tp[0:NH], gateblk[0:NH], ALU.mult)

        def fftv(src_rows, c0, nrow):
            return src_rows[c0:c0 + Cg, :].rearrange("c (a b) -> a c b", b=128)

        gt = {}

        def load_group(g):
            c0 = g * Cg
            vin, x1in, x2in = vin_r[g % 2], x1_r[g % 2], x2_r[g % 2]
            kfs = [kf_r[0][g % 2], kf_r[1][g % 2]]
            kb.dma("sp", vin[0:NH], fftv(st.zT[0:256], c0, NH))
            kb.dma("sp", x1in[0:NH], fftv(st.zT[256:512], c0, NH))
            kb.dma("sp", x2in[0:NH], fftv(st.zT[512:768], c0, NH))
            for o in range(2):
                kb.dma("sp", kfs[o][0:N1], fftv(st.kfT[o * 256:(o + 1) * 256], c0, N1))
            gt[g] = (vin, x1in, x2in, kfs, hyst_r[g % 2])

        def chain(g, b, t):
            vin, x1in, x2in, kfs, hyst = gt[g]
            c0 = g * Cg
            bs = slice(4 * b, 4 * b + 4)
            u2 = t["u2"]
            cc = c0 + 4 * b
            yield from conv_block(0, kfs[0][:, bs, :], vin[:, bs, :], x1in[:, bs, :], u2[0:NH],
                                  skipb[0:NH, cc:cc + 4], t)
            yield from conv_block(1, kfs[1][:, bs, :], u2, x2in[:, bs, :], hyst[0:NH, bs, :],
                                  skipb[0:NH, 256 + cc:256 + cc + 4], t)
            gdone[g] += 1
            if gdone[g] == Cg // 4:
                kb.dma(STQ, fftv(st.hyT, c0, NH), hyst[0:NH])
                if g + 2 < NG:
                    load_group(g + 2)

        NCHAIN = 2
        gdone = {g: 0 for g in range(NG)}
        load_group(0)
        if NG > 1:
            load_group(1)
        todo = [(g, b) for g in range(NG) for b in range(Cg // 4)]
        active = []
        held = {}
        nstart = 0
        while todo or active:
            while todo and len(active) < NCHAIN:
                g, b = todo.pop(0)
                ch = {"id": nstart, "gen": chain(g, b, tmps[nstart % 2]), "done": False}
                nstart += 1
                ch["tok"] = next(ch["gen"])
                active.append(ch)
            progressed = False
            for ch in list(active):
                kind, tile = ch["tok"]
                if kind == "W":
                    if held.get(tile) not in (None, ch["id"]):
                        continue
                    held[tile] = ch["id"]
                try:
                    ch["tok"] = next(ch["gen"])
                except StopIteration:
                    ch["done"] = True
                if kind == "R":
                    held[tile] = None
                progressed = True
                if ch["done"]:
                    active.remove(ch)
            assert progressed, "hyena chain interleave deadlock"
        kb.pop()

    def phase_B(l, streams):
        kb.push()
        kT = kb.tile("kT", [128, 2, NKEY], BF16, multi=True)
        va = kb.tile("va", [128, NKEY // 128, 2, 66], BF16, multi=True)
        nld = 6
        per = (NKEY // 128) // nld
        for i in range(nld):
            kb.dma("sp", kT[:, :, i * per * 128:(i + 1) * per * 128], kT2d[:, :, i * per * 128:(i + 1) * per * 128])
            kb.dma("sp", va[:, i * per:(i + 1) * per], vaugd[i * per:(i + 1) * per].rearrange("c p k d -> p c k d"))
        qt_r = kb.ring("qt", 2, [128, 4, 512], BF16)
        pt_r = kb.ring("pt", 3, [128, 2, 512], BF16)
        ost_r = kb.ring("ost", 2, [128, 4, 512], BF16)
        osb_r = kb.ring("osb", 2, [128, 512], F32)
        otmp = kb.tile("otmp", [64, 512], BF16)
        rden_r = kb.ring("rden", 2, [128, 512], F32)
        ps_s = [kb.psum(f"ps_s{i}", [128, 2, 512]) for i in range(2)]
        ps_o = [kb.psum(f"ps_o{i}", [128, 512]) for i in range(2)]
        ps_bc = kb.psum("ps_bc", [64, 512])

        items = []
        for st in streams:
            nk = CTX if st.name == "ctx" else NKEY
            nch = nk // 128
            for i in range(st.NT):
                for j in range(4):
                    for ch in range(nch):
                        items.append((st, i, j, ch, nch))
        qcur = {}
        tord = {}
        for (st_, i_, _j, _c, _n) in items:
            tord.setdefault((st_.name, i_), len(tord))

        def get_q(st, i):
            key = (st.name, i)
            if key not in qcur:
                qt = qt_r[tord[key] % 2]
                kb.dma("sp", qt[:, :, :st.T], st.qT[:, :, i * st.T:(i + 1) * st.T])
                qcur[key] = qt
            return qcur[key]

        def emit_qk(n):
            st, i, j, ch, nch = items[n]
            qt = get_q(st, i)
            kv = j // 2
            pss = ps_s[n % 2]
            for hf in range(2):
                P = slice(hf * 64, hf * 64 + 64)
                kb.mm(pss[:, hf, :st.T], kT[P, kv, ch * 128:(ch + 1) * 128], qt[P, j, :st.T])

        pending = []

        def finalize2(st, i, hd, rden, osb, ost):
            NQ = st.T
            kb.mm(ps_bc[:, :NQ], onesf[64:65, 0:64], rden[64:65, :NQ])
            if hd % 2 == 0:
                kb.tt("dve", ost[0:64, hd // 2, :NQ], osb[0:64, :NQ], ps_bc[:, :NQ], ALU.mult)
            else:
                kb.tt("dve", otmp[0:64, :NQ], osb[0:64, :NQ], ps_bc[:, :NQ], ALU.mult)
                kb.copy("dve", ost[64:128, hd // 2, :NQ], otmp[0:64, :NQ])
            if hd == 7:
                kb.dma(STQ, st.attnT[:, :, i * NQ:(i + 1) * NQ], ost[:, :, :NQ])

        emit_qk(0)
        for n in range(len(items)):
            st, i, j, ch, nch = items[n]
            NQ = st.T
            kv = j // 2
            if n + 1 < len(items):
                emit_qk(n + 1)
            pt = pt_r[n % 3]
            kb.act(pt[:, :, :NQ], ps_s[n % 2][:, :, :NQ], AF.Exp, scale=0.125)
            for hf in range(2):
                kb.mm(ps_o[hf][0:65, :NQ], va[:, ch, kv, 0:65], pt[:, hf, :NQ], start=(ch == 0),
                      stop=(ch == nch - 1))
            while pending:
                finalize2(*pending.pop(0))
            if ch == nch - 1:
                ost = ost_r[tord[(st.name, i)] % 2]
                for hf in range(2):
                    osb, rden = osb_r[hf], rden_r[hf]
                    kb.copy("act", osb[0:65, :NQ], ps_o[hf][0:65, :NQ])
                    kb.recip(rden[64:65, :NQ], osb[64:65, :NQ])
                    pending.append((st, i, 2 * j + hf, rden, osb, ost))
        while pending:
            finalize2(*pending.pop(0))
        kb.pop()

    def prep_C(st, i, TT, which, xt, sq, srt, rstd, h, ps_ss):
        t0 = i * TT
        xsrc = st.x.rearrange("(c p) t -> p c t", p=128)
        kb.dma("sp", xt[:, :, :TT], xsrc[:, :, t0:t0 + TT])
        ai = norm_mod(st, xt, sq, h, ps_ss, srt, rstd, which, TT)
        norm_apply(st, xt, xt, h, rstd, ai, TT)

    def phase_C1(l, streams, TT):
        kb.push()
        wG = kb.tile("wG", [128, 8, 4096], BF16, multi=True)
        wBA = kb.tile("wBA", [128, 4, 1024], BF16, multi=True)
        wBH = kb.tile("wBH", [128, 2, 1024], BF16, multi=True)
        wBP = kb.tile("wBP", [128, 2, 1024], BF16, multi=True)
        wBG = kb.tile("wBG", [128, 2, 1024], BF16, multi=True)
        wO = kb.tile("wO", [128, 8, 1024], BF16, multi=True)
        wsrc = w_in[l].rearrange("(kc p) n -> p kc n", p=128)
        for kc in range(8):
            kb.dma("pool", wG[:, kc, :], wsrc[:, kc, 2304:6400])
        for c in range(4):
            kb.dma("pool", wBA[:, c, :], w_br_attn[l, c * 128:(c + 1) * 128, :])
        for (wt, src) in ((wBH, w_br_hy), (wBP, w_br_pool), (wBG, w_br_gm)):
            for c in range(2):
                kb.dma("pool", wt[:, c, :], src[l, c * 128:(c + 1) * 128, :])
        for kc in range(8):
            kb.dma("pool", wO[:, kc, :], w_out[l, kc * 128:(kc + 1) * 128, :])
        xt = kb.tile("xt", [128, 8, TT], F32)
        sq = kb.tile("sq", [128, 8, TT], BF16)
        srt = kb.tile("srt", [128, TT], F32)
        rstd = kb.tile("rstd", [128, TT], F32)
        h_r = kb.ring("h", 2, [128, 8, TT], BF16)
        at = kb.tile("at", [128, 4, TT], BF16)
        hyt_r = kb.ring("hyt", 2, [128, 2, TT], BF16)
        plt_r = kb.ring("plt", 2, [128, 2, TT], BF16)
        gmt_r = kb.ring("gmt", 2, [128, 2, TT], BF16)
        g_r = kb.ring("g", 2, [128, TT], BF16)
        tmp_r = kb.ring("tmp", 2, [128, TT], F32)
        acc = kb.tile("acc", [128, TT], F32)
        mg = kb.tile("mg", [128, 8, TT], BF16)
        xr_r = kb.ring("xr", 2, [128, TT], F32)
        xo_r = kb.ring("xo", 2, [128, TT], F32)
        ps_g = [kb.psum(f"ps_g{i}", [128, 512]) for i in range(2)]
        ps_b = [kb.psum(f"ps_b{i}", [128, 512]) for i in range(2)]
        ps_o = [kb.psum(f"ps_o{i}", [128, 512]) for i in range(2)]
        ps_ss = kb.psum("ps_ss", [128, 512])
        tiles = [(st, i) for st in streams for i in range(st.S // min(TT, st.S))]
        cnt = {"h": 0, "n": 0, "o": 0}
        hmap = {}

        def prep(k):
            st, i = tiles[k]
            T_ = min(TT, st.S)
            h = h_r[cnt["h"] % 2]
            cnt["h"] += 1
            prep_C(st, i, T_, 1, xt, sq, srt, rstd, h, ps_ss)
            hmap[k] = h

        prep(0)
        for k, (st, i) in enumerate(tiles):
            T_ = min(TT, st.S)
            t0 = i * T_
            h = hmap.pop(k)
            hyt, plt, gmt = hyt_r[k % 2], plt_r[k % 2], gmt_r[k % 2]
            for (dst, src) in ((hyt, st.hyT), (plt, st.plT), (gmt, st.gmT)):
                kb.dma("sp", dst[:, :, :T_], src.rearrange("(c p) t -> p c t", p=128)[:, :, t0:t0 + T_])
            kb.dma("sp", at[:, :, :T_], st.attnT[:, :, t0:t0 + T_])
            branches = ((1, wBH, hyt, 2, 128), (2, wBP, plt, 2, 128), (3, wBG, gmt, 2, 128), (0, wBA, at, 4, 128))
            for m in range(8):
                if m == 4 and k + 1 < len(tiles):
                    prep(k + 1)
                ms = slice(m * 128, (m + 1) * 128)
                for bi, (gi, wB, src, nk, kp) in enumerate(branches):
                    pg = ps_g[cnt["n"] % 2]
                    pb = ps_b[cnt["n"] % 2]
                    g = g_r[cnt["n"] % 2]
                    tmp = tmp_r[cnt["n"] % 2]
                    cnt["n"] += 1
                    for kc in range(8):
                        kb.mm(pg[:, :T_], wG[:, kc, gi * 1024 + m * 128:gi * 1024 + (m + 1) * 128], h[:, kc, :T_],
                              start=(kc == 0), stop=(kc == 7))
                    for c in range(nk):
                        kb.mm(pb[:, :T_], wB[0:kp, c, ms], src[0:kp, c, :T_], start=(c == 0), stop=(c == nk - 1))
                    kb.act(g[:, :T_], pg[:, :T_], AF.Sigmoid)
                    if bi == 0:
                        kb.tt("dve", acc[:, :T_], g[:, :T_], pb[:, :T_], ALU.mult)
                    else:
                        kb.tt("dve", tmp[:, :T_], g[:, :T_], pb[:, :T_], ALU.mult)
                        dst = acc[:, :T_] if bi < 3 else mg[:, m, :T_]
                        kb.tt("dve", dst, acc[:, :T_], tmp[:, :T_], ALU.add)
            xsrc = st.x.rearrange("(c p) t -> p c t", p=128)
            xdst = st.xmid.rearrange("(c p) t -> p c t", p=128)
            for m in range(8):
                po = ps_o[cnt["o"] % 2]
                xr = xr_r[cnt["o"] % 2]
                xo = xo_r[cnt["o"] % 2]
                cnt["o"] += 1
                kb.dma("sp", xr[:, :T_], xsrc[:, m, t0:t0 + T_])
                for kc in range(8):
                    kb.mm(po[:, :T_], wO[:, kc, m * 128:(m + 1) * 128], mg[:, kc, :T_], start=(kc == 0), stop=(kc == 7))
                kb.stt(xo[:, :T_], po[:, :T_], modp[:, st.si, 2, m:m + 1], xr[:, :T_], ALU.mult, ALU.add)
                kb.dma(STQ, xdst[:, m, t0:t0 + T_], xo[:, :T_])
        kb.pop()

    def phase_C2(l, streams, TT, final):
        kb.push()
        wg = kb.tile("wg", [128, 8, HID], BF16, multi=True)
        wu = kb.tile("wu", [128, 8, HID], BF16, multi=True)
        wd = kb.tile("wd", [128, 22, D], BF16, multi=True)
        for kc in range(8):
            kb.dma("pool", wg[:, kc, :], ffn_g[l, kc * 128:(kc + 1) * 128, :])
            kb.dma("pool", wu[:, kc, :], ffn_u[l, kc * 128:(kc + 1) * 128, :])
        for j in range(22):
            kb.dma("pool", wd[:, j, :], ffn_d[l, j * 128:(j + 1) * 128, :])
        xt = kb.tile("xt", [128, 8, TT], F32)
        sq = kb.tile("sq", [128, 8, TT], BF16)
        srt = kb.tile("srt", [128, TT], F32)
        rstd = kb.tile("rstd", [128, TT], F32)
        h_r = kb.ring("h", 2, [128, 8, TT], BF16)
        a_t = kb.tile("a", [128, 22, TT], BF16)
        sl_r = kb.ring("sl", 2, [128, TT], F32)
        xr_r = kb.ring("xr", 2, [128, TT], F32)
        xo_r = kb.ring("xo", 2, [128, TT], F32)
        if final:
            xf = kb.tile("xf", [128, 8, TT], F32)
            fsq = kb.tile("fsq", [128, 8, TT], BF16)
            fnw = sp_col("final_norm_w")
        ps_g = [kb.psum(f"ps_g{i}", [128, 512]) for i in range(2)]
        ps_u = [kb.psum(f"ps_u{i}", [128, 512]) for i in range(2)]
        ps_o = [kb.psum(f"ps_o{i}", [128, 512]) for i in range(2)]
        ps_ss = kb.psum("ps_ss", [128, 512])
        tiles = [(st, i) for st in streams for i in range(st.S // min(TT, st.S))]
        cnt = {"h": 0, "n": 0, "o": 0}
        hmap = {}

        def prep(k):
            st, i = tiles[k]
            T_ = min(TT, st.S)
            h = h_r[cnt["h"] % 2]
            cnt["h"] += 1
            prep_C(st, i, T_, 2, xt, sq, srt, rstd, h, ps_ss)
            hmap[k] = h

        prep(0)
        for k, (st, i) in enumerate(tiles):
            T_ = min(TT, st.S)
            t0 = i * T_
            h = hmap.pop(k)
            for j in range(22):
                if j == 12 and k + 1 < len(tiles):
                    prep(k + 1)
                pg = ps_g[cnt["n"] % 2]
                pu = ps_u[cnt["n"] % 2]
                sl = sl_r[cnt["n"] % 2]
                cnt["n"] += 1
                for kc in range(8):
                    kb.mm(pg[:, :T_], wg[:, kc, j * 128:(j + 1) * 128], h[:, kc, :T_], start=(kc == 0), stop=(kc == 7))
                for kc in range(8):
                    kb.mm(pu[:, :T_], wu[:, kc, j * 128:(j + 1) * 128], h[:, kc, :T_], start=(kc == 0), stop=(kc == 7))
                kb.act(sl[:, :T_], pg[:, :T_], AF.Silu)
                kb.tt("dve", a_t[:, j, :T_], sl[:, :T_], pu[:, :T_], ALU.mult)
            xsrc = st.x.rearrange("(c p) t -> p c t", p=128)
            xdst = st.xnext.rearrange("(c p) t -> p c t", p=128)
            dofinal = final and st.name == "lat"
            for m in range(8):
                po = ps_o[cnt["o"] % 2]
                xr = xr_r[cnt["o"] % 2]
                xo = xo_r[cnt["o"] % 2]
                cnt["o"] += 1
                kb.dma("sp", xr[:, :T_], xsrc[:, m, t0:t0 + T_])
                for j in range(22):
                    kb.mm(po[:, :T_], wd[:, j, m * 128:(m + 1) * 128], a_t[:, j, :T_], start=(j == 0), stop=(j == 21))
                if dofinal:
                    kb.stt(xf[:, m, :T_], po[:, :T_], modp[:, st.si, 5, m:m + 1], xr[:, :T_], ALU.mult, ALU.add)
                else:
                    kb.stt(xo[:, :T_], po[:, :T_], modp[:, st.si, 5, m:m + 1], xr[:, :T_], ALU.mult, ALU.add)
                    kb.dma(STQ, xdst[:, m, t0:t0 + T_], xo[:, :T_])
            if dofinal:
                kb.act(fsq[:, :, :T_], xf[:, :, :T_], AF.Square)
                for c in range(8):
                    kb.mm(ps_ss[:, :T_], ones1024, fsq[:, c, :T_], start=(c == 0), stop=(c == 7))
                kb.act(srt[:, :T_], ps_ss[:, :T_], AF.Ln, bias=epst)
                kb.act(rstd[:, :T_], srt[:, :T_], AF.Exp, scale=-0.5)
                kb.tt("dve", xf[:, :, :T_], xf[:, :, :T_], rstd[:, :T_].unsqueeze(1).bcast([128, 8, T_]), ALU.mult)
                odst = outT.rearrange("(c p) t -> p c t", p=128)
                for c in range(8):
                    xo = xo_r[cnt["o"] % 2]
                    cnt["o"] += 1
                    kb.act(xo[:, :T_], xf[:, c, :T_], AF.Identity, scale=fnw[:, c:c + 1])
                    kb.dma(STQ, odst[:, c, t0:t0 + T_], xo[:, :T_])
        kb.pop()

    TC1 = 512
    TC2 = 256
    for l in range(nlayers):
        last = (l == DEPTH - 1)
        streams = [lat] if last else [cxs, lat]
        lat.xmid, cxs.xmid = xa, ca
        lat.xnext, cxs.xnext = xb, cb
        phase_M(l)
        if stop_phase == ("M", l):
            break
        phase_A(l)
        if stop_phase == ("A", l):
            break
        if "H" not in skip:
            for st in streams:
                phase_H(l, st)
        if stop_phase == ("H", l):
            break
        phase_B(l, streams)
        if stop_phase == ("B", l):
            break
        phase_C1(l, streams, TC1)
        for st in streams:
            st.x = st.xmid
        if stop_phase == ("C1", l):
            break
        phase_C2(l, streams, TC2, final=last)
        for st in streams:
            st.x = st.xnext
        if stop_phase == ("C2", l):
            break

    kb.barrier()
    return nc


SP_SPEC = [("norm1_w", 16), ("norm2_w", 16), ("b_mod", 96), ("final_norm_w", 8), ("q_norm_w", 2), ("k_norm_w", 2),
           ("hy_conv_w", 36), ("hy_conv_b", 12), ("pool_scale", 4), ("gm_bs", 8), ("hy_b1", 2), ("hy_freq", 2),
           ("hy_b2", 4), ("hy_decay", 16)]
SP_OFF = {}
_o = 0
for _n, _c in SP_SPEC:
    SP_OFF[_n] = (_o, _c)
    _o += _c
NSP = _o


def _pack_small(inp):
    sp = np.zeros((128, NSP), np.float32)

    def put(name, arr):
        o, n = SP_OFF[name]
        assert arr.shape == (128, n), (name, arr.shape, n)
        sp[:, o:o + n] = arr

    def fm(a, nch):
        L = a.shape[0]
        return a.reshape(L, nch, 128).transpose(2, 0, 1).reshape(128, L * nch)

    put("norm1_w", fm(inp["norm1_w"], 8))
    put("norm2_w", fm(inp["norm2_w"], 8))
    put("b_mod", fm(inp["b_mod"], 48))
    put("final_norm_w", inp["final_norm_w"].reshape(8, 128).T)
    put("q_norm_w", np.tile(inp["q_norm_w"].T, (2, 1)))
    put("k_norm_w", np.tile(inp["k_norm_w"].T, (2, 1)))
    put("hy_conv_w", inp["hy_conv_w"].reshape(DEPTH, 3, 6, 128).transpose(3, 0, 1, 2).reshape(128, DEPTH * 18))
    put("hy_conv_b", fm(inp["hy_conv_b"], 6))
    put("pool_scale", fm(inp["pool_scale"], 2))
    put("gm_bs", inp["gm_bs"].transpose(2, 0, 1).reshape(128, DEPTH * 4))
    h64 = lambda a: np.concatenate([a, np.zeros_like(a)], axis=0)
    put("hy_b1", h64(inp["hy_b1"].T))
    put("hy_freq", h64(inp["hy_freq"].T))
    put("hy_b2", h64(inp["hy_b2"].transpose(2, 0, 1).reshape(64, DEPTH * 2)))
    put("hy_decay", fm(inp["hy_decay"], 8))
    return sp


_CONST_CACHE = {}


def _consts():
    if _CONST_CACHE:
        return _CONST_CACHE
    cbf = np.zeros((128, 3, 128), np.float32)
    cbf[:, 0, :] = np.eye(128)
    k = np.arange(128)
    cbf[k, 1, k ^ 1] = 1.0
    cosF, sinS = _rope_tables()
    _CONST_CACHE.update(dict(
        cbf=_bf(cbf), ropeT=np.stack([cosF, sinS]).astype(np.float32),
        invcnt_l=_pool_invcnt(SEQ), invcnt_c=_pool_invcnt(CTX)))
    for nm, n in (("lat", SEQ), ("ctx", CTX)):
        zz, tr = _hy_tables(n)
        tf, tb = _dft_tables(2 * n // 128)
        _CONST_CACHE["zpos_" + nm] = zz
        _CONST_CACHE["trow_" + nm] = tr
        _CONST_CACHE["dftf_" + nm] = tf
        _CONST_CACHE["dftb_" + nm] = tb
    return _CONST_CACHE


def make_in_maps(inp):
    inp = {k: np.asarray(v) for k, v in inp.items()}
    shared = dict(_consts())
    for k in ("w_mod", "w_in", "w_br_attn", "w_br_hyena", "w_br_pool", "w_br_gmlp", "w_out", "ffn_w_gate",
              "ffn_w_up", "ffn_w_down", "gm_norm_w", "pool_w", "hy_w1", "hy_w2", "hy_w3", "hy_skip"):
        shared[k] = np.ascontiguousarray(inp[k], dtype=np.float32)
    shared["smallp"] = _pack_small(inp)
    shared["gm_wsT"] = np.ascontiguousarray(inp["gm_ws"].transpose(0, 1, 3, 2))
    shared["hy_skip"] = np.ascontiguousarray(inp["hy_skip"].reshape(DEPTH, 512), dtype=np.float32)
    maps = []
    for b in range(NCORE):
        m = dict(shared)
        m["xT"] = np.ascontiguousarray(inp["x"][b].T)
        m["ctxT"] = np.ascontiguousarray(inp["ctx"][b].T)
        cv = np.stack([inp["c"][b].reshape(8, 128).T, inp["c_ctx"].reshape(8, 128).T], axis=-1)
        m["cvec"] = np.ascontiguousarray(cv, dtype=np.float32)
        maps.append(m)
    return maps


def kernel(**inputs):
    nc = build_program()
    maps = make_in_maps(inputs)
    res = run_bass_kernel_spmd(nc, maps, core_ids=list(range(NCORE)))
    out = np.stack([np.ascontiguousarray(r["outT"].T) for r in res.results], axis=0)
    return out.astype(np.float32)
```
